# Optimizing a Trainium2 kernel written in Bass

```python
import jax, jax.numpy as jnp
from jax import lax
import numpy as np

D_MODEL = 1024
BATCH = 4
SEQ = 8192
DEPTH = 4
DEC_BATCH = 8
DEC_SEQ = 4096
PAST_LEN = 128

N_MIXERS = 2
HEAD_DIM = 64
E_MIX = D_MODEL
A_GROUPS = ((128, 1), (512, 4), (2048, 16))
N_GROUPS_A = len(A_GROUPS)
H_A = E_MIX // HEAD_DIM
BLK_A = 64
QKV_A = 3 * N_GROUPS_A * E_MIX
IN_A = QKV_A + E_MIX
H_B = E_MIX // HEAD_DIM
KV_B = 4
REP_B = H_B // KV_B
WIN_B = 128
BLK_B = 128
DQ_B = H_B * HEAD_DIM
DKV_B = KV_B * HEAD_DIM
IN_B = DQ_B + 2 * DKV_B + E_MIX
N_LAYERS_A = (DEPTH + 1) // 2
N_LAYERS_B = DEPTH // 2
DEEPNORM_ALPHA = (2.0 * DEPTH) ** 0.25
DEEPNORM_BETA = (8.0 * DEPTH) ** -0.25
LN_EPS = 1e-5

kernel_name = "hybrid_dilated_swa_gqa_encoder"


def _alibi_slopes(n):
    return jnp.asarray(2.0 ** (-8.0 * np.arange(1, n + 1) / n), dtype=jnp.float32)


def _layernorm(h, g, b):
    h32 = h.astype(jnp.float32)
    mu = jnp.mean(h32, axis=-1, keepdims=True)
    var = jnp.mean(jnp.square(h32 - mu), axis=-1, keepdims=True)
    return ((h32 - mu) * lax.rsqrt(var + LN_EPS) * g.astype(jnp.float32) + b.astype(jnp.float32)).astype(h.dtype)


def _dilated_group(q, k, v, dil, n_side, slopes):
    B, S, H, Dh = q.shape
    L = S // dil
    nb = -(-L // BLK_A)
    Lp = nb * BLK_A

    def to_res(t):
        t = t.reshape(B, L, dil, H, Dh).transpose(0, 2, 1, 3, 4)
        return jnp.pad(t, ((0, 0), (0, 0), (0, Lp - L), (0, 0), (0, 0)))

    def band(t):
        t = jnp.pad(t, ((0, 0), (0, 0), (BLK_A, BLK_A), (0, 0), (0, 0))).reshape(B, dil, nb + 2, BLK_A, H, Dh)
        return jnp.concatenate([t[:, :, :-2], t[:, :, 1:-1], t[:, :, 2:]], axis=3)

    qb = to_res(q).reshape(B, dil, nb, BLK_A, H, Dh)
    kb = band(to_res(k))
    vb = band(to_res(v))
    s = jnp.einsum('bgnqhd,bgnkhd->bgnhqk', qb, kb, preferred_element_type=jnp.float32) * (Dh ** -0.5)
    i = jnp.arange(BLK_A)[:, None]
    j = jnp.arange(3 * BLK_A)[None, :]
    rel = j - BLK_A - i
    kpos = jnp.arange(nb)[:, None, None] * BLK_A + (j - BLK_A)[None]
    valid = (jnp.abs(rel) <= n_side)[None] & (kpos >= 0) & (kpos < L)
    bias = -slopes[:, None, None] * (dil * jnp.abs(rel)).astype(jnp.float32)[None]
    s = jnp.where(valid[:, None], s + bias, -jnp.inf)
    m = jnp.max(s, axis=-1, keepdims=True)
    p = jnp.exp(s - m)
    den = jnp.sum(p, axis=-1, keepdims=True)
    o = jnp.einsum('bgnhqk,bgnkhd->bgnqhd', p / den, vb.astype(jnp.float32))
    lse = (m + jnp.log(den))[..., 0]
    o = o.reshape(B, dil, Lp, H, Dh)[:, :, :L].transpose(0, 2, 1, 3, 4).reshape(B, S, H, Dh)
    lse = lse.transpose(0, 1, 2, 4, 3).reshape(B, dil, Lp, H)[:, :, :L].transpose(0, 2, 1, 3).reshape(B, S, H)
    return o, lse


def _mixer_a(u, w_in, w_out):
    B, S, _ = u.shape
    proj = u @ w_in
    qkv = proj[..., :QKV_A].reshape(B, S, N_GROUPS_A, 3, H_A, HEAD_DIM)
    z = proj[..., QKV_A:]
    slopes = _alibi_slopes(H_A)
    outs, lses = [], []
    for g, (win, dil) in enumerate(A_GROUPS):
        o, l = _dilated_group(qkv[:, :, g, 0], qkv[:, :, g, 1], qkv[:, :, g, 2], dil, win // (2 * dil), slopes)
        outs.append(o)
        lses.append(l)
    wts = jax.nn.softmax(jnp.stack(lses), axis=0)[..., None]
    att = jnp.sum(wts * jnp.stack(outs), axis=0).reshape(B, S, E_MIX).astype(u.dtype)
    return (att * jax.nn.silu(z)) @ w_out


def _mixer_b(u, w_in, w_out, sink):
    B, S, _ = u.shape
    nb = S // BLK_B
    proj = u @ w_in
    q = proj[..., :DQ_B].reshape(B, nb, BLK_B, KV_B, REP_B, HEAD_DIM)
    k = proj[..., DQ_B:DQ_B + DKV_B].reshape(B, S, KV_B, HEAD_DIM)
    v = proj[..., DQ_B + DKV_B:DQ_B + 2 * DKV_B].reshape(B, S, KV_B, HEAD_DIM)
    z = proj[..., DQ_B + 2 * DKV_B:]

    def band(t):
        t = jnp.pad(t, ((0, 0), (BLK_B, BLK_B), (0, 0), (0, 0))).reshape(B, nb + 2, BLK_B, KV_B, HEAD_DIM)
        return jnp.concatenate([t[:, :-2], t[:, 1:-1], t[:, 2:]], axis=2)

    kb, vb = band(k), band(v)
    s = jnp.einsum('bnqgrd,bnkgd->bngrqk', q, kb, preferred_element_type=jnp.float32) * (HEAD_DIM ** -0.5)
    i = jnp.arange(BLK_B)[:, None]
    j = jnp.arange(3 * BLK_B)[None, :]
    rel = j - BLK_B - i
    kpos = jnp.arange(nb)[:, None, None] * BLK_B + (j - BLK_B)[None]
    valid = (jnp.abs(rel) <= WIN_B)[None] & (kpos >= 0) & (kpos < S)
    slopes = _alibi_slopes(H_B).reshape(KV_B, REP_B)
    bias = -slopes[:, :, None, None] * jnp.abs(rel).astype(jnp.float32)
    s = jnp.where(valid[:, None, None], s + bias, -jnp.inf)
    snk = sink.astype(jnp.float32).reshape(KV_B, REP_B)[:, :, None, None]
    m = jnp.maximum(jnp.max(s, axis=-1, keepdims=True), snk)
    p = jnp.exp(s - m)
    den = jnp.sum(p, axis=-1, keepdims=True) + jnp.exp(snk - m)
    o = jnp.einsum('bngrqk,bnkgd->bnqgrd', p / den, vb.astype(jnp.float32))
    o = o.reshape(B, S, E_MIX).astype(u.dtype)
    return (o * jax.nn.silu(z)) @ w_out


def _trunk(x, c, w_mod, b_mod, ln_g, ln_b, w_in_a, w_out_a, w_in_b, w_out_b, sink_b):
    for l in range(DEPTH):
        mod = jax.nn.silu(c) @ w_mod[l] + b_mod[l]
        shift, scale, gate = jnp.split(mod[:, None, :], 3, axis=-1)
        u = x * (1 + scale) + shift
        if l % N_MIXERS == 0:
            y = _mixer_a(u, w_in_a[l // N_MIXERS], w_out_a[l // N_MIXERS])
        else:
            y = _mixer_b(u, w_in_b[l // N_MIXERS], w_out_b[l // N_MIXERS], sink_b[l // N_MIXERS])
        x = _layernorm(DEEPNORM_ALPHA * x + gate * y, ln_g[l], ln_b[l])
    return x


def setup_inputs(seed: int = 0) -> dict:
    key = jax.random.key(seed)
    ks = jax.random.split(key, 15)
    f32 = jnp.float32
    d_sc = D_MODEL ** -0.5
    e_sc = E_MIX ** -0.5
    return {
        "x_prompt": jax.random.normal(ks[0], (BATCH, SEQ, D_MODEL), f32),
        "x_sample": jax.random.normal(ks[1], (DEC_BATCH, DEC_SEQ, D_MODEL), f32),
        "c_prompt": jax.random.normal(ks[2], (BATCH, D_MODEL), f32),
        "c_sample": jax.random.normal(ks[3], (DEC_BATCH, D_MODEL), f32),
        "w_mod": jax.random.normal(ks[4], (DEPTH, D_MODEL, 3 * D_MODEL), f32) * (0.5 * d_sc),
        "b_mod": jax.random.normal(ks[5], (DEPTH, 3 * D_MODEL), f32) * 0.01,
        "ln_g": 1.0 + 0.02 * jax.random.normal(ks[6], (DEPTH, D_MODEL), f32),
        "ln_b": 0.02 * jax.random.normal(ks[7], (DEPTH, D_MODEL), f32),
        "w_in_a": jax.random.normal(ks[8], (N_LAYERS_A, D_MODEL, IN_A), f32) * d_sc,
        "w_out_a": jax.random.normal(ks[9], (N_LAYERS_A, E_MIX, D_MODEL), f32) * (e_sc * DEEPNORM_BETA),
        "w_in_b": jax.random.normal(ks[10], (N_LAYERS_B, D_MODEL, IN_B), f32) * d_sc,
        "w_out_b": jax.random.normal(ks[11], (N_LAYERS_B, E_MIX, D_MODEL), f32) * (e_sc * DEEPNORM_BETA),
        "sink_b": jax.random.normal(ks[12], (N_LAYERS_B, H_B), f32),
    }


def reference(x_prompt, x_sample, c_prompt, c_sample, w_mod, b_mod, ln_g, ln_b,
              w_in_a, w_out_a, w_in_b, w_out_b, sink_b):
    y_prompt = _trunk(x_prompt, c_prompt, w_mod, b_mod, ln_g, ln_b, w_in_a, w_out_a, w_in_b, w_out_b, sink_b)
    y_sample = _trunk(x_sample, c_sample, w_mod, b_mod, ln_g, ln_b, w_in_a, w_out_a, w_in_b, w_out_b, sink_b)
    return (y_prompt, y_sample)
```

```python
import contextlib
import numpy as np
import concourse.bass as bass
import concourse.mybir as mybir
from concourse.bass_utils import run_bass_kernel_spmd

F32 = mybir.dt.float32
BF16 = mybir.dt.bfloat16
ALU = mybir.AluOpType
AF = mybir.ActivationFunctionType

NCORES = 8
D = 1024
TOK = 8192
SB = 2048
NSB = TOK // SB
DEPTH = 4
ALPHA = (2.0 * DEPTH) ** 0.25
LN_EPS = 1e-5
A_DILS = (1, 4, 16)

LT = {
    "A": dict(groups=[(1, 64), (4, 64), (16, 64)], nfm=7, ngrp_fm=14, ngrp_v=6, W=256),
    "B": dict(groups=[(1, 128)], nfm=3, ngrp_fm=6, ngrp_v=2, W=384),
}
def fm_types(t):
    return 7 if t == "A" else 3


def layer_type(l):
    return "A" if l % 2 == 0 else "B"


class Ev:
    def __init__(self, nc, stack, name):
        self.h = stack.enter_context(nc.semaphore(name))
        self.n = 0


class Prog:
    def __init__(self, nc, stack):
        self.nc = nc
        self.stack = stack
        self.q = {k: [] for k in ("pe", "act", "dve", "pool", "sp")}
        self.nev = 0

    def ev(self, name):
        self.nev += 1
        return Ev(self.nc, self.stack, f"{name}_{self.nev}")

    def op(self, eng, fn, ev=None, amt=1):
        self.q[eng].append((fn, ev, amt))
        if ev is not None:
            ev.n += amt
            return ev.n
        return None

    def dma(self, fn, ev):
        return self.op("sp", fn, ev, 16)

    def wait(self, eng, ev, val):
        if val <= 0:
            return
        self.q[eng].append(("wait", ev, val))

    def replay(self, eng_name, eng):
        last_wait = {}
        for item in self.q[eng_name]:
            if item[0] == "wait":
                _, ev, val = item
                if last_wait.get(id(ev), 0) >= val:
                    continue
                last_wait[id(ev)] = val
                eng.wait_ge(ev.h, val)
            else:
                fn, ev, amt = item
                ins = fn(eng)
                if ev is not None:
                    ins.then_inc(ev.h, amt)


def build_program(nlayers=DEPTH):
    nc = bass.Bass("TRN2", target_bir_lowering=False)
    dt = nc.dram_tensor
    x_in = dt("x", [TOK, D], F32, kind="ExternalInput").ap()
    cvec = dt("cvec", [2, D], F32, kind="ExternalInput").ap()
    ident_d = dt("ident", [128, 128], F32, kind="ExternalInput").ap()
    w_d = []
    for l in range(nlayers):
        t = LT[layer_type(l)]
        w_d.append(dt(f"w{l}", [t["ngrp_fm"] + t["ngrp_v"], 128, 8, 512], F32, kind="ExternalInput").ap())
    wo_d = dt("wo", [DEPTH, 128, 8, D], F32, kind="ExternalInput").ap()
    wm_d = dt("wm", [DEPTH, 6, 128, 8, 512], F32, kind="ExternalInput").ap()
    bmod_d = dt("bmod", [DEPTH, 3 * D], F32, kind="ExternalInput").ap()
    lng_d = dt("lng", [DEPTH, D], F32, kind="ExternalInput").ap()
    lnb_d = dt("lnb", [DEPTH, D], F32, kind="ExternalInput").ap()
    sink_d = dt("sink", [2, 16], F32, kind="ExternalInput").ap()
    dA_d = dt("dA", [8, 3, 3, 128, 2, 512], F32, kind="ExternalInput").ap()
    dB_d = dt("dB", [8, 1, 3, 128, 2, 384], F32, kind="ExternalInput").ap()
    y_out = dt("y", [TOK, D], F32, kind="ExternalOutput").ap()
    xs = [dt(f"xs{i}", [TOK, D], F32).ap() for i in range(2)]
    uT_d = dt("uT", [NSB, 128, 8, SB], BF16).ap()
    fm_d = dt("fm", [7, 8, 128, TOK], BF16).ap()
    Vs_d = dt("Vs", [3, 8, 128, 64, 256], BF16).ap()
    gs_d = dt("gs", [8, 128, TOK], BF16).ap()

    stack = contextlib.ExitStack()
    with stack:
        sb = lambda name, shape, dtp: stack.enter_context(nc.sbuf_tensor(name, shape, dtp))
        BFA = sb("bfa", [128, 44032], BF16)
        FA = sb("fa", [128, 12288], F32)
        MOD = sb("modt", [128, 8, D], F32)
        screp = sb("screp", [128, 2, 8, 128], F32)
        ident = sb("identt", [128, 128], F32)
        small = sb("smallt", [128, 64], F32)
        esink = sb("esink", [128, 32], F32)
        csb = sb("csb", [128, 16], F32)
        biasb = sb("biasb", [128, 2, 512], F32)
        PS = stack.enter_context(nc.psum_tensor("ps", [128, 8, 512], F32))
        pg = Prog(nc, stack)

        GATE = [MOD[:, 0, :], MOD[:, 1, :]]
        SC1 = [MOD[:, 2, :], MOD[:, 3, :]]
        SH = [MOD[:, 4, :], MOD[:, 5, :]]
        LNG = MOD[:, 6, :]
        LNB = MOD[:, 7, :]

        misc_ld = pg.ev("miscld")

        pg.dma(lambda e: e.dma_start(out=ident[:], in_=ident_d[:, :]), misc_ld)
        pg.dma(lambda e: e.dma_start(out=csb[:, :].rearrange("p (c k) -> p c k", c=2),
                                     in_=cvec.rearrange("c (k p) -> p c k", p=128),
                                     allow_slow_non_contiguous=True), misc_ld)
        pg.dma(lambda e: e.dma_start(out=esink[:, :], in_=sink_d.rearrange("a h -> (a h)").partition_broadcast(128)), misc_ld)
        pro_ev = pg.ev("pro")
        pg.wait("act", misc_ld, misc_ld.n)
        pg.op("act", lambda e: e.activation(out=csb[:, :], in_=csb[:, :], func=AF.Silu), pro_ev)
        pg.op("act", lambda e: e.activation(out=esink[:, :], in_=esink[:, :], func=AF.Exp), pro_ev)
        pg.op("pool", lambda e: e.memset(screp[:], 1.0), pro_ev)
        pg.wait("dve", pro_ev, pro_ev.n)
        pro2 = pg.ev("pro2")
        for c in range(2):
            for k in range(8):
                pg.op("dve", lambda e, c=c, k=k: e.tensor_scalar(
                    out=screp[:, c, k, :], in0=screp[:, c, k, :], scalar1=csb[:, c * 8 + k:c * 8 + k + 1],
                    scalar2=None, op0=ALU.mult), pro2)

        wst_views = [FA[:, 0:4096].rearrange("p (k c) -> p k c", k=8), FA[:, 4096:8192].rearrange("p (k c) -> p k c", k=8)]
        m_ld = pg.ev("mld")
        m_pe = pg.ev("mpe")
        m_dve = pg.ev("mdve")

        def phase_M(l):
            jobs = []
            if l >= 0:
                jobs += [(l, 4, GATE, 0, 0.0), (l, 5, GATE, 1, 0.0)]
            if l + 1 < nlayers:
                jobs += [(l + 1, 0, SH, 0, 0.0), (l + 1, 1, SH, 1, 0.0), (l + 1, 2, SC1, 0, 1.0), (l + 1, 3, SC1, 1, 1.0)]
            if l >= 0:
                pg.wait("sp", m_dve, m_dve.n)
                pg.dma(lambda e: e.dma_start(out=LNG, in_=lng_d[l, :].partition_broadcast(128)), m_ld)
                pg.dma(lambda e: e.dma_start(out=LNB, in_=lnb_d[l, :].partition_broadcast(128)), m_ld)
            for ji, (ll, cg, tgt, half, add1) in enumerate(jobs):
                slot = ji % 2
                pg.wait("sp", m_pe, m_pe.n)
                pg.wait("sp", m_dve, m_dve.n)
                pg.dma(lambda e, ll=ll, cg=cg, slot=slot: e.dma_start(out=wst_views[slot], in_=wm_d[ll, cg]), m_ld)
                pg.dma(lambda e, ll=ll, cg=cg, slot=slot: e.dma_start(
                    out=biasb[:, slot, :], in_=bmod_d[ll, cg * 512:(cg + 1) * 512].partition_broadcast(128)), m_ld)
                pg.wait("pe", m_ld, m_ld.n)
                pg.wait("pe", pro2, pro2.n)
                pg.wait("pe", m_dve, m_dve.n)
                for c in range(2):
                    for k in range(8):
                        pg.op("pe", lambda e, c=c, k=k, slot=slot: e.matmul(
                            PS[:, c, :], lhsT=screp[:, c, k, :], rhs=wst_views[slot][:, k, :],
                            start=(k == 0), stop=(k == 7)), m_pe if (c == 1 and k == 7) else None)
                pg.wait("dve", m_pe, m_pe.n)
                for c in range(2):
                    pg.op("dve", lambda e, c=c, tgt=tgt, half=half, add1=add1, slot=slot: e.scalar_tensor_tensor(
                        out=tgt[c][:, half * 512:(half + 1) * 512], in0=PS[:, c, :], scalar=add1,
                        in1=biasb[:, slot, :], op0=ALU.add, op1=ALU.add), m_dve)
            return

        gT = [BFA[:, i * 4096:(i + 1) * 4096].rearrange("p (k t) -> p k t", k=8) for i in range(2)]
        wo_bf = BFA[:, 8192:16384].rearrange("p (k c) -> p k c", k=8)
        uTst = [BFA[:, 16384 + i * 4096:16384 + (i + 1) * 4096].rearrange("p (k t) -> p k t", k=8) for i in range(2)]
        xt = [FA[:, i * 1024:(i + 1) * 1024] for i in range(4)]
        ht = [FA[:, 4096 + i * 1024:4096 + (i + 1) * 1024] for i in range(2)]
        ut = [FA[:, 6144 + i * 1024:6144 + (i + 1) * 1024] for i in range(2)]
        junk = BFA[:, 24576:25600]
        wo_st = FA[:, 8192:12288].rearrange("p (k c) -> p k c", k=8)

        o_gld = [pg.ev(f"ogld{i}") for i in range(2)]
        o_xld = [pg.ev(f"oxld{i}") for i in range(4)]
        o_wld = pg.ev("owld")
        o_wcast = pg.ev("owcast")
        o_y = [pg.ev(f"oy{i}") for i in range(2)]
        o_h = [pg.ev(f"oh{i}") for i in range(2)]
        o_h2 = [pg.ev(f"oh2{i}") for i in range(2)]
        o_sq = [pg.ev(f"osq{i}") for i in range(2)]
        o_st = [pg.ev(f"ost{i}") for i in range(2)]
        o_xn = [pg.ev(f"oxn{i}") for i in range(2)]
        o_xo = [pg.ev(f"oxo{i}") for i in range(4)]
        o_xst = [pg.ev(f"oxst{i}") for i in range(4)]
        o_u1 = [pg.ev(f"ou1{i}") for i in range(2)]
        o_u2 = [pg.ev(f"ou2{i}") for i in range(2)]
        o_tp = [pg.ev(f"otp{i}") for i in range(2)]
        o_te = [pg.ev(f"ote{i}") for i in range(2)]
        o_ust = [pg.ev(f"oust{i}") for i in range(2)]
        cnt = dict(sub=0, tg=0)
        xt_free = [[], [], [], []]
        ht_free = [None, None]

        def phase_O(l):
            full = l >= 0
            last = (l == nlayers - 1)
            src = x_in if l <= 0 else xs[(l - 1) % 2]
            dst = y_out if last else xs[l % 2]
            mod_ready_dve = m_dve.n
            mod_ready_ld = m_ld.n
            pg.wait("sp", m_pe, m_pe.n)
            pg.wait("sp", m_dve, m_dve.n)
            if full:
                for hlf in range(2):
                    pg.wait("sp", o_wcast, o_wcast.n)
                    pg.dma(lambda e, hlf=hlf: e.dma_start(out=wo_st, in_=wo_d[l, :, :, hlf * 512:(hlf + 1) * 512]), o_wld)
                    pg.wait("pool", o_wld, o_wld.n)
                    pg.op("pool", lambda e, hlf=hlf: e.tensor_copy(out=wo_bf[:, :, hlf * 512:(hlf + 1) * 512], in_=wo_st), o_wcast)
            base_sub = cnt["sub"]
            base_tg = cnt["tg"]
            NSUB = TOK // 128

            def emit_xload(i):
                si = base_sub + i
                xs_ = si % 4
                for (ev_, n_) in xt_free[xs_]:
                    pg.wait("sp", ev_, n_)
                xt_free[xs_] = []
                row0 = i * 128
                pg.dma(lambda e: e.dma_start(out=xt[xs_], in_=src[row0:row0 + 128, :]), o_xld[xs_])
                return o_xld[xs_].n

            def emit_gload(tg):
                gslot = (base_tg + tg) % 2
                pg.wait("sp", o_y[0], o_y[0].n)
                pg.wait("sp", o_y[1], o_y[1].n)
                pg.dma(lambda e: e.dma_start(
                    out=gT[gslot], in_=gs_d[:, :, tg * 512:(tg + 1) * 512].rearrange("k p t -> p k t")), o_gld[gslot])
                return o_gld[gslot].n

            pend_u = []
            xld_n = {0: emit_xload(0)}
            gld_n = {}
            if full:
                gld_n[0] = emit_gload(0)
            for tg in range(TOK // 512):
                c = tg // 8
                gi = base_tg + tg
                gslot = gi % 2
                uslot = gi % 2
                if full and tg + 1 < TOK // 512:
                    gld_n[tg + 1] = emit_gload(tg + 1)
                for s in range(4):
                    i = tg * 4 + s
                    si = base_sub + i
                    xs_ = si % 4
                    p2 = si % 2
                    row0 = i * 128
                    if i + 1 < NSUB:
                        xld_n[i + 1] = emit_xload(i + 1)
                    b = 8 * p2
                    if full:
                        pg.wait("pe", o_gld[gslot], gld_n[tg])
                        pg.wait("pe", o_wcast, o_wcast.n)
                        pg.wait("pe", o_h[p2], o_h[p2].n)
                        for hlf in range(2):
                            for k in range(8):
                                pg.op("pe", lambda e, hlf=hlf, k=k, p2=p2, gslot=gslot, s=s: e.matmul(
                                    PS[:, 2 * p2 + hlf, :], lhsT=gT[gslot][:, k, s * 128:(s + 1) * 128],
                                    rhs=wo_bf[:, k, hlf * 512:(hlf + 1) * 512], start=(k == 0), stop=(k == 7)),
                                    o_y[p2] if (hlf == 1 and k == 7) else None)
                        Y = PS[:, 2 * p2:2 * p2 + 2, :].rearrange("p a b -> p (a b)")
                        pg.wait("dve", o_y[p2], o_y[p2].n)
                        pg.wait("dve", m_dve, mod_ready_dve)
                        pg.op("dve", lambda e, b=b: e.memset(small[:, b:b + 2], 0.0))
                        pg.op("dve", lambda e, p2=p2, Y=Y, c=c: e.tensor_tensor(out=ht[p2], in0=Y, in1=GATE[c], op=ALU.mult), o_h[p2])
                        pg.wait("dve", o_xld[xs_], xld_n[i])
                        pg.wait("dve", o_h[p2], o_h[p2].n)
                        pg.op("dve", lambda e, p2=p2, xs_=xs_, b=b: e.scalar_tensor_tensor(
                            out=ht[p2], in0=xt[xs_], scalar=ALPHA, in1=ht[p2], op0=ALU.mult, op1=ALU.add,
                            accum_out=small[:, b:b + 1]), o_h2[p2])
                        pg.wait("act", o_h2[p2], o_h2[p2].n)
                        pg.op("act", lambda e, p2=p2, b=b: e.activation(out=junk, in_=ht[p2], func=AF.Square,
                                                                        accum_out=small[:, b + 1:b + 2]), o_sq[p2])
                        pg.wait("dve", o_sq[p2], o_sq[p2].n)
                        ops = [
                            lambda e, b=b: e.tensor_scalar(out=small[:, b + 2:b + 3], in0=small[:, b:b + 1], scalar1=-1.0 / D, scalar2=None, op0=ALU.mult),
                            lambda e, b=b: e.tensor_tensor(out=small[:, b + 3:b + 4], in0=small[:, b + 2:b + 3], in1=small[:, b + 2:b + 3], op=ALU.mult),
                            lambda e, b=b: e.scalar_tensor_tensor(out=small[:, b + 4:b + 5], in0=small[:, b + 1:b + 2], scalar=1.0 / D, in1=small[:, b + 3:b + 4], op0=ALU.mult, op1=ALU.subtract),
                            lambda e, b=b: e.tensor_scalar(out=small[:, b + 4:b + 5], in0=small[:, b + 4:b + 5], scalar1=LN_EPS, scalar2=None, op0=ALU.add),
                        ]
                        for f in ops:
                            n = pg.op("dve", f, o_st[p2])
                            pg.wait("dve", o_st[p2], n)
                        pg.wait("act", o_st[p2], o_st[p2].n)
                        n = pg.op("act", lambda e, b=b: e.activation(out=small[:, b + 5:b + 6], in_=small[:, b + 4:b + 5], func=AF.Sqrt), o_sq[p2])
                        pg.wait("dve", o_sq[p2], n)
                        n = pg.op("dve", lambda e, b=b: e.reciprocal(out=small[:, b + 5:b + 6], in_=small[:, b + 5:b + 6]), o_st[p2])
                        pg.wait("dve", o_st[p2], n)
                        pg.wait("dve", m_ld, mod_ready_ld)
                        n = pg.op("dve", lambda e, p2=p2, b=b: e.scalar_tensor_tensor(
                            out=ht[p2], in0=ht[p2], scalar=small[:, b + 2:b + 3], in1=LNG, op0=ALU.add, op1=ALU.mult), o_xn[p2])
                        pg.wait("dve", o_xn[p2], n)
                        pg.op("dve", lambda e, p2=p2, xs_=xs_, b=b: e.scalar_tensor_tensor(
                            out=xt[xs_], in0=ht[p2], scalar=small[:, b + 5:b + 6], in1=LNB, op0=ALU.mult, op1=ALU.add), o_xo[xs_])
                        pg.wait("sp", o_xo[xs_], o_xo[xs_].n)
                        pg.dma(lambda e, xs_=xs_, row0=row0: e.dma_start(out=dst[row0:row0 + 128, :], in_=xt[xs_]), o_xst[xs_])
                        xt_free[xs_].append((o_xst[xs_], o_xst[xs_].n))
                    if not last:
                        xo_n = o_xo[xs_].n

                        def ublock(i=i, tg=tg, s=s, xs_=xs_, p2=p2, c=c, uslot=uslot, xo_n=xo_n):
                            if full:
                                pg.wait("pool", o_xo[xs_], xo_n)
                            else:
                                pg.wait("pool", o_xld[xs_], xld_n[i])
                            pg.wait("pool", m_dve, mod_ready_dve)
                            pg.wait("pool", o_tp[p2], o_tp[p2].n)
                            n = pg.op("pool", lambda e: e.tensor_tensor(out=ut[p2], in0=xt[xs_], in1=SC1[c], op=ALU.mult), o_u1[p2])
                            xt_free[xs_].append((o_u1[p2], o_u1[p2].n))
                            pg.wait("pool", o_u1[p2], n)
                            pg.op("pool", lambda e: e.tensor_tensor(out=ut[p2], in0=ut[p2], in1=SH[c], op=ALU.add), o_u2[p2])
                            pg.wait("pe", o_u2[p2], o_u2[p2].n)
                            pg.wait("pe", misc_ld, misc_ld.n)
                            pg.wait("pe", o_te[p2], o_te[p2].n)
                            for k in range(8):
                                pg.op("pe", lambda e, k=k: e.transpose(
                                    out=PS[:, 4 + 2 * p2 + k // 4, (k % 4) * 128:(k % 4 + 1) * 128],
                                    in_=ut[p2][:, k * 128:(k + 1) * 128], identity=ident[:]),
                                    o_tp[p2] if k == 7 else None)
                            pg.wait("act", o_tp[p2], o_tp[p2].n)
                            if s == 0:
                                pg.wait("act", o_ust[uslot], o_ust[uslot].n)
                            pg.op("act", lambda e: e.activation(
                                out=uTst[uslot][:, :, s * 128:(s + 1) * 128],
                                in_=PS[:, 4 + 2 * p2:6 + 2 * p2, :].rearrange("p a (b t) -> p (a b) t", t=128), func=AF.Identity), o_te[p2])
                            if s == 3:
                                pg.wait("sp", o_te[0], o_te[0].n)
                                pg.wait("sp", o_te[1], o_te[1].n)
                                J = tg // 4
                                pg.dma(lambda e: e.dma_start(
                                    out=uT_d[J, :, :, (tg % 4) * 512:(tg % 4 + 1) * 512], in_=uTst[uslot]), o_ust[uslot])
                        pend_u.append(ublock)
                        while len(pend_u) > 2:
                            pend_u.pop(0)()
            while pend_u:
                pend_u.pop(0)()
            cnt["sub"] += NSUB
            cnt["tg"] += TOK // 512
            for evs in (o_xst, o_ust):
                for e_ in evs:
                    pg.wait("sp", e_, e_.n)

        uT_sb = BFA[:, 0:16384].rearrange("p (k t) -> p k t", k=8)
        wbf = [BFA[:, 16384 + i * 4096:16384 + (i + 1) * 4096].rearrange("p (k c) -> p k c", k=8) for i in range(2)]
        fstage = [BFA[:, 24576 + i * 2048:24576 + (i + 1) * 2048] for i in range(2)]
        vstage = [BFA[:, 28672 + i * 4096:28672 + (i + 1) * 4096].rearrange("p (t a c) -> p t a c", t=4, a=4) for i in range(2)]
        wst3 = [FA[:, i * 4096:(i + 1) * 4096].rearrange("p (k c) -> p k c", k=8) for i in range(3)]
        p_uld = pg.ev("puld")
        p_wld = [pg.ev(f"pwld{i}") for i in range(3)]
        p_cast = pg.ev("pcast")
        p_mm = [pg.ev(f"pmm{i}") for i in range(2)]
        p_evf = [pg.ev(f"pevf{i}") for i in range(2)]
        p_evv = [pg.ev(f"pevv{i}") for i in range(2)]
        p_fst = [pg.ev(f"pfst{i}") for i in range(2)]
        p_vst = [pg.ev(f"pvst{i}") for i in range(2)]
        p_ones = pg.ev("pones")
        pc = dict(w=0, set=0, f=0, v=0, ev=0)
        set_free = [None, None]

        def phase_P(l):
            ltype = layer_type(l)
            t = LT[ltype]
            groups = t["groups"]
            W = w_d[l]
            ngf, ngv = t["ngrp_fm"], t["ngrp_v"]
            nfm = t["nfm"]
            pg.wait("pool", p_vst[0], p_vst[0].n)
            pg.wait("pool", p_vst[1], p_vst[1].n)
            for i in range(2):
                pg.op("pool", lambda e, i=i: e.memset(vstage[i][:, :, :, 64:192], 1.0), p_ones)
            seq = [(J, wg) for J in range(NSB) for wg in range(ngf + ngv)]
            wbase = pc["w"]
            ldn = {}

            def emit_wload(i):
                w = wbase + i
                s3 = w % 3
                pg.wait("sp", p_cast, w - 2)
                wg_ = seq[i][1]
                pg.dma(lambda e: e.dma_start(out=wst3[s3], in_=W[wg_]), p_wld[s3])
                ldn[i] = p_wld[s3].n

            emit_wload(0)
            emit_wload(1)
            for i_, (J, wg) in enumerate(seq):
                if wg == 0:
                    pg.wait("sp", p_mm[0], p_mm[0].n)
                    pg.wait("sp", p_mm[1], p_mm[1].n)
                    pg.dma(lambda e, J=J: e.dma_start(out=uT_sb, in_=uT_d[J]), p_uld)
                if i_ + 2 < len(seq):
                    emit_wload(i_ + 2)
                if True:
                    wi = wbase + i_
                    s3 = wi % 3
                    ws = wi % 2
                    pg.wait("pool", p_wld[s3], ldn[i_])
                    pg.wait("pool", p_mm[0], pc.get(("mm0", ws), 0))
                    pg.wait("pool", p_mm[1], pc.get(("mm1", ws), 0))
                    pg.op("pool", lambda e, ws=ws, s3=s3: e.tensor_copy(out=wbf[ws], in_=wst3[s3]), p_cast)
                    assert p_cast.n == wi + 1
                    pg.wait("pe", p_cast, wi + 1)
                    pg.wait("pe", p_uld, p_uld.n)
                    if wg < ngf:
                        for b in range(4):
                            bi = wg * 4 + b
                            ty, pair = bi // 8, bi % 8
                            if ltype == "A":
                                is_z = (ty == 6)
                                d = 1 if is_z else groups[ty // 2][0]
                            else:
                                is_z = (ty == 2)
                                d = 1
                            si = pc["set"]
                            pc["set"] += 1
                            S_ = si % 2
                            if set_free[S_] is not None:
                                pg.wait("pe", set_free[S_][0], set_free[S_][1])
                            for k in range(8):
                                for s in range(4):
                                    pg.op("pe", lambda e, S_=S_, k=k, s=s, ws=ws, b=b: e.matmul(
                                        PS[:, 4 * S_ + s, :], lhsT=wbf[ws][:, k, b * 128:(b + 1) * 128],
                                        rhs=uT_sb[:, k, s * 512:(s + 1) * 512], start=(k == 0), stop=(k == 7)),
                                        p_mm[S_] if (k == 7 and s == 3) else None)
                            fi = pc["f"]
                            pc["f"] += 1
                            fs = fi % 2
                            eng = "act" if (is_z or fi % 2 == 0) else "dve"
                            pg.wait(eng, p_mm[S_], p_mm[S_].n)
                            pg.wait(eng, p_fst[fs], p_fst[fs].n)
                            src_ap = PS[:, 4 * S_:4 * S_ + 4, :].rearrange("p a b -> p (a b)")
                            if d > 1:
                                src_ap = src_ap.rearrange("p (u r) -> p r u", r=d)
                                dst_ap = fstage[fs].rearrange("p (r u) -> p r u", r=d)
                            else:
                                dst_ap = fstage[fs]
                            if eng == "act":
                                fn = AF.Silu if is_z else AF.Identity
                                pg.op("act", lambda e, dst_ap=dst_ap, src_ap=src_ap, fn=fn: e.activation(out=dst_ap, in_=src_ap, func=fn), p_evf[fs])
                            else:
                                pg.op("dve", lambda e, dst_ap=dst_ap, src_ap=src_ap: e.tensor_copy(out=dst_ap, in_=src_ap), p_evf[fs])
                            set_free[S_] = (p_evf[fs], p_evf[fs].n)
                            n = SB // d
                            pg.wait("sp", p_evf[fs], p_evf[fs].n)
                            pg.dma(lambda e, ty=ty, pair=pair, fs=fs, d=d, n=n, J=J: e.dma_start(
                                out=fm_d[ty, pair].rearrange("p (r u) -> p r u", r=d)[:, :, J * n:(J + 1) * n],
                                in_=fstage[fs].rearrange("p (r u) -> p r u", r=d)), p_fst[fs])
                    else:
                        vg = wg - ngf
                        g, ph = vg // 2, vg % 2
                        d = groups[g][0]
                        for sidx in range(4):
                            if d == 1:
                                tiles = [(0, 4 * sidx + i) for i in range(4)]
                            elif d == 4:
                                tiles = [(sidx, i) for i in range(4)]
                            else:
                                tiles = [(4 * sidx + i, 0) for i in range(4)]
                            si = pc["set"]
                            pc["set"] += 1
                            S_ = si % 2
                            if set_free[S_] is not None:
                                pg.wait("pe", set_free[S_][0], set_free[S_][1])
                            for ti, (r, mp) in enumerate(tiles):
                                for k in range(8):
                                    if d > 1:
                                        lh = uT_sb[:, k, :].rearrange("p (u r) -> p r u", r=d)[:, r, mp * 128:(mp + 1) * 128]
                                    else:
                                        lh = uT_sb[:, k, mp * 128:(mp + 1) * 128]
                                    pg.op("pe", lambda e, S_=S_, ti=ti, k=k, ws=ws, lh=lh: e.matmul(
                                        PS[:, 4 * S_ + ti, :], lhsT=lh, rhs=wbf[ws][:, k, :], start=(k == 0), stop=(k == 7)),
                                        p_mm[S_] if (k == 7 and ti == 3) else None)
                            vi = pc["v"]
                            pc["v"] += 1
                            vs = vi % 2
                            veng = "act" if vi % 2 == 0 else "dve"
                            pg.wait(veng, p_mm[S_], p_mm[S_].n)
                            pg.wait(veng, p_vst[vs], p_vst[vs].n)
                            pg.wait(veng, p_ones, p_ones.n)
                            srcv = PS[:, 4 * S_:4 * S_ + 4, :].rearrange("p t (a j e) -> p t a j e", a=4, j=2)
                            for j_, c0 in ((0, 0), (1, 192)):
                                if veng == "act":
                                    pg.op("act", lambda e, vs=vs, srcv=srcv, j_=j_, c0=c0: e.activation(
                                        out=vstage[vs][:, :, :, c0:c0 + 64], in_=srcv[:, :, :, j_, :], func=AF.Identity), p_evv[vs])
                                else:
                                    pg.op("dve", lambda e, vs=vs, srcv=srcv, j_=j_, c0=c0: e.tensor_copy(
                                        out=vstage[vs][:, :, :, c0:c0 + 64], in_=srcv[:, :, :, j_, :]), p_evv[vs])
                            set_free[S_] = (p_evv[vs], p_evv[vs].n)
                            pg.wait("sp", p_evv[vs], p_evv[vs].n)
                            ntr = 64 // d
                            for ti, (r, mp) in enumerate(tiles):
                                kt = r * ntr + J * (16 // d) + mp
                                pg.dma(lambda e, g=g, ph=ph, kt=kt, vs=vs, ti=ti: e.dma_start(
                                    out=Vs_d[g, 4 * ph:4 * ph + 4, :, kt, :].rearrange("a k c -> k a c"),
                                    in_=vstage[vs][:, ti, :, :]), p_vst[vs])
                    pc[("mm0", ws)] = p_mm[0].n
                    pc[("mm1", ws)] = p_mm[1].n
            pc["w"] += len(seq)
            for evs in (p_fst, p_vst):
                for e_ in evs:
                    pg.wait("sp", e_, e_.n)

        KT = [BFA[:, i * 3072:(i + 1) * 3072] for i in range(2)]
        QT = [BFA[:, 6144 + i * 2048:6144 + (i + 1) * 2048] for i in range(2)]
        VT = [BFA[:, 10240 + i * 6144:10240 + (i + 1) * 6144].rearrange("p (t c) -> p t c", c=256) for i in range(2)]
        ZT = [BFA[:, 22528 + i * 2048:22528 + (i + 1) * 2048] for i in range(2)]
        GT = [BFA[:, 26624 + i * 2048:26624 + (i + 1) * 2048] for i in range(2)]
        EP = [BFA[:, 30720 + i * 1024:30720 + (i + 1) * 1024].rearrange("p (h n) -> p h n", h=2) for i in range(4)]
        DTB = BFA[:, 34816:34816 + 9216]
        ACC = FA[:, 0:4096].rearrange("p (h t) -> p h t", h=2)
        RS = FA[:, 4096:6144]
        DTS = [FA[:, 6144 + i * 3072:6144 + (i + 1) * 3072] for i in range(2)]

        a_ld = [pg.ev(f"ald{i}") for i in range(2)]
        a_zld = [pg.ev(f"azld{i}") for i in range(2)]
        a_dld = [pg.ev(f"adld{i}") for i in range(2)]
        a_dcast = pg.ev("adcast")
        a_ln = pg.ev("aln")
        a_s = [pg.ev(f"as{i}") for i in range(2)]
        a_e = [pg.ev(f"ae{i}") for i in range(4)]
        a_p = [pg.ev(f"ap{i}") for i in range(4)]
        a_pv = [pg.ev(f"apv{i}") for i in range(4)]
        a_evac = [pg.ev(f"aevac{i}") for i in range(2)]
        a_fin = pg.ev("afin")
        a_gst = [pg.ev(f"agst{i}") for i in range(2)]
        ac = dict(unit=0, batch=0, pj=0, dst=0)
        unit_end = {}
        fin_hist = []
        exp_n = {}

        def phase_A(l):
            ltype = layer_type(l)
            t = LT[ltype]
            groups = t["groups"]
            PW = 512 if ltype == "A" else 384
            dtab = dA_d if ltype == "A" else dB_d
            ng = len(groups)
            zty = fm_types(ltype) - 1
            KEEP = 2
            units = []
            for pair in range(8):
                for J in range(NSB):
                    for g, (d, ns) in enumerate(groups):
                        parts = 2 if d == 16 else 1
                        for part in range(parts):
                            nr = d // parts
                            units.append(dict(pair=pair, J=J, g=g, d=d, ns=ns, r0=part * nr, nr=nr,
                                              first=(g == 0 and part == 0), lastu=(g == ng - 1 and part == parts - 1)))
            ubase = ac["unit"]

            def emit_loads(ui):
                u = units[ui]
                ug = ubase + ui
                slot = ug % 2
                d, ns, J, pair, g = u["d"], u["ns"], u["J"], u["pair"], u["g"]
                Lr = TOK // d
                n = SB // d
                P0 = J * n
                KW = n + 256
                klo, khi = max(0, P0 - 128), min(Lr, P0 + n + 128)
                ntw = n // 128 + 2
                mlo, mhi = max(0, P0 // 128 - 1), min(Lr // 128 - 1, (P0 + n) // 128)
                if ltype == "A":
                    kty, qty = 2 * g + 1, 2 * g
                else:
                    kty, qty = 1, 0
                r0, nr = u["r0"], u["nr"]
                if (ug - 2) in unit_end:
                    pg.wait("sp", a_evac[0], unit_end[ug - 2][0])
                    pg.wait("sp", a_evac[1], unit_end[ug - 2][1])
                else:
                    assert ug - 2 < ubase or ug < 2, (ug, ubase)
                pg.dma(lambda e: e.dma_start(
                    out=KT[slot][:, 0:nr * KW].rearrange("p (r w) -> p r w", r=nr)[:, :, klo - (P0 - 128):khi - (P0 - 128)],
                    in_=fm_d[kty, pair].rearrange("p (r u) -> p r u", r=d)[:, r0:r0 + nr, klo:khi]), a_ld[slot])
                pg.dma(lambda e: e.dma_start(
                    out=QT[slot][:, 0:nr * n].rearrange("p (r w) -> p r w", r=nr),
                    in_=fm_d[qty, pair].rearrange("p (r u) -> p r u", r=d)[:, r0:r0 + nr, P0:P0 + n]), a_ld[slot])
                t0 = mlo - (P0 // 128 - 1)
                t1 = mhi + 1 - (P0 // 128 - 1)
                pg.dma(lambda e: e.dma_start(
                    out=VT[slot][:, 0:nr * ntw, :].rearrange("p (r t) c -> p r t c", r=nr)[:, :, t0:t1, :],
                    in_=Vs_d[g, pair].rearrange("k (r t) c -> k r t c", r=d)[:, r0:r0 + nr, mlo:mhi + 1, :]), a_ld[slot])
                u["slot"] = slot
                u["ldn"] = a_ld[slot].n

            pending = []

            def flush(keep):
                while len(pending) > keep:
                    pending.pop(0)[1]()

            cur_zs = 0
            emit_loads(0)
            loaded = 0
            for ui, u in enumerate(units):
                slot = u["slot"]
                d, ns, J, pair, g = u["d"], u["ns"], u["J"], u["pair"], u["g"]
                Lr = TOK // d
                n = SB // d
                P0 = J * n
                KW = n + 256
                ntw = n // 128 + 2
                mid = Lr // 2
                r0, nr = u["r0"], u["nr"]
                ug = ubase + ui
                if u["first"]:
                    pj = ac["pj"]
                    ac["pj"] += 1
                    cur_zs = pj % 2
                    zs = cur_zs
                    if len(fin_hist) >= 2:
                        pg.wait("sp", a_fin, fin_hist[-2])
                    pg.dma(lambda e, zs=zs, pair=pair, J=J: e.dma_start(
                        out=ZT[zs], in_=fm_d[zty, pair][:, J * SB:(J + 1) * SB]), a_zld[zs])
                    if J == 0:
                        for i_ in range(4):
                            pg.wait("pool", a_p[i_], a_p[i_].n)
                        for gg in range(ng):
                            di = ac["dst"]
                            ac["dst"] += 1
                            hs = di % 2
                            pg.wait("sp", a_dcast, max(0, di - 1))
                            pg.dma(lambda e, gg=gg, pair=pair, hs=hs: e.dma_start(
                                out=DTS[hs][:, 0:6 * PW].rearrange("p (v h w) -> p v h w", v=3, h=2),
                                in_=dtab[pair, gg].rearrange("v p h w -> p v h w")), a_dld[hs])
                            pg.wait("pool", a_dld[hs], a_dld[hs].n)
                            pg.op("pool", lambda e, gg=gg, hs=hs: e.tensor_copy(
                                out=DTB[:, gg * 6 * PW:(gg + 1) * 6 * PW], in_=DTS[hs][:, 0:6 * PW]), a_dcast)
                            assert a_dcast.n == di + 1
                packs = []
                if ltype == "B":
                    qsz = 128
                    for qt_i in range(n // qsz):
                        q0 = P0 + qt_i * qsz
                        segs = []
                        for j in range(3):
                            m = q0 // 128 - 1 + j
                            segs.append(dict(ri=0, m=m, col=128 * j, N=128, qlo=q0, ocol=0))
                        packs.append(dict(q0=q0, segs=segs, nruns=1, rr=r0, ocols=qsz))
                elif d < 16:
                    qsz = 256
                    offs, Ns, qoff = [0, 64, 256, 448], [64, 192, 192, 64], [0, 0, 64, 192]
                    for ri in range(nr):
                        for qt_i in range(n // qsz):
                            q0 = P0 + qt_i * qsz
                            segs = []
                            for j in range(4):
                                m = q0 // 128 - 1 + j
                                segs.append(dict(ri=ri, m=m, col=offs[j], N=Ns[j], qlo=q0 + qoff[j], ocol=qoff[j]))
                            packs.append(dict(q0=q0, segs=segs, nruns=1, rr=r0 + ri, ocols=qsz))
                else:
                    qsz = 128
                    offs, Ns, qoff = [0, 64, 192], [64, 128, 64], [0, 0, 64]
                    q0 = P0
                    for rp in range(nr // 2):
                        segs = []
                        for a_ in range(2):
                            ri = 2 * rp + a_
                            for j in range(3):
                                m = q0 // 128 - 1 + j
                                segs.append(dict(ri=ri, m=m, col=256 * a_ + offs[j], N=Ns[j], qlo=q0 + qoff[j], ocol=128 * a_ + qoff[j]))
                        packs.append(dict(q0=q0, segs=segs, nruns=2, rr=r0 + 2 * rp, ocols=256))
                for pi, pk in enumerate(packs):
                    q0 = pk["q0"]
                    segs = [sg for sg in pk["segs"] if 0 <= sg["m"] < Lr // 128]
                    var = 0
                    if q0 == mid - qsz:
                        var = 1
                    elif q0 == mid:
                        var = 2
                    bi = ac["batch"]
                    ac["batch"] += 1
                    bs = bi % 4
                    ss = bi % 2
                    oset = bi % 2
                    pg.wait("pe", a_ld[slot], u["ldn"])
                    if (bi - 2) in exp_n:
                        pg.wait("pe", a_e[(bi - 2) % 4], exp_n[bi - 2])
                    for si_, sg in enumerate(segs):
                        kcol = sg["ri"] * KW + (128 * sg["m"] - (P0 - 128))
                        qcol = sg["ri"] * n + (sg["qlo"] - P0)
                        for h in range(2):
                            pg.op("pe", lambda e, ss=ss, h=h, slot=slot, kcol=kcol, qcol=qcol, sg=sg: e.matmul(
                                PS[:, 2 * ss + h, sg["col"]:sg["col"] + sg["N"]],
                                lhsT=KT[slot][h * 64:(h + 1) * 64, kcol:kcol + 128],
                                rhs=QT[slot][h * 64:(h + 1) * 64, qcol:qcol + sg["N"]], start=True, stop=True),
                                a_s[ss] if (h == 1 and si_ == len(segs) - 1) else None)
                    pg.wait("act", a_s[ss], a_s[ss].n)
                    pg.wait("act", a_pv[bs], a_pv[bs].n)
                    exp_n[bi] = pg.op("act", lambda e, bs=bs, ss=ss: e.activation(
                        out=EP[bs][:, :, 0:PW], in_=PS[:, 2 * ss:2 * ss + 2, 0:PW], func=AF.Exp, scale=0.125), a_e[bs])
                    pg.wait("dve", a_e[bs], a_e[bs].n)
                    pg.wait("dve", a_dcast, a_dcast.n)
                    dview = DTB[:, (g * 3 + var) * 2 * PW:(g * 3 + var + 1) * 2 * PW].rearrange("p (h w) -> p h w", h=2)
                    pg.op("dve", lambda e, bs=bs, dview=dview: e.tensor_tensor(
                        out=EP[bs][:, :, 0:PW], in0=EP[bs][:, :, 0:PW], in1=dview, op=ALU.mult), a_p[bs])
                    p_need = a_p[bs].n
                    ufirst = u["first"]
                    last_of_unit = (pi == len(packs) - 1)
                    tok0 = q0 - P0

                    def pv(bs=bs, oset=oset, slot=slot, segs=segs, pk=pk, p_need=p_need, ufirst=ufirst,
                           last_of_unit=last_of_unit, ug=ug, d=d, ntw=ntw, P0=P0, tok0=tok0, qsz=qsz):
                        pg.wait("pe", a_p[bs], p_need)
                        pg.wait("pe", a_evac[oset], a_evac[oset].n)
                        for h in range(2):
                            for si_, sg in enumerate(segs):
                                vtile = sg["ri"] * ntw + (sg["m"] - (P0 // 128 - 1))
                                pg.op("pe", lambda e, h=h, sg=sg, vtile=vtile, si_=si_: e.matmul(
                                    PS[:, 4 + 2 * oset + h, sg["ocol"]:sg["ocol"] + sg["N"]],
                                    lhsT=VT[slot][:, vtile, h * 128:(h + 1) * 128],
                                    rhs=EP[bs][:, h, sg["col"]:sg["col"] + sg["N"]],
                                    start=(si_ == 0), stop=(si_ == len(segs) - 1), skip_group_check=True),
                                    a_pv[bs] if (h == 1 and si_ == len(segs) - 1) else None)
                        pg.wait("dve", a_pv[bs], a_pv[bs].n)
                        oc = pk["ocols"]
                        osrc = PS[:, 4 + 2 * oset:6 + 2 * oset, 0:oc]
                        if d == 1:
                            accv = ACC[:, :, tok0:tok0 + oc]
                        elif pk["nruns"] == 1:
                            accv = ACC.rearrange("p h (u r) -> p h r u", r=d)[:, :, pk["rr"], tok0:tok0 + oc]
                        else:
                            accv = ACC.rearrange("p h (u r) -> p h r u", r=d)[:, :, pk["rr"]:pk["rr"] + 2, tok0:tok0 + qsz]
                            osrc = osrc.rearrange("p h (a u) -> p h a u", a=2)
                        if ufirst:
                            pg.wait("dve", a_fin, a_fin.n)
                            pg.op("dve", lambda e: e.tensor_copy(out=accv, in_=osrc), a_evac[oset])
                        else:
                            pg.op("dve", lambda e: e.tensor_tensor(out=accv, in0=osrc, in1=accv, op=ALU.add), a_evac[oset])
                        if last_of_unit:
                            unit_end[ug] = (a_evac[0].n, a_evac[1].n)
                    pending.append((ui, pv))
                    flush(KEEP)
                    if loaded == ui and ui + 1 < len(units) and all(tag != ui - 1 for tag, _ in pending):
                        emit_loads(ui + 1)
                        loaded = ui + 1
                if u["lastu"]:
                    flush(0)
                if loaded == ui and ui + 1 < len(units):
                    flush(0)
                    emit_loads(ui + 1)
                    loaded = ui + 1
                if u["lastu"]:
                    zs = cur_zs
                    fin_before = a_fin.n
                    if ltype == "B":
                        pg.wait("dve", a_evac[0], a_evac[0].n)
                        pg.wait("dve", a_evac[1], a_evac[1].n)
                        pg.wait("dve", pro_ev, pro_ev.n)
                        li = l // 2
                        ca, cb = li * 16 + 2 * pair, li * 16 + 2 * pair + 1
                        pg.op("dve", lambda e, ca=ca: e.tensor_scalar(out=ACC[64:128, 0, :], in0=ACC[64:128, 0, :],
                                                                      scalar1=esink[64:128, ca:ca + 1], scalar2=None, op0=ALU.add), a_evac[0])
                        pg.op("dve", lambda e, cb=cb: e.tensor_scalar(out=ACC[0:64, 1, :], in0=ACC[0:64, 1, :],
                                                                      scalar1=esink[0:64, cb:cb + 1], scalar2=None, op0=ALU.add), a_evac[1])
                    pg.wait("act", a_evac[0], a_evac[0].n)
                    pg.wait("act", a_evac[1], a_evac[1].n)
                    pg.wait("act", a_fin, fin_before)
                    pg.op("act", lambda e: e.activation(out=RS[0:64, :], in_=ACC[64:128, 0, :], func=AF.Ln), a_ln)
                    n_ = pg.op("act", lambda e: e.activation(out=RS[64:128, :], in_=ACC[0:64, 1, :], func=AF.Ln), a_ln)
                    pg.wait("act", a_ln, n_)
                    pg.op("act", lambda e: e.activation(out=RS, in_=RS, func=AF.Exp, scale=-1.0), a_ln)
                    pg.wait("pool", a_ln, a_ln.n)
                    pg.wait("pool", a_zld[zs], a_zld[zs].n)
                    pg.wait("pool", a_gst[zs], a_gst[zs].n)
                    n_ = pg.op("pool", lambda e: e.tensor_tensor(out=RS[0:64, :], in0=ACC[0:64, 0, :], in1=RS[0:64, :], op=ALU.mult), a_fin)
                    n_ = pg.op("pool", lambda e: e.tensor_tensor(out=RS[64:128, :], in0=ACC[64:128, 1, :], in1=RS[64:128, :], op=ALU.mult), a_fin)
                    pg.wait("pool", a_fin, n_)
                    pg.op("pool", lambda e, zs=zs: e.tensor_tensor(out=GT[zs], in0=RS, in1=ZT[zs], op=ALU.mult), a_fin)
                    fin_hist.append(a_fin.n)
                    pg.wait("sp", a_fin, a_fin.n)
                    pg.dma(lambda e, zs=zs, pair=pair, J=J: e.dma_start(out=gs_d[pair, :, J * SB:(J + 1) * SB], in_=GT[zs]), a_gst[zs])
            ac["unit"] += len(units)
            for e_ in a_gst:
                pg.wait("sp", e_, e_.n)

        phase_M(-1)
        phase_O(-1)
        for l in range(nlayers):
            phase_P(l)
            phase_A(l)
            phase_M(l)
            phase_O(l)
        pg.wait("sp", o_xst[0], o_xst[0].n)
        pg.wait("sp", o_xst[1], o_xst[1].n)
        pg.wait("sp", o_xst[2], o_xst[2].n)

        with nc.Block() as block:
            @block.sync
            def _(e):
                pg.replay("sp", e)

            @block.tensor
            def _(e):
                pg.replay("pe", e)

            @block.scalar
            def _(e):
                pg.replay("act", e)

            @block.vector
            def _(e):
                pg.replay("dve", e)

            @block.gpsimd
            def _(e):
                pg.replay("pool", e)
    return nc


def _group_layout(wcat):
    nf = wcat.shape[1]
    assert nf % 512 == 0
    return np.ascontiguousarray(wcat.reshape(8, 128, nf // 512, 512).transpose(2, 1, 0, 3))


def _layer_weights(l, w_in_a, w_in_b):
    if l % 2 == 0:
        w = w_in_a[l // 2]
        cols = []
        for g in range(3):
            cols.append(w[:, g * 3072:g * 3072 + 1024])
            cols.append(w[:, g * 3072 + 1024:g * 3072 + 2048])
        cols.append(w[:, 9216:10240])
        for g in range(3):
            cols.append(w[:, g * 3072 + 2048:g * 3072 + 3072])
        return _group_layout(np.concatenate(cols, axis=1))
    w = w_in_b[l // 2]
    q = w[:, 0:1024]
    k = w[:, 1024:1280].reshape(1024, 4, 64)
    v = w[:, 1280:1536].reshape(1024, 4, 64)
    z = w[:, 1536:2560]
    kk = np.repeat(k, 4, axis=1).reshape(1024, 1024)
    vv = np.repeat(v, 4, axis=1).reshape(1024, 1024)
    return _group_layout(np.concatenate([q, kk, z, vv], axis=1))


def _dtable(groups, W, join, ltype):
    ng = len(groups)
    PW = 512 if ltype == "A" else 384
    out = np.zeros((8, ng, 3, 128, 2, PW), np.float32)
    slopes = 2.0 ** (-8.0 * np.arange(1, 17) / 16.0)
    k = np.arange(128)[:, None]
    j = np.arange(W)[None, :]
    for gi, (d, ns) in enumerate(groups):
        rel = k + ns - j
        valid = (np.abs(rel) <= ns)
        if ltype == "B":
            sl_ = [(256, 384), (128, 256), (0, 128)]
            cross_last, cross_first = (256, 384), (0, 128)
            reps = 1
        elif d < 16:
            sl_ = [(192, 256), (64, 256), (0, 192), (0, 64)]
            cross_last, cross_first = (448, 512), (0, 64)
            reps = 1
        else:
            sl_ = [(192, 256), (64, 192), (0, 64)]
            cross_last, cross_first = (192, 256), (0, 64)
            reps = 2
        for pair in range(8):
            for h in range(2):
                sl = slopes[2 * pair + h]
                base = np.where(valid, np.exp(-sl * d * np.abs(rel).astype(np.float64)), 0.0).astype(np.float32)
                one = np.concatenate([base[:, a:b] for a, b in sl_], axis=1)
                v1 = one.copy()
                v2 = one.copy()
                if not join:
                    v1[:, cross_last[0]:cross_last[1]] = 0.0
                    v2[:, cross_first[0]:cross_first[1]] = 0.0
                for vi, tb in enumerate((one, v1, v2)):
                    out[pair, gi, vi, :, h, :] = np.concatenate([tb] * reps, axis=1)
    return out


_CACHE = {}


def _get_program(nlayers):
    if nlayers not in _CACHE:
        _CACHE[nlayers] = build_program(nlayers)
    return _CACHE[nlayers]


def make_in_maps(x_prompt, x_sample, c_prompt, c_sample, w_mod, b_mod, ln_g, ln_b,
                 w_in_a, w_out_a, w_in_b, w_out_b, sink_b, nlayers=DEPTH):
    f = lambda a: np.ascontiguousarray(np.asarray(a, dtype=np.float32))
    x_prompt, x_sample, c_prompt, c_sample = f(x_prompt), f(x_sample), f(c_prompt), f(c_sample)
    w_mod, b_mod, ln_g, ln_b = f(w_mod), f(b_mod), f(ln_g), f(ln_b)
    w_in_a, w_out_a, w_in_b, w_out_b, sink_b = f(w_in_a), f(w_out_a), f(w_in_b), f(w_out_b), f(sink_b)
    shared = {}
    for l in range(nlayers):
        shared[f"w{l}"] = _layer_weights(l, w_in_a, w_in_b)
    wo = np.stack([(w_out_a if l % 2 == 0 else w_out_b)[l // 2] for l in range(DEPTH)])
    shared["wo"] = np.ascontiguousarray(wo.reshape(DEPTH, 8, 128, D).transpose(0, 2, 1, 3))
    shared["wm"] = np.ascontiguousarray(w_mod.reshape(DEPTH, 8, 128, 6, 512).transpose(0, 3, 2, 1, 4))
    shared["bmod"] = b_mod
    shared["lng"] = ln_g
    shared["lnb"] = ln_b
    shared["sink"] = sink_b
    shared["ident"] = np.eye(128, dtype=np.float32)
    dA = {j: _dtable(LT["A"]["groups"], 256, j, "A") for j in (False, True)}
    dB = {j: _dtable(LT["B"]["groups"], 384, j, "B") for j in (False, True)}
    in_maps = []
    for i in range(NCORES):
        m = dict(shared)
        if i < 4:
            m["x"] = x_prompt[i]
            m["cvec"] = np.ascontiguousarray(np.stack([c_prompt[i], c_prompt[i]]))
            m["dA"], m["dB"] = dA[True], dB[True]
        else:
            j = 2 * (i - 4)
            m["x"] = np.ascontiguousarray(x_sample[j:j + 2].reshape(TOK, D))
            m["cvec"] = np.ascontiguousarray(c_sample[j:j + 2])
            m["dA"], m["dB"] = dA[False], dB[False]
        in_maps.append(m)
    return in_maps


def kernel(x_prompt, x_sample, c_prompt, c_sample, w_mod, b_mod, ln_g, ln_b,
           w_in_a, w_out_a, w_in_b, w_out_b, sink_b):
    in_maps = make_in_maps(x_prompt, x_sample, c_prompt, c_sample, w_mod, b_mod, ln_g, ln_b,
                           w_in_a, w_out_a, w_in_b, w_out_b, sink_b)
    nc = _get_program(DEPTH)
    res = run_bass_kernel_spmd(nc, in_maps, core_ids=list(range(NCORES)))
    ys = [np.asarray(r["y"], dtype=np.float32) for r in res.results]
    y_prompt = np.stack(ys[0:4]).reshape(4, TOK, D)
    y_sample = np.concatenate([y.reshape(2, TOK // 2, D) for y in ys[4:8]], axis=0)
    return (y_prompt, y_sample)
```

```python
import contextlib
import numpy as np
import concourse.bass as bass
import concourse.mybir as mybir
from concourse.bass_utils import run_bass_kernel_spmd

F32 = mybir.dt.float32
BF16 = mybir.dt.bfloat16
ALU = mybir.AluOpType
AF = mybir.ActivationFunctionType

NCORES = 8
D = 1024
TOK = 8192
SB = 2048
NSB = TOK // SB
DEPTH = 4
ALPHA = (2.0 * DEPTH) ** 0.25
LN_EPS = 1e-5
A_DILS = (1, 4, 16)

LT = {
    "A": dict(groups=[(1, 64), (4, 64), (16, 64)], nfm=7, ngrp_fm=14, ngrp_v=6, W=256),
    "B": dict(groups=[(1, 128)], nfm=3, ngrp_fm=6, ngrp_v=2, W=384),
}
def fm_types(t):
    return 7 if t == "A" else 3


def layer_type(l):
    return "A" if l % 2 == 0 else "B"


class Ev:
    def __init__(self, nc, stack, name):
        self.h = stack.enter_context(nc.semaphore(name))
        self.n = 0


class Prog:
    def __init__(self, nc, stack):
        self.nc = nc
        self.stack = stack
        self.q = {k: [] for k in ("pe", "act", "dve", "pool", "sp")}
        self.nev = 0

    def ev(self, name):
        self.nev += 1
        return Ev(self.nc, self.stack, f"{name}_{self.nev}")

    def op(self, eng, fn, ev=None, amt=1):
        self.q[eng].append((fn, ev, amt))
        if ev is not None:
            ev.n += amt
            return ev.n
        return None

    def dma(self, fn, ev):
        return self.op("sp", fn, ev, 16)

    def wait(self, eng, ev, val):
        if val <= 0:
            return
        self.q[eng].append(("wait", ev, val))

    def replay(self, eng_name, eng):
        last_wait = {}
        for item in self.q[eng_name]:
            if item[0] == "wait":
                _, ev, val = item
                if last_wait.get(id(ev), 0) >= val:
                    continue
                last_wait[id(ev)] = val
                eng.wait_ge(ev.h, val)
            else:
                fn, ev, amt = item
                ins = fn(eng)
                if ev is not None:
                    ins.then_inc(ev.h, amt)


def build_program(nlayers=DEPTH):
    nc = bass.Bass("TRN2", target_bir_lowering=False)
    dt = nc.dram_tensor
    x_in = dt("x", [TOK, D], F32, kind="ExternalInput").ap()
    cvec = dt("cvec", [2, D], F32, kind="ExternalInput").ap()
    ident_d = dt("ident", [128, 128], F32, kind="ExternalInput").ap()
    w_d = []
    for l in range(nlayers):
        t = LT[layer_type(l)]
        w_d.append(dt(f"w{l}", [t["ngrp_fm"] + t["ngrp_v"], 128, 8, 512], F32, kind="ExternalInput").ap())
    wo_d = dt("wo", [DEPTH, 128, 8, D], F32, kind="ExternalInput").ap()
    wm_d = dt("wm", [DEPTH, 6, 128, 8, 512], F32, kind="ExternalInput").ap()
    bmod_d = dt("bmod", [DEPTH, 3 * D], F32, kind="ExternalInput").ap()
    lng_d = dt("lng", [DEPTH, D], F32, kind="ExternalInput").ap()
    lnb_d = dt("lnb", [DEPTH, D], F32, kind="ExternalInput").ap()
    sink_d = dt("sink", [2, 16], F32, kind="ExternalInput").ap()
    dA_d = dt("dA", [8, 3, 3, 128, 2, 512], F32, kind="ExternalInput").ap()
    dB_d = dt("dB", [8, 1, 3, 128, 2, 384], F32, kind="ExternalInput").ap()
    y_out = dt("y", [TOK, D], F32, kind="ExternalOutput").ap()
    xs = [dt(f"xs{i}", [TOK, D], F32).ap() for i in range(2)]
    uT_d = dt("uT", [NSB, 128, 8, SB], BF16).ap()
    fm_d = dt("fm", [7, 8, 128, TOK], BF16).ap()
    Vs_d = dt("Vs", [3, 8, 128, 64, 256], BF16).ap()
    gs_d = dt("gs", [8, 128, TOK], BF16).ap()

    stack = contextlib.ExitStack()
    with stack:
        sb = lambda name, shape, dtp: stack.enter_context(nc.sbuf_tensor(name, shape, dtp))
        BFA = sb("bfa", [128, 44032], BF16)
        FA = sb("fa", [128, 12288], F32)
        MOD = sb("modt", [128, 8, D], F32)
        screp = sb("screp", [128, 2, 8, 128], F32)
        ident = sb("identt", [128, 128], F32)
        small = sb("smallt", [128, 64], F32)
        esink = sb("esink", [128, 32], F32)
        csb = sb("csb", [128, 16], F32)
        biasb = sb("biasb", [128, 2, 512], F32)
        PS = stack.enter_context(nc.psum_tensor("ps", [128, 8, 512], F32))
        pg = Prog(nc, stack)

        GATE = [MOD[:, 0, :], MOD[:, 1, :]]
        SC1 = [MOD[:, 2, :], MOD[:, 3, :]]
        SH = [MOD[:, 4, :], MOD[:, 5, :]]
        LNG = MOD[:, 6, :]
        LNB = MOD[:, 7, :]

        misc_ld = pg.ev("miscld")

        pg.dma(lambda e: e.dma_start(out=ident[:], in_=ident_d[:, :]), misc_ld)
        pg.dma(lambda e: e.dma_start(out=csb[:, :].rearrange("p (c k) -> p c k", c=2),
                                     in_=cvec.rearrange("c (k p) -> p c k", p=128),
                                     allow_slow_non_contiguous=True), misc_ld)
        pg.dma(lambda e: e.dma_start(out=esink[:, :], in_=sink_d.rearrange("a h -> (a h)").partition_broadcast(128)), misc_ld)
        pro_ev = pg.ev("pro")
        pg.wait("act", misc_ld, misc_ld.n)
        pg.op("act", lambda e: e.activation(out=csb[:, :], in_=csb[:, :], func=AF.Silu), pro_ev)
        pg.op("act", lambda e: e.activation(out=esink[:, :], in_=esink[:, :], func=AF.Exp), pro_ev)
        pg.op("pool", lambda e: e.memset(screp[:], 1.0), pro_ev)
        pg.wait("dve", pro_ev, pro_ev.n)
        pro2 = pg.ev("pro2")
        for c in range(2):
            for k in range(8):
                pg.op("dve", lambda e, c=c, k=k: e.tensor_scalar(
                    out=screp[:, c, k, :], in0=screp[:, c, k, :], scalar1=csb[:, c * 8 + k:c * 8 + k + 1],
                    scalar2=None, op0=ALU.mult), pro2)

        wst_views = [FA[:, 0:4096].rearrange("p (k c) -> p k c", k=8), FA[:, 4096:8192].rearrange("p (k c) -> p k c", k=8)]
        m_ld = pg.ev("mld")
        m_pe = pg.ev("mpe")
        m_dve = pg.ev("mdve")

        def phase_M(l):
            jobs = []
            if l >= 0:
                jobs += [(l, 4, GATE, 0, 0.0), (l, 5, GATE, 1, 0.0)]
            if l + 1 < nlayers:
                jobs += [(l + 1, 0, SH, 0, 0.0), (l + 1, 1, SH, 1, 0.0), (l + 1, 2, SC1, 0, 1.0), (l + 1, 3, SC1, 1, 1.0)]
            if l >= 0:
                pg.wait("sp", m_dve, m_dve.n)
                pg.dma(lambda e: e.dma_start(out=LNG, in_=lng_d[l, :].partition_broadcast(128)), m_ld)
                pg.dma(lambda e: e.dma_start(out=LNB, in_=lnb_d[l, :].partition_broadcast(128)), m_ld)
            for ji, (ll, cg, tgt, half, add1) in enumerate(jobs):
                slot = ji % 2
                pg.wait("sp", m_pe, m_pe.n)
                pg.wait("sp", m_dve, m_dve.n)
                pg.dma(lambda e, ll=ll, cg=cg, slot=slot: e.dma_start(out=wst_views[slot], in_=wm_d[ll, cg]), m_ld)
                pg.dma(lambda e, ll=ll, cg=cg, slot=slot: e.dma_start(
                    out=biasb[:, slot, :], in_=bmod_d[ll, cg * 512:(cg + 1) * 512].partition_broadcast(128)), m_ld)
                pg.wait("pe", m_ld, m_ld.n)
                pg.wait("pe", pro2, pro2.n)
                pg.wait("pe", m_dve, m_dve.n)
                for c in range(2):
                    for k in range(8):
                        pg.op("pe", lambda e, c=c, k=k, slot=slot: e.matmul(
                            PS[:, c, :], lhsT=screp[:, c, k, :], rhs=wst_views[slot][:, k, :],
                            start=(k == 0), stop=(k == 7)), m_pe if (c == 1 and k == 7) else None)
                pg.wait("dve", m_pe, m_pe.n)
                for c in range(2):
                    pg.op("dve", lambda e, c=c, tgt=tgt, half=half, add1=add1, slot=slot: e.scalar_tensor_tensor(
                        out=tgt[c][:, half * 512:(half + 1) * 512], in0=PS[:, c, :], scalar=add1,
                        in1=biasb[:, slot, :], op0=ALU.add, op1=ALU.add), m_dve)
            return

        gT = [BFA[:, i * 4096:(i + 1) * 4096].rearrange("p (k t) -> p k t", k=8) for i in range(2)]
        wo_bf = BFA[:, 8192:16384].rearrange("p (k c) -> p k c", k=8)
        uTst = [BFA[:, 16384 + i * 4096:16384 + (i + 1) * 4096].rearrange("p (k t) -> p k t", k=8) for i in range(2)]
        xt = [FA[:, i * 1024:(i + 1) * 1024] for i in range(4)]
        ht = [FA[:, 4096 + i * 1024:4096 + (i + 1) * 1024] for i in range(2)]
        ut = [FA[:, 6144 + i * 1024:6144 + (i + 1) * 1024] for i in range(2)]
        junk = BFA[:, 24576:25600]
        wo_st = FA[:, 8192:12288].rearrange("p (k c) -> p k c", k=8)

        o_gld = [pg.ev(f"ogld{i}") for i in range(2)]
        o_xld = [pg.ev(f"oxld{i}") for i in range(4)]
        o_wld = pg.ev("owld")
        o_wcast = pg.ev("owcast")
        o_y = [pg.ev(f"oy{i}") for i in range(2)]
        o_h = [pg.ev(f"oh{i}") for i in range(2)]
        o_h2 = [pg.ev(f"oh2{i}") for i in range(2)]
        o_sq = [pg.ev(f"osq{i}") for i in range(2)]
        o_st = [pg.ev(f"ost{i}") for i in range(2)]
        o_xn = [pg.ev(f"oxn{i}") for i in range(2)]
        o_xo = [pg.ev(f"oxo{i}") for i in range(4)]
        o_xst = [pg.ev(f"oxst{i}") for i in range(4)]
        o_u1 = [pg.ev(f"ou1{i}") for i in range(2)]
        o_u2 = [pg.ev(f"ou2{i}") for i in range(2)]
        o_tp = [pg.ev(f"otp{i}") for i in range(2)]
        o_te = [pg.ev(f"ote{i}") for i in range(2)]
        o_ust = [pg.ev(f"oust{i}") for i in range(2)]
        cnt = dict(sub=0, tg=0)
        xt_free = [[], [], [], []]
        ht_free = [None, None]

        def phase_O(l):
            full = l >= 0
            last = (l == nlayers - 1)
            src = x_in if l <= 0 else xs[(l - 1) % 2]
            dst = y_out if last else xs[l % 2]
            mod_ready_dve = m_dve.n
            mod_ready_ld = m_ld.n
            pg.wait("sp", m_pe, m_pe.n)
            pg.wait("sp", m_dve, m_dve.n)
            if full:
                for hlf in range(2):
                    pg.wait("sp", o_wcast, o_wcast.n)
                    pg.dma(lambda e, hlf=hlf: e.dma_start(out=wo_st, in_=wo_d[l, :, :, hlf * 512:(hlf + 1) * 512]), o_wld)
                    pg.wait("pool", o_wld, o_wld.n)
                    pg.op("pool", lambda e, hlf=hlf: e.tensor_copy(out=wo_bf[:, :, hlf * 512:(hlf + 1) * 512], in_=wo_st), o_wcast)
            base_sub = cnt["sub"]
            base_tg = cnt["tg"]
            NSUB = TOK // 128

            def emit_xload(i):
                si = base_sub + i
                xs_ = si % 4
                for (ev_, n_) in xt_free[xs_]:
                    pg.wait("sp", ev_, n_)
                xt_free[xs_] = []
                row0 = i * 128
                pg.dma(lambda e: e.dma_start(out=xt[xs_], in_=src[row0:row0 + 128, :]), o_xld[xs_])
                return o_xld[xs_].n

            def emit_gload(tg):
                gslot = (base_tg + tg) % 2
                pg.wait("sp", o_y[0], o_y[0].n)
                pg.wait("sp", o_y[1], o_y[1].n)
                pg.dma(lambda e: e.dma_start(
                    out=gT[gslot], in_=gs_d[:, :, tg * 512:(tg + 1) * 512].rearrange("k p t -> p k t")), o_gld[gslot])
                return o_gld[gslot].n

            pend_u = []
            xld_n = {0: emit_xload(0)}
            gld_n = {}
            if full:
                gld_n[0] = emit_gload(0)
            for tg in range(TOK // 512):
                c = tg // 8
                gi = base_tg + tg
                gslot = gi % 2
                uslot = gi % 2
                if full and tg + 1 < TOK // 512:
                    gld_n[tg + 1] = emit_gload(tg + 1)
                for s in range(4):
                    i = tg * 4 + s
                    si = base_sub + i
                    xs_ = si % 4
                    p2 = si % 2
                    row0 = i * 128
                    if i + 1 < NSUB:
                        xld_n[i + 1] = emit_xload(i + 1)
                    b = 8 * p2
                    if full:
                        pg.wait("pe", o_gld[gslot], gld_n[tg])
                        pg.wait("pe", o_wcast, o_wcast.n)
                        pg.wait("pe", o_h[p2], o_h[p2].n)
                        for hlf in range(2):
                            for k in range(8):
                                pg.op("pe", lambda e, hlf=hlf, k=k, p2=p2, gslot=gslot, s=s: e.matmul(
                                    PS[:, 2 * p2 + hlf, :], lhsT=gT[gslot][:, k, s * 128:(s + 1) * 128],
                                    rhs=wo_bf[:, k, hlf * 512:(hlf + 1) * 512], start=(k == 0), stop=(k == 7)),
                                    o_y[p2] if (hlf == 1 and k == 7) else None)
                        Y = PS[:, 2 * p2:2 * p2 + 2, :].rearrange("p a b -> p (a b)")
                        pg.wait("dve", o_y[p2], o_y[p2].n)
                        pg.wait("dve", m_dve, mod_ready_dve)
                        pg.op("dve", lambda e, b=b: e.memset(small[:, b:b + 2], 0.0))
                        pg.op("dve", lambda e, p2=p2, Y=Y, c=c: e.tensor_tensor(out=ht[p2], in0=Y, in1=GATE[c], op=ALU.mult), o_h[p2])
                        pg.wait("dve", o_xld[xs_], xld_n[i])
                        pg.wait("dve", o_h[p2], o_h[p2].n)
                        pg.op("dve", lambda e, p2=p2, xs_=xs_, b=b: e.scalar_tensor_tensor(
                            out=ht[p2], in0=xt[xs_], scalar=ALPHA, in1=ht[p2], op0=ALU.mult, op1=ALU.add,
                            accum_out=small[:, b:b + 1]), o_h2[p2])
                        pg.wait("act", o_h2[p2], o_h2[p2].n)
                        pg.op("act", lambda e, p2=p2, b=b: e.activation(out=junk, in_=ht[p2], func=AF.Square,
                                                                        accum_out=small[:, b + 1:b + 2]), o_sq[p2])
                        pg.wait("dve", o_sq[p2], o_sq[p2].n)
                        ops = [
                            lambda e, b=b: e.tensor_scalar(out=small[:, b + 2:b + 3], in0=small[:, b:b + 1], scalar1=-1.0 / D, scalar2=None, op0=ALU.mult),
                            lambda e, b=b: e.tensor_tensor(out=small[:, b + 3:b + 4], in0=small[:, b + 2:b + 3], in1=small[:, b + 2:b + 3], op=ALU.mult),
                            lambda e, b=b: e.scalar_tensor_tensor(out=small[:, b + 4:b + 5], in0=small[:, b + 1:b + 2], scalar=1.0 / D, in1=small[:, b + 3:b + 4], op0=ALU.mult, op1=ALU.subtract),
                            lambda e, b=b: e.tensor_scalar(out=small[:, b + 4:b + 5], in0=small[:, b + 4:b + 5], scalar1=LN_EPS, scalar2=None, op0=ALU.add),
                        ]
                        for f in ops:
                            n = pg.op("dve", f, o_st[p2])
                            pg.wait("dve", o_st[p2], n)
                        pg.wait("act", o_st[p2], o_st[p2].n)
                        n = pg.op("act", lambda e, b=b: e.activation(out=small[:, b + 5:b + 6], in_=small[:, b + 4:b + 5], func=AF.Sqrt), o_sq[p2])
                        pg.wait("dve", o_sq[p2], n)
                        n = pg.op("dve", lambda e, b=b: e.reciprocal(out=small[:, b + 5:b + 6], in_=small[:, b + 5:b + 6]), o_st[p2])
                        pg.wait("dve", o_st[p2], n)
                        pg.wait("dve", m_ld, mod_ready_ld)
                        n = pg.op("dve", lambda e, p2=p2, b=b: e.scalar_tensor_tensor(
                            out=ht[p2], in0=ht[p2], scalar=small[:, b + 2:b + 3], in1=LNG, op0=ALU.add, op1=ALU.mult), o_xn[p2])
                        pg.wait("dve", o_xn[p2], n)
                        pg.op("dve", lambda e, p2=p2, xs_=xs_, b=b: e.scalar_tensor_tensor(
                            out=xt[xs_], in0=ht[p2], scalar=small[:, b + 5:b + 6], in1=LNB, op0=ALU.mult, op1=ALU.add), o_xo[xs_])
                        pg.wait("sp", o_xo[xs_], o_xo[xs_].n)
                        pg.dma(lambda e, xs_=xs_, row0=row0: e.dma_start(out=dst[row0:row0 + 128, :], in_=xt[xs_]), o_xst[xs_])
                        xt_free[xs_].append((o_xst[xs_], o_xst[xs_].n))
                    if not last:
                        xo_n = o_xo[xs_].n

                        def ublock(i=i, tg=tg, s=s, xs_=xs_, p2=p2, c=c, uslot=uslot, xo_n=xo_n):
                            if full:
                                pg.wait("pool", o_xo[xs_], xo_n)
                            else:
                                pg.wait("pool", o_xld[xs_], xld_n[i])
                            pg.wait("pool", m_dve, mod_ready_dve)
                            pg.wait("pool", o_tp[p2], o_tp[p2].n)
                            n = pg.op("pool", lambda e: e.tensor_tensor(out=ut[p2], in0=xt[xs_], in1=SC1[c], op=ALU.mult), o_u1[p2])
                            xt_free[xs_].append((o_u1[p2], o_u1[p2].n))
                            pg.wait("pool", o_u1[p2], n)
                            pg.op("pool", lambda e: e.tensor_tensor(out=ut[p2], in0=ut[p2], in1=SH[c], op=ALU.add), o_u2[p2])
                            pg.wait("pe", o_u2[p2], o_u2[p2].n)
                            pg.wait("pe", misc_ld, misc_ld.n)
                            pg.wait("pe", o_te[p2], o_te[p2].n)
                            for k in range(8):
                                pg.op("pe", lambda e, k=k: e.transpose(
                                    out=PS[:, 4 + 2 * p2 + k // 4, (k % 4) * 128:(k % 4 + 1) * 128],
                                    in_=ut[p2][:, k * 128:(k + 1) * 128], identity=ident[:]),
                                    o_tp[p2] if k == 7 else None)
                            pg.wait("act", o_tp[p2], o_tp[p2].n)
                            if s == 0:
                                pg.wait("act", o_ust[uslot], o_ust[uslot].n)
                            pg.op("act", lambda e: e.activation(
                                out=uTst[uslot][:, :, s * 128:(s + 1) * 128],
                                in_=PS[:, 4 + 2 * p2:6 + 2 * p2, :].rearrange("p a (b t) -> p (a b) t", t=128), func=AF.Identity), o_te[p2])
                            if s == 3:
                                pg.wait("sp", o_te[0], o_te[0].n)
                                pg.wait("sp", o_te[1], o_te[1].n)
                                J = tg // 4
                                pg.dma(lambda e: e.dma_start(
                                    out=uT_d[J, :, :, (tg % 4) * 512:(tg % 4 + 1) * 512], in_=uTst[uslot]), o_ust[uslot])
                        pend_u.append(ublock)
                        while len(pend_u) > 2:
                            pend_u.pop(0)()
            while pend_u:
                pend_u.pop(0)()
            cnt["sub"] += NSUB
            cnt["tg"] += TOK // 512
            for evs in (o_xst, o_ust):
                for e_ in evs:
                    pg.wait("sp", e_, e_.n)

        uT_sb = BFA[:, 0:16384].rearrange("p (k t) -> p k t", k=8)
        wbf = [BFA[:, 16384 + i * 4096:16384 + (i + 1) * 4096].rearrange("p (k c) -> p k c", k=8) for i in range(2)]
        fstage = [BFA[:, 24576 + i * 2048:24576 + (i + 1) * 2048] for i in range(2)]
        vstage = [BFA[:, 28672 + i * 4096:28672 + (i + 1) * 4096].rearrange("p (t a c) -> p t a c", t=4, a=4) for i in range(2)]
        wst3 = [FA[:, i * 4096:(i + 1) * 4096].rearrange("p (k c) -> p k c", k=8) for i in range(3)]
        p_uld = pg.ev("puld")
        p_wld = [pg.ev(f"pwld{i}") for i in range(3)]
        p_cast = pg.ev("pcast")
        p_mm = [pg.ev(f"pmm{i}") for i in range(2)]
        p_evf = [pg.ev(f"pevf{i}") for i in range(2)]
        p_evv = [pg.ev(f"pevv{i}") for i in range(2)]
        p_fst = [pg.ev(f"pfst{i}") for i in range(2)]
        p_vst = [pg.ev(f"pvst{i}") for i in range(2)]
        p_ones = pg.ev("pones")
        pc = dict(w=0, set=0, f=0, v=0, ev=0)
        set_free = [None, None]

        def phase_P(l):
            ltype = layer_type(l)
            t = LT[ltype]
            groups = t["groups"]
            W = w_d[l]
            ngf, ngv = t["ngrp_fm"], t["ngrp_v"]
            nfm = t["nfm"]
            pg.wait("pool", p_vst[0], p_vst[0].n)
            pg.wait("pool", p_vst[1], p_vst[1].n)
            for i in range(2):
                pg.op("pool", lambda e, i=i: e.memset(vstage[i][:, :, :, 64:192], 1.0), p_ones)
            seq = [(J, wg) for J in range(NSB) for wg in range(ngf + ngv)]
            wbase = pc["w"]
            ldn = {}

            def emit_wload(i):
                w = wbase + i
                s3 = w % 3
                pg.wait("sp", p_cast, w - 2)
                wg_ = seq[i][1]
                pg.dma(lambda e: e.dma_start(out=wst3[s3], in_=W[wg_]), p_wld[s3])
                ldn[i] = p_wld[s3].n

            emit_wload(0)
            emit_wload(1)
            for i_, (J, wg) in enumerate(seq):
                if wg == 0:
                    pg.wait("sp", p_mm[0], p_mm[0].n)
                    pg.wait("sp", p_mm[1], p_mm[1].n)
                    pg.dma(lambda e, J=J: e.dma_start(out=uT_sb, in_=uT_d[J]), p_uld)
                if i_ + 2 < len(seq):
                    emit_wload(i_ + 2)
                if True:
                    wi = wbase + i_
                    s3 = wi % 3
                    ws = wi % 2
                    pg.wait("pool", p_wld[s3], ldn[i_])
                    pg.wait("pool", p_mm[0], pc.get(("mm0", ws), 0))
                    pg.wait("pool", p_mm[1], pc.get(("mm1", ws), 0))
                    pg.op("pool", lambda e, ws=ws, s3=s3: e.tensor_copy(out=wbf[ws], in_=wst3[s3]), p_cast)
                    assert p_cast.n == wi + 1
                    pg.wait("pe", p_cast, wi + 1)
                    pg.wait("pe", p_uld, p_uld.n)
                    if wg < ngf:
                        for b in range(4):
                            bi = wg * 4 + b
                            ty, pair = bi // 8, bi % 8
                            if ltype == "A":
                                is_z = (ty == 6)
                                d = 1 if is_z else groups[ty // 2][0]
                            else:
                                is_z = (ty == 2)
                                d = 1
                            si = pc["set"]
                            pc["set"] += 1
                            S_ = si % 2
                            if set_free[S_] is not None:
                                pg.wait("pe", set_free[S_][0], set_free[S_][1])
                            for k in range(8):
                                for s in range(4):
                                    pg.op("pe", lambda e, S_=S_, k=k, s=s, ws=ws, b=b: e.matmul(
                                        PS[:, 4 * S_ + s, :], lhsT=wbf[ws][:, k, b * 128:(b + 1) * 128],
                                        rhs=uT_sb[:, k, s * 512:(s + 1) * 512], start=(k == 0), stop=(k == 7)),
                                        p_mm[S_] if (k == 7 and s == 3) else None)
                            fi = pc["f"]
                            pc["f"] += 1
                            fs = fi % 2
                            eng = "act" if (is_z or fi % 2 == 0) else "dve"
                            pg.wait(eng, p_mm[S_], p_mm[S_].n)
                            pg.wait(eng, p_fst[fs], p_fst[fs].n)
                            src_ap = PS[:, 4 * S_:4 * S_ + 4, :].rearrange("p a b -> p (a b)")
                            if d > 1:
                                src_ap = src_ap.rearrange("p (u r) -> p r u", r=d)
                                dst_ap = fstage[fs].rearrange("p (r u) -> p r u", r=d)
                            else:
                                dst_ap = fstage[fs]
                            if eng == "act":
                                fn = AF.Silu if is_z else AF.Identity
                                pg.op("act", lambda e, dst_ap=dst_ap, src_ap=src_ap, fn=fn: e.activation(out=dst_ap, in_=src_ap, func=fn), p_evf[fs])
                            else:
                                pg.op("dve", lambda e, dst_ap=dst_ap, src_ap=src_ap: e.tensor_copy(out=dst_ap, in_=src_ap), p_evf[fs])
                            set_free[S_] = (p_evf[fs], p_evf[fs].n)
                            n = SB // d
                            pg.wait("sp", p_evf[fs], p_evf[fs].n)
                            pg.dma(lambda e, ty=ty, pair=pair, fs=fs, d=d, n=n, J=J: e.dma_start(
                                out=fm_d[ty, pair].rearrange("p (r u) -> p r u", r=d)[:, :, J * n:(J + 1) * n],
                                in_=fstage[fs].rearrange("p (r u) -> p r u", r=d)), p_fst[fs])
                    else:
                        vg = wg - ngf
                        g, ph = vg // 2, vg % 2
                        d = groups[g][0]
                        for sidx in range(4):
                            if d == 1:
                                tiles = [(0, 4 * sidx + i) for i in range(4)]
                            elif d == 4:
                                tiles = [(sidx, i) for i in range(4)]
                            else:
                                tiles = [(4 * sidx + i, 0) for i in range(4)]
                            si = pc["set"]
                            pc["set"] += 1
                            S_ = si % 2
                            if set_free[S_] is not None:
                                pg.wait("pe", set_free[S_][0], set_free[S_][1])
                            for ti, (r, mp) in enumerate(tiles):
                                for k in range(8):
                                    if d > 1:
                                        lh = uT_sb[:, k, :].rearrange("p (u r) -> p r u", r=d)[:, r, mp * 128:(mp + 1) * 128]
                                    else:
                                        lh = uT_sb[:, k, mp * 128:(mp + 1) * 128]
                                    pg.op("pe", lambda e, S_=S_, ti=ti, k=k, ws=ws, lh=lh: e.matmul(
                                        PS[:, 4 * S_ + ti, :], lhsT=lh, rhs=wbf[ws][:, k, :], start=(k == 0), stop=(k == 7)),
                                        p_mm[S_] if (k == 7 and ti == 3) else None)
                            vi = pc["v"]
                            pc["v"] += 1
                            vs = vi % 2
                            veng = "act" if vi % 2 == 0 else "dve"
                            pg.wait(veng, p_mm[S_], p_mm[S_].n)
                            pg.wait(veng, p_vst[vs], p_vst[vs].n)
                            pg.wait(veng, p_ones, p_ones.n)
                            srcv = PS[:, 4 * S_:4 * S_ + 4, :].rearrange("p t (a j e) -> p t a j e", a=4, j=2)
                            for j_, c0 in ((0, 0), (1, 192)):
                                if veng == "act":
                                    pg.op("act", lambda e, vs=vs, srcv=srcv, j_=j_, c0=c0: e.activation(
                                        out=vstage[vs][:, :, :, c0:c0 + 64], in_=srcv[:, :, :, j_, :], func=AF.Identity), p_evv[vs])
                                else:
                                    pg.op("dve", lambda e, vs=vs, srcv=srcv, j_=j_, c0=c0: e.tensor_copy(
                                        out=vstage[vs][:, :, :, c0:c0 + 64], in_=srcv[:, :, :, j_, :]), p_evv[vs])
                            set_free[S_] = (p_evv[vs], p_evv[vs].n)
                            pg.wait("sp", p_evv[vs], p_evv[vs].n)
                            ntr = 64 // d
                            for ti, (r, mp) in enumerate(tiles):
                                kt = r * ntr + J * (16 // d) + mp
                                pg.dma(lambda e, g=g, ph=ph, kt=kt, vs=vs, ti=ti: e.dma_start(
                                    out=Vs_d[g, 4 * ph:4 * ph + 4, :, kt, :].rearrange("a k c -> k a c"),
                                    in_=vstage[vs][:, ti, :, :]), p_vst[vs])
                    pc[("mm0", ws)] = p_mm[0].n
                    pc[("mm1", ws)] = p_mm[1].n
            pc["w"] += len(seq)
            for evs in (p_fst, p_vst):
                for e_ in evs:
                    pg.wait("sp", e_, e_.n)

        KT = [BFA[:, i * 3072:(i + 1) * 3072] for i in range(2)]
        QT = [BFA[:, 6144 + i * 2048:6144 + (i + 1) * 2048] for i in range(2)]
        VT = [BFA[:, 10240 + i * 6144:10240 + (i + 1) * 6144].rearrange("p (t c) -> p t c", c=256) for i in range(2)]
        ZT = [BFA[:, 22528 + i * 2048:22528 + (i + 1) * 2048] for i in range(2)]
        GT = [BFA[:, 26624 + i * 2048:26624 + (i + 1) * 2048] for i in range(2)]
        EP = [BFA[:, 30720 + i * 1024:30720 + (i + 1) * 1024].rearrange("p (h n) -> p h n", h=2) for i in range(4)]
        DTB = BFA[:, 34816:34816 + 9216]
        ACC2 = [FA[:, i * 4096:(i + 1) * 4096].rearrange("p (h t) -> p h t", h=2) for i in range(2)]
        RS = FA[:, 8192:10240]
        DTS = [FA[:, 10240 + i * 1024:10240 + (i + 1) * 1024] for i in range(2)]

        a_ld = [pg.ev(f"ald{i}") for i in range(2)]
        a_zld = [pg.ev(f"azld{i}") for i in range(2)]
        a_dld = [pg.ev(f"adld{i}") for i in range(2)]
        a_dcast = pg.ev("adcast")
        a_ln = pg.ev("aln")
        a_s = [pg.ev(f"as{i}") for i in range(2)]
        a_e = [pg.ev(f"ae{i}") for i in range(4)]
        a_p = [pg.ev(f"ap{i}") for i in range(4)]
        a_pv = [pg.ev(f"apv{i}") for i in range(4)]
        a_evac = [pg.ev(f"aevac{i}") for i in range(2)]
        a_fin = pg.ev("afin")
        a_gst = [pg.ev(f"agst{i}") for i in range(2)]
        ac = dict(unit=0, batch=0, pj=0, dst=0)
        unit_end = {}
        fin_hist = []
        exp_n = {}

        def phase_A(l):
            ltype = layer_type(l)
            t = LT[ltype]
            groups = t["groups"]
            PW = 512 if ltype == "A" else 384
            dtab = dA_d if ltype == "A" else dB_d
            ng = len(groups)
            zty = fm_types(ltype) - 1
            KEEP = 2
            units = []
            for pair in range(8):
                for J in range(NSB):
                    for g, (d, ns) in enumerate(groups):
                        parts = 2 if d == 16 else 1
                        for part in range(parts):
                            nr = d // parts
                            units.append(dict(pair=pair, J=J, g=g, d=d, ns=ns, r0=part * nr, nr=nr,
                                              first=(g == 0 and part == 0), lastu=(g == ng - 1 and part == parts - 1)))
            ubase = ac["unit"]

            def emit_loads(ui):
                u = units[ui]
                ug = ubase + ui
                slot = ug % 2
                d, ns, J, pair, g = u["d"], u["ns"], u["J"], u["pair"], u["g"]
                Lr = TOK // d
                n = SB // d
                P0 = J * n
                KW = n + 256
                klo, khi = max(0, P0 - 128), min(Lr, P0 + n + 128)
                ntw = n // 128 + 2
                mlo, mhi = max(0, P0 // 128 - 1), min(Lr // 128 - 1, (P0 + n) // 128)
                if ltype == "A":
                    kty, qty = 2 * g + 1, 2 * g
                else:
                    kty, qty = 1, 0
                r0, nr = u["r0"], u["nr"]
                if (ug - 2) in unit_end:
                    pg.wait("sp", a_evac[0], unit_end[ug - 2][0])
                    pg.wait("sp", a_evac[1], unit_end[ug - 2][1])
                else:
                    assert ug - 2 < ubase or ug < 2, (ug, ubase)
                pg.dma(lambda e: e.dma_start(
                    out=KT[slot][:, 0:nr * KW].rearrange("p (r w) -> p r w", r=nr)[:, :, klo - (P0 - 128):khi - (P0 - 128)],
                    in_=fm_d[kty, pair].rearrange("p (r u) -> p r u", r=d)[:, r0:r0 + nr, klo:khi]), a_ld[slot])
                pg.dma(lambda e: e.dma_start(
                    out=QT[slot][:, 0:nr * n].rearrange("p (r w) -> p r w", r=nr),
                    in_=fm_d[qty, pair].rearrange("p (r u) -> p r u", r=d)[:, r0:r0 + nr, P0:P0 + n]), a_ld[slot])
                t0 = mlo - (P0 // 128 - 1)
                t1 = mhi + 1 - (P0 // 128 - 1)
                pg.dma(lambda e: e.dma_start(
                    out=VT[slot][:, 0:nr * ntw, :].rearrange("p (r t) c -> p r t c", r=nr)[:, :, t0:t1, :],
                    in_=Vs_d[g, pair].rearrange("k (r t) c -> k r t c", r=d)[:, r0:r0 + nr, mlo:mhi + 1, :]), a_ld[slot])
                u["slot"] = slot
                u["ldn"] = a_ld[slot].n

            pending = []

            def flush(keep):
                while len(pending) > keep:
                    pending.pop(0)[1]()

            cur_zs = 0
            emit_loads(0)
            loaded = 0
            for ui, u in enumerate(units):
                slot = u["slot"]
                d, ns, J, pair, g = u["d"], u["ns"], u["J"], u["pair"], u["g"]
                Lr = TOK // d
                n = SB // d
                P0 = J * n
                KW = n + 256
                ntw = n // 128 + 2
                mid = Lr // 2
                r0, nr = u["r0"], u["nr"]
                ug = ubase + ui
                if u["first"]:
                    pj = ac["pj"]
                    ac["pj"] += 1
                    cur_zs = pj % 2
                    zs = cur_zs
                    fin_need = fin_hist[-2] if len(fin_hist) >= 2 else 0
                    if len(fin_hist) >= 2:
                        pg.wait("sp", a_fin, fin_hist[-2])
                    pg.dma(lambda e, zs=zs, pair=pair, J=J: e.dma_start(
                        out=ZT[zs], in_=fm_d[zty, pair][:, J * SB:(J + 1) * SB]), a_zld[zs])
                    if J == 0:
                        for i_ in range(4):
                            pg.wait("pool", a_p[i_], a_p[i_].n)
                        for gg in range(ng):
                            for vv in range(3):
                                di = ac["dst"]
                                ac["dst"] += 1
                                hs = di % 2
                                pg.wait("sp", a_dcast, max(0, di - 1))
                                pg.dma(lambda e, gg=gg, vv=vv, pair=pair, hs=hs: e.dma_start(
                                    out=DTS[hs][:, 0:2 * PW].rearrange("p (h w) -> p h w", h=2),
                                    in_=dtab[pair, gg, vv]), a_dld[hs])
                                pg.wait("pool", a_dld[hs], a_dld[hs].n)
                                pg.op("pool", lambda e, gg=gg, vv=vv, hs=hs: e.tensor_copy(
                                    out=DTB[:, (gg * 3 + vv) * 2 * PW:(gg * 3 + vv + 1) * 2 * PW], in_=DTS[hs][:, 0:2 * PW]), a_dcast)
                                assert a_dcast.n == di + 1
                packs = []
                if ltype == "B":
                    qsz = 128
                    for qt_i in range(n // qsz):
                        q0 = P0 + qt_i * qsz
                        segs = []
                        for j in range(3):
                            m = q0 // 128 - 1 + j
                            segs.append(dict(ri=0, m=m, col=128 * j, N=128, qlo=q0, ocol=0))
                        packs.append(dict(q0=q0, segs=segs, nruns=1, rr=r0, ocols=qsz))
                elif d < 16:
                    qsz = 256
                    offs, Ns, qoff = [0, 64, 256, 448], [64, 192, 192, 64], [0, 0, 64, 192]
                    for ri in range(nr):
                        for qt_i in range(n // qsz):
                            q0 = P0 + qt_i * qsz
                            segs = []
                            for j in range(4):
                                m = q0 // 128 - 1 + j
                                segs.append(dict(ri=ri, m=m, col=offs[j], N=Ns[j], qlo=q0 + qoff[j], ocol=qoff[j]))
                            packs.append(dict(q0=q0, segs=segs, nruns=1, rr=r0 + ri, ocols=qsz))
                else:
                    qsz = 128
                    offs, Ns, qoff = [0, 64, 192], [64, 128, 64], [0, 0, 64]
                    q0 = P0
                    for rp in range(nr // 2):
                        segs = []
                        for a_ in range(2):
                            ri = 2 * rp + a_
                            for j in range(3):
                                m = q0 // 128 - 1 + j
                                segs.append(dict(ri=ri, m=m, col=256 * a_ + offs[j], N=Ns[j], qlo=q0 + qoff[j], ocol=128 * a_ + qoff[j]))
                        packs.append(dict(q0=q0, segs=segs, nruns=2, rr=r0 + 2 * rp, ocols=256))
                for pi, pk in enumerate(packs):
                    q0 = pk["q0"]
                    segs = [sg for sg in pk["segs"] if 0 <= sg["m"] < Lr // 128]
                    var = 0
                    if q0 == mid - qsz:
                        var = 1
                    elif q0 == mid:
                        var = 2
                    bi = ac["batch"]
                    ac["batch"] += 1
                    bs = bi % 4
                    ss = bi % 2
                    oset = bi % 2
                    pg.wait("pe", a_ld[slot], u["ldn"])
                    if (bi - 2) in exp_n:
                        pg.wait("pe", a_e[(bi - 2) % 4], exp_n[bi - 2])
                    for si_, sg in enumerate(segs):
                        kcol = sg["ri"] * KW + (128 * sg["m"] - (P0 - 128))
                        qcol = sg["ri"] * n + (sg["qlo"] - P0)
                        for h in range(2):
                            pg.op("pe", lambda e, ss=ss, h=h, slot=slot, kcol=kcol, qcol=qcol, sg=sg: e.matmul(
                                PS[:, 2 * ss + h, sg["col"]:sg["col"] + sg["N"]],
                                lhsT=KT[slot][h * 64:(h + 1) * 64, kcol:kcol + 128],
                                rhs=QT[slot][h * 64:(h + 1) * 64, qcol:qcol + sg["N"]], start=True, stop=True),
                                a_s[ss] if (h == 1 and si_ == len(segs) - 1) else None)
                    pg.wait("act", a_s[ss], a_s[ss].n)
                    pg.wait("act", a_pv[bs], a_pv[bs].n)
                    exp_n[bi] = pg.op("act", lambda e, bs=bs, ss=ss: e.activation(
                        out=EP[bs][:, :, 0:PW], in_=PS[:, 2 * ss:2 * ss + 2, 0:PW], func=AF.Exp, scale=0.125), a_e[bs])
                    pg.wait("dve", a_e[bs], a_e[bs].n)
                    pg.wait("dve", a_dcast, a_dcast.n)
                    dview = DTB[:, (g * 3 + var) * 2 * PW:(g * 3 + var + 1) * 2 * PW].rearrange("p (h w) -> p h w", h=2)
                    pg.op("dve", lambda e, bs=bs, dview=dview: e.tensor_tensor(
                        out=EP[bs][:, :, 0:PW], in0=EP[bs][:, :, 0:PW], in1=dview, op=ALU.mult), a_p[bs])
                    p_need = a_p[bs].n
                    ufirst = u["first"]
                    last_of_unit = (pi == len(packs) - 1)
                    tok0 = q0 - P0

                    ACC = ACC2[cur_zs]

                    def pv(bs=bs, oset=oset, slot=slot, segs=segs, pk=pk, p_need=p_need, ufirst=ufirst,
                           last_of_unit=last_of_unit, ug=ug, d=d, ntw=ntw, P0=P0, tok0=tok0, qsz=qsz, ACC=ACC, fin_need=fin_need):
                        pg.wait("pe", a_p[bs], p_need)
                        pg.wait("pe", a_evac[oset], a_evac[oset].n)
                        for h in range(2):
                            for si_, sg in enumerate(segs):
                                vtile = sg["ri"] * ntw + (sg["m"] - (P0 // 128 - 1))
                                pg.op("pe", lambda e, h=h, sg=sg, vtile=vtile, si_=si_: e.matmul(
                                    PS[:, 4 + 2 * oset + h, sg["ocol"]:sg["ocol"] + sg["N"]],
                                    lhsT=VT[slot][:, vtile, h * 128:(h + 1) * 128],
                                    rhs=EP[bs][:, h, sg["col"]:sg["col"] + sg["N"]],
                                    start=(si_ == 0), stop=(si_ == len(segs) - 1), skip_group_check=True),
                                    a_pv[bs] if (h == 1 and si_ == len(segs) - 1) else None)
                        pg.wait("dve", a_pv[bs], a_pv[bs].n)
                        oc = pk["ocols"]
                        osrc = PS[:, 4 + 2 * oset:6 + 2 * oset, 0:oc]
                        if d == 1:
                            accv = ACC[:, :, tok0:tok0 + oc]
                        elif pk["nruns"] == 1:
                            accv = ACC.rearrange("p h (u r) -> p h r u", r=d)[:, :, pk["rr"], tok0:tok0 + oc]
                        else:
                            accv = ACC.rearrange("p h (u r) -> p h r u", r=d)[:, :, pk["rr"]:pk["rr"] + 2, tok0:tok0 + qsz]
                            osrc = osrc.rearrange("p h (a u) -> p h a u", a=2)
                        if ufirst:
                            pg.wait("dve", a_fin, fin_need)
                            pg.op("dve", lambda e: e.tensor_copy(out=accv, in_=osrc), a_evac[oset])
                        else:
                            pg.op("dve", lambda e: e.tensor_tensor(out=accv, in0=osrc, in1=accv, op=ALU.add), a_evac[oset])
                        if last_of_unit:
                            unit_end[ug] = (a_evac[0].n, a_evac[1].n)
                    pending.append((ui, pv))
                    flush(KEEP)
                    if loaded == ui and ui + 1 < len(units) and all(tag != ui - 1 for tag, _ in pending):
                        emit_loads(ui + 1)
                        loaded = ui + 1
                if u["lastu"]:
                    flush(0)
                if loaded == ui and ui + 1 < len(units):
                    flush(0)
                    emit_loads(ui + 1)
                    loaded = ui + 1
                if u["lastu"]:
                    zs = cur_zs
                    ACC = ACC2[cur_zs]
                    a_den, b_den = ACC[64:128, 0, :], ACC[0:64, 1, :]
                    a_num, b_num = ACC[0:64, 0, :], ACC[64:128, 1, :]
                    fin_before = a_fin.n
                    if ltype == "B":
                        pg.wait("dve", a_evac[0], a_evac[0].n)
                        pg.wait("dve", a_evac[1], a_evac[1].n)
                        pg.wait("dve", pro_ev, pro_ev.n)
                        li = l // 2
                        ca, cb = li * 16 + 2 * pair, li * 16 + 2 * pair + 1
                        pg.op("dve", lambda e, ca=ca, a_den=a_den: e.tensor_scalar(out=a_den, in0=a_den,
                                                                      scalar1=esink[64:128, ca:ca + 1], scalar2=None, op0=ALU.add), a_evac[0])
                        pg.op("dve", lambda e, cb=cb, b_den=b_den: e.tensor_scalar(out=b_den, in0=b_den,
                                                                      scalar1=esink[0:64, cb:cb + 1], scalar2=None, op0=ALU.add), a_evac[1])
                    pg.wait("act", a_evac[0], a_evac[0].n)
                    pg.wait("act", a_evac[1], a_evac[1].n)
                    pg.wait("act", a_fin, fin_before)
                    pg.op("act", lambda e, a_den=a_den: e.activation(out=RS[0:64, :], in_=a_den, func=AF.Ln), a_ln)
                    n_ = pg.op("act", lambda e, b_den=b_den: e.activation(out=RS[64:128, :], in_=b_den, func=AF.Ln), a_ln)
                    pg.wait("act", a_ln, n_)
                    pg.op("act", lambda e: e.activation(out=RS, in_=RS, func=AF.Exp, scale=-1.0), a_ln)
                    pg.wait("pool", a_ln, a_ln.n)
                    pg.wait("pool", a_zld[zs], a_zld[zs].n)
                    pg.wait("pool", a_gst[zs], a_gst[zs].n)
                    n_ = pg.op("pool", lambda e, a_num=a_num: e.tensor_tensor(out=RS[0:64, :], in0=a_num, in1=RS[0:64, :], op=ALU.mult), a_fin)
                    n_ = pg.op("pool", lambda e, b_num=b_num: e.tensor_tensor(out=RS[64:128, :], in0=b_num, in1=RS[64:128, :], op=ALU.mult), a_fin)
                    pg.wait("pool", a_fin, n_)
                    pg.op("pool", lambda e, zs=zs: e.tensor_tensor(out=GT[zs], in0=RS, in1=ZT[zs], op=ALU.mult), a_fin)
                    fin_hist.append(a_fin.n)
                    pg.wait("sp", a_fin, a_fin.n)
                    pg.dma(lambda e, zs=zs, pair=pair, J=J: e.dma_start(out=gs_d[pair, :, J * SB:(J + 1) * SB], in_=GT[zs]), a_gst[zs])
            ac["unit"] += len(units)
            for e_ in a_gst:
                pg.wait("sp", e_, e_.n)

        phase_M(-1)
        phase_O(-1)
        for l in range(nlayers):
            phase_P(l)
            phase_A(l)
            phase_M(l)
            phase_O(l)
        pg.wait("sp", o_xst[0], o_xst[0].n)
        pg.wait("sp", o_xst[1], o_xst[1].n)
        pg.wait("sp", o_xst[2], o_xst[2].n)

        with nc.Block() as block:
            @block.sync
            def _(e):
                pg.replay("sp", e)

            @block.tensor
            def _(e):
                pg.replay("pe", e)

            @block.scalar
            def _(e):
                pg.replay("act", e)

            @block.vector
            def _(e):
                pg.replay("dve", e)

            @block.gpsimd
            def _(e):
                pg.replay("pool", e)
    return nc


def _group_layout(wcat):
    nf = wcat.shape[1]
    assert nf % 512 == 0
    return np.ascontiguousarray(wcat.reshape(8, 128, nf // 512, 512).transpose(2, 1, 0, 3))


def _layer_weights(l, w_in_a, w_in_b):
    if l % 2 == 0:
        w = w_in_a[l // 2]
        cols = []
        for g in range(3):
            cols.append(w[:, g * 3072:g * 3072 + 1024])
            cols.append(w[:, g * 3072 + 1024:g * 3072 + 2048])
        cols.append(w[:, 9216:10240])
        for g in range(3):
            cols.append(w[:, g * 3072 + 2048:g * 3072 + 3072])
        return _group_layout(np.concatenate(cols, axis=1))
    w = w_in_b[l // 2]
    q = w[:, 0:1024]
    k = w[:, 1024:1280].reshape(1024, 4, 64)
    v = w[:, 1280:1536].reshape(1024, 4, 64)
    z = w[:, 1536:2560]
    kk = np.repeat(k, 4, axis=1).reshape(1024, 1024)
    vv = np.repeat(v, 4, axis=1).reshape(1024, 1024)
    return _group_layout(np.concatenate([q, kk, z, vv], axis=1))


def _dtable(groups, W, join, ltype):
    ng = len(groups)
    PW = 512 if ltype == "A" else 384
    out = np.zeros((8, ng, 3, 128, 2, PW), np.float32)
    slopes = 2.0 ** (-8.0 * np.arange(1, 17) / 16.0)
    k = np.arange(128)[:, None]
    j = np.arange(W)[None, :]
    for gi, (d, ns) in enumerate(groups):
        rel = k + ns - j
        valid = (np.abs(rel) <= ns)
        if ltype == "B":
            sl_ = [(256, 384), (128, 256), (0, 128)]
            cross_last, cross_first = (256, 384), (0, 128)
            reps = 1
        elif d < 16:
            sl_ = [(192, 256), (64, 256), (0, 192), (0, 64)]
            cross_last, cross_first = (448, 512), (0, 64)
            reps = 1
        else:
            sl_ = [(192, 256), (64, 192), (0, 64)]
            cross_last, cross_first = (192, 256), (0, 64)
            reps = 2
        for pair in range(8):
            for h in range(2):
                sl = slopes[2 * pair + h]
                base = np.where(valid, np.exp(-sl * d * np.abs(rel).astype(np.float64)), 0.0).astype(np.float32)
                one = np.concatenate([base[:, a:b] for a, b in sl_], axis=1)
                v1 = one.copy()
                v2 = one.copy()
                if not join:
                    v1[:, cross_last[0]:cross_last[1]] = 0.0
                    v2[:, cross_first[0]:cross_first[1]] = 0.0
                for vi, tb in enumerate((one, v1, v2)):
                    out[pair, gi, vi, :, h, :] = np.concatenate([tb] * reps, axis=1)
    return out


_CACHE = {}


def _get_program(nlayers):
    if nlayers not in _CACHE:
        _CACHE[nlayers] = build_program(nlayers)
    return _CACHE[nlayers]


def make_in_maps(x_prompt, x_sample, c_prompt, c_sample, w_mod, b_mod, ln_g, ln_b,
                 w_in_a, w_out_a, w_in_b, w_out_b, sink_b, nlayers=DEPTH):
    f = lambda a: np.ascontiguousarray(np.asarray(a, dtype=np.float32))
    x_prompt, x_sample, c_prompt, c_sample = f(x_prompt), f(x_sample), f(c_prompt), f(c_sample)
    w_mod, b_mod, ln_g, ln_b = f(w_mod), f(b_mod), f(ln_g), f(ln_b)
    w_in_a, w_out_a, w_in_b, w_out_b, sink_b = f(w_in_a), f(w_out_a), f(w_in_b), f(w_out_b), f(sink_b)
    shared = {}
    for l in range(nlayers):
        shared[f"w{l}"] = _layer_weights(l, w_in_a, w_in_b)
    wo = np.stack([(w_out_a if l % 2 == 0 else w_out_b)[l // 2] for l in range(DEPTH)])
    shared["wo"] = np.ascontiguousarray(wo.reshape(DEPTH, 8, 128, D).transpose(0, 2, 1, 3))
    shared["wm"] = np.ascontiguousarray(w_mod.reshape(DEPTH, 8, 128, 6, 512).transpose(0, 3, 2, 1, 4))
    shared["bmod"] = b_mod
    shared["lng"] = ln_g
    shared["lnb"] = ln_b
    shared["sink"] = sink_b
    shared["ident"] = np.eye(128, dtype=np.float32)
    dA = {j: _dtable(LT["A"]["groups"], 256, j, "A") for j in (False, True)}
    dB = {j: _dtable(LT["B"]["groups"], 384, j, "B") for j in (False, True)}
    in_maps = []
    for i in range(NCORES):
        m = dict(shared)
        if i < 4:
            m["x"] = x_prompt[i]
            m["cvec"] = np.ascontiguousarray(np.stack([c_prompt[i], c_prompt[i]]))
            m["dA"], m["dB"] = dA[True], dB[True]
        else:
            j = 2 * (i - 4)
            m["x"] = np.ascontiguousarray(x_sample[j:j + 2].reshape(TOK, D))
            m["cvec"] = np.ascontiguousarray(c_sample[j:j + 2])
            m["dA"], m["dB"] = dA[False], dB[False]
        in_maps.append(m)
    return in_maps


def kernel(x_prompt, x_sample, c_prompt, c_sample, w_mod, b_mod, ln_g, ln_b,
           w_in_a, w_out_a, w_in_b, w_out_b, sink_b):
    in_maps = make_in_maps(x_prompt, x_sample, c_prompt, c_sample, w_mod, b_mod, ln_g, ln_b,
                           w_in_a, w_out_a, w_in_b, w_out_b, sink_b)
    nc = _get_program(DEPTH)
    res = run_bass_kernel_spmd(nc, in_maps, core_ids=list(range(NCORES)))
    ys = [np.asarray(r["y"], dtype=np.float32) for r in res.results]
    y_prompt = np.stack(ys[0:4]).reshape(4, TOK, D)
    y_sample = np.concatenate([y.reshape(2, TOK // 2, D) for y in ys[4:8]], axis=0)
    return (y_prompt, y_sample)
```

```python
import contextlib
import numpy as np
import concourse.bass as bass
import concourse.mybir as mybir
from concourse.bass_utils import run_bass_kernel_spmd

F32 = mybir.dt.float32
BF16 = mybir.dt.bfloat16
ALU = mybir.AluOpType
AF = mybir.ActivationFunctionType

NCORES = 8
D = 1024
TOK = 8192
SB = 2048
NSB = TOK // SB
DEPTH = 4
ALPHA = (2.0 * DEPTH) ** 0.25
LN_EPS = 1e-5
A_DILS = (1, 4, 16)

LT = {
    "A": dict(groups=[(1, 64), (4, 64), (16, 64)], nfm=7, ngrp_fm=14, ngrp_v=6, W=256),
    "B": dict(groups=[(1, 128)], nfm=3, ngrp_fm=6, ngrp_v=2, W=384),
}
def fm_types(t):
    return 7 if t == "A" else 3


def layer_type(l):
    return "A" if l % 2 == 0 else "B"


class Ev:
    def __init__(self, nc, stack, name):
        self.h = stack.enter_context(nc.semaphore(name))
        self.n = 0


class Prog:
    def __init__(self, nc, stack):
        self.nc = nc
        self.stack = stack
        self.q = {k: [] for k in ("pe", "act", "dve", "pool", "sp")}
        self.nev = 0

    def ev(self, name):
        self.nev += 1
        return Ev(self.nc, self.stack, f"{name}_{self.nev}")

    def op(self, eng, fn, ev=None, amt=1):
        self.q[eng].append((fn, ev, amt))
        if ev is not None:
            ev.n += amt
            return ev.n
        return None

    def dma(self, fn, ev):
        return self.op("sp", fn, ev, 16)

    def wait(self, eng, ev, val):
        if val <= 0:
            return
        self.q[eng].append(("wait", ev, val))

    def replay(self, eng_name, eng):
        last_wait = {}
        for item in self.q[eng_name]:
            if item[0] == "wait":
                _, ev, val = item
                if last_wait.get(id(ev), 0) >= val:
                    continue
                last_wait[id(ev)] = val
                eng.wait_ge(ev.h, val)
            else:
                fn, ev, amt = item
                ins = fn(eng)
                if ev is not None:
                    ins.then_inc(ev.h, amt)


def build_program(nlayers=DEPTH):
    nc = bass.Bass("TRN2", target_bir_lowering=False)
    dt = nc.dram_tensor
    x_in = dt("x", [TOK, D], F32, kind="ExternalInput").ap()
    cvec = dt("cvec", [2, D], F32, kind="ExternalInput").ap()
    ident_d = dt("ident", [128, 128], F32, kind="ExternalInput").ap()
    w_d = []
    for l in range(nlayers):
        t = LT[layer_type(l)]
        w_d.append(dt(f"w{l}", [t["ngrp_fm"] + t["ngrp_v"], 128, 8, 512], F32, kind="ExternalInput").ap())
    wo_d = dt("wo", [DEPTH, 128, 8, D], F32, kind="ExternalInput").ap()
    wm_d = dt("wm", [DEPTH, 6, 128, 8, 512], F32, kind="ExternalInput").ap()
    bmod_d = dt("bmod", [DEPTH, 3 * D], F32, kind="ExternalInput").ap()
    lng_d = dt("lng", [DEPTH, D], F32, kind="ExternalInput").ap()
    lnb_d = dt("lnb", [DEPTH, D], F32, kind="ExternalInput").ap()
    sink_d = dt("sink", [2, 16], F32, kind="ExternalInput").ap()
    dA_d = dt("dA", [8, 3, 3, 128, 2, 512], F32, kind="ExternalInput").ap()
    dB_d = dt("dB", [8, 1, 3, 128, 2, 384], F32, kind="ExternalInput").ap()
    y_out = dt("y", [TOK, D], F32, kind="ExternalOutput").ap()
    xs = [dt(f"xs{i}", [TOK, D], F32).ap() for i in range(2)]
    uT_d = dt("uT", [NSB, 128, 8, SB], BF16).ap()
    fm_d = dt("fm", [7, 8, 128, TOK], BF16).ap()
    Vs_d = dt("Vs", [3, 8, 128, 64, 256], BF16).ap()
    gs_d = dt("gs", [8, 128, TOK], BF16).ap()

    stack = contextlib.ExitStack()
    with stack:
        sb = lambda name, shape, dtp: stack.enter_context(nc.sbuf_tensor(name, shape, dtp))
        BFA = sb("bfa", [128, 55296], BF16)
        FA = sb("fa", [128, 12288], F32)
        MOD = sb("modt", [128, 8, D], F32)
        screp = sb("screp", [128, 2, 8, 128], F32)
        ident = sb("identt", [128, 128], F32)
        small = sb("smallt", [128, 64], F32)
        esink = sb("esink", [128, 32], F32)
        csb = sb("csb", [128, 16], F32)
        biasb = sb("biasb", [128, 2, 512], F32)
        PS = stack.enter_context(nc.psum_tensor("ps", [128, 8, 512], F32))
        pg = Prog(nc, stack)

        GATE = [MOD[:, 0, :], MOD[:, 1, :]]
        SC1 = [MOD[:, 2, :], MOD[:, 3, :]]
        SH = [MOD[:, 4, :], MOD[:, 5, :]]
        LNG = MOD[:, 6, :]
        LNB = MOD[:, 7, :]

        misc_ld = pg.ev("miscld")

        pg.dma(lambda e: e.dma_start(out=ident[:], in_=ident_d[:, :]), misc_ld)
        pg.dma(lambda e: e.dma_start(out=csb[:, :].rearrange("p (c k) -> p c k", c=2),
                                     in_=cvec.rearrange("c (k p) -> p c k", p=128),
                                     allow_slow_non_contiguous=True), misc_ld)
        pg.dma(lambda e: e.dma_start(out=esink[:, :], in_=sink_d.rearrange("a h -> (a h)").partition_broadcast(128)), misc_ld)
        pro_ev = pg.ev("pro")
        pg.wait("act", misc_ld, misc_ld.n)
        pg.op("act", lambda e: e.activation(out=csb[:, :], in_=csb[:, :], func=AF.Silu), pro_ev)
        pg.op("act", lambda e: e.activation(out=esink[:, :], in_=esink[:, :], func=AF.Exp), pro_ev)
        pg.op("pool", lambda e: e.memset(screp[:], 1.0), pro_ev)
        pg.wait("dve", pro_ev, pro_ev.n)
        pro2 = pg.ev("pro2")
        for c in range(2):
            for k in range(8):
                pg.op("dve", lambda e, c=c, k=k: e.tensor_scalar(
                    out=screp[:, c, k, :], in0=screp[:, c, k, :], scalar1=csb[:, c * 8 + k:c * 8 + k + 1],
                    scalar2=None, op0=ALU.mult), pro2)

        wst_views = [FA[:, 0:4096].rearrange("p (k c) -> p k c", k=8), FA[:, 4096:8192].rearrange("p (k c) -> p k c", k=8)]
        m_ld = pg.ev("mld")
        m_pe = pg.ev("mpe")
        m_dve = pg.ev("mdve")

        def phase_M(l):
            jobs = []
            if l >= 0:
                jobs += [(l, 4, GATE, 0, 0.0), (l, 5, GATE, 1, 0.0)]
            if l + 1 < nlayers:
                jobs += [(l + 1, 0, SH, 0, 0.0), (l + 1, 1, SH, 1, 0.0), (l + 1, 2, SC1, 0, 1.0), (l + 1, 3, SC1, 1, 1.0)]
            if l >= 0:
                pg.wait("sp", m_dve, m_dve.n)
                pg.dma(lambda e: e.dma_start(out=LNG, in_=lng_d[l, :].partition_broadcast(128)), m_ld)
                pg.dma(lambda e: e.dma_start(out=LNB, in_=lnb_d[l, :].partition_broadcast(128)), m_ld)
            for ji, (ll, cg, tgt, half, add1) in enumerate(jobs):
                slot = ji % 2
                pg.wait("sp", m_pe, m_pe.n)
                pg.wait("sp", m_dve, m_dve.n)
                pg.dma(lambda e, ll=ll, cg=cg, slot=slot: e.dma_start(out=wst_views[slot], in_=wm_d[ll, cg]), m_ld)
                pg.dma(lambda e, ll=ll, cg=cg, slot=slot: e.dma_start(
                    out=biasb[:, slot, :], in_=bmod_d[ll, cg * 512:(cg + 1) * 512].partition_broadcast(128)), m_ld)
                pg.wait("pe", m_ld, m_ld.n)
                pg.wait("pe", pro2, pro2.n)
                pg.wait("pe", m_dve, m_dve.n)
                for c in range(2):
                    for k in range(8):
                        pg.op("pe", lambda e, c=c, k=k, slot=slot: e.matmul(
                            PS[:, c, :], lhsT=screp[:, c, k, :], rhs=wst_views[slot][:, k, :],
                            start=(k == 0), stop=(k == 7)), m_pe if (c == 1 and k == 7) else None)
                pg.wait("dve", m_pe, m_pe.n)
                for c in range(2):
                    pg.op("dve", lambda e, c=c, tgt=tgt, half=half, add1=add1, slot=slot: e.scalar_tensor_tensor(
                        out=tgt[c][:, half * 512:(half + 1) * 512], in0=PS[:, c, :], scalar=add1,
                        in1=biasb[:, slot, :], op0=ALU.add, op1=ALU.add), m_dve)
            return

        gT = [BFA[:, i * 4096:(i + 1) * 4096].rearrange("p (k t) -> p k t", k=8) for i in range(2)]
        wo_bf = BFA[:, 8192:16384].rearrange("p (k c) -> p k c", k=8)
        uTst = [BFA[:, 16384 + i * 4096:16384 + (i + 1) * 4096].rearrange("p (k t) -> p k t", k=8) for i in range(2)]
        xt = [FA[:, i * 1024:(i + 1) * 1024] for i in range(4)]
        ht = [FA[:, 4096 + i * 1024:4096 + (i + 1) * 1024] for i in range(2)]
        ut = [FA[:, 6144 + i * 1024:6144 + (i + 1) * 1024] for i in range(2)]
        junk = BFA[:, 24576:25600]
        wo_st = FA[:, 8192:12288].rearrange("p (k c) -> p k c", k=8)

        o_gld = [pg.ev(f"ogld{i}") for i in range(2)]
        o_xld = [pg.ev(f"oxld{i}") for i in range(4)]
        o_wld = pg.ev("owld")
        o_wcast = pg.ev("owcast")
        o_y = [pg.ev(f"oy{i}") for i in range(2)]
        o_h = [pg.ev(f"oh{i}") for i in range(2)]
        o_h2 = [pg.ev(f"oh2{i}") for i in range(2)]
        o_sq = [pg.ev(f"osq{i}") for i in range(2)]
        o_st = [pg.ev(f"ost{i}") for i in range(2)]
        o_xn = [pg.ev(f"oxn{i}") for i in range(2)]
        o_xo = [pg.ev(f"oxo{i}") for i in range(4)]
        o_xst = [pg.ev(f"oxst{i}") for i in range(4)]
        o_u1 = [pg.ev(f"ou1{i}") for i in range(2)]
        o_u2 = [pg.ev(f"ou2{i}") for i in range(2)]
        o_tp = [pg.ev(f"otp{i}") for i in range(2)]
        o_te = [pg.ev(f"ote{i}") for i in range(2)]
        o_ust = [pg.ev(f"oust{i}") for i in range(2)]
        cnt = dict(sub=0, tg=0)
        xt_free = [[], [], [], []]
        ht_free = [None, None]

        def phase_O(l):
            full = l >= 0
            last = (l == nlayers - 1)
            src = x_in if l <= 0 else xs[(l - 1) % 2]
            dst = y_out if last else xs[l % 2]
            mod_ready_dve = m_dve.n
            mod_ready_ld = m_ld.n
            pg.wait("sp", m_pe, m_pe.n)
            pg.wait("sp", m_dve, m_dve.n)
            if full:
                for hlf in range(2):
                    pg.wait("sp", o_wcast, o_wcast.n)
                    pg.dma(lambda e, hlf=hlf: e.dma_start(out=wo_st, in_=wo_d[l, :, :, hlf * 512:(hlf + 1) * 512]), o_wld)
                    pg.wait("pool", o_wld, o_wld.n)
                    pg.op("pool", lambda e, hlf=hlf: e.tensor_copy(out=wo_bf[:, :, hlf * 512:(hlf + 1) * 512], in_=wo_st), o_wcast)
            base_sub = cnt["sub"]
            base_tg = cnt["tg"]
            NSUB = TOK // 128

            def emit_xload(i):
                si = base_sub + i
                xs_ = si % 4
                for (ev_, n_) in xt_free[xs_]:
                    pg.wait("sp", ev_, n_)
                xt_free[xs_] = []
                row0 = i * 128
                pg.dma(lambda e: e.dma_start(out=xt[xs_], in_=src[row0:row0 + 128, :]), o_xld[xs_])
                return o_xld[xs_].n

            def emit_gload(tg):
                gslot = (base_tg + tg) % 2
                pg.wait("sp", o_y[0], o_y[0].n)
                pg.wait("sp", o_y[1], o_y[1].n)
                pg.dma(lambda e: e.dma_start(
                    out=gT[gslot], in_=gs_d[:, :, tg * 512:(tg + 1) * 512].rearrange("k p t -> p k t")), o_gld[gslot])
                return o_gld[gslot].n

            pend_u = []
            xld_n = {0: emit_xload(0)}
            gld_n = {}
            if full:
                gld_n[0] = emit_gload(0)
            for tg in range(TOK // 512):
                c = tg // 8
                gi = base_tg + tg
                gslot = gi % 2
                uslot = gi % 2
                if full and tg + 1 < TOK // 512:
                    gld_n[tg + 1] = emit_gload(tg + 1)
                for s in range(4):
                    i = tg * 4 + s
                    si = base_sub + i
                    xs_ = si % 4
                    p2 = si % 2
                    row0 = i * 128
                    if i + 1 < NSUB:
                        xld_n[i + 1] = emit_xload(i + 1)
                    b = 8 * p2
                    if full:
                        pg.wait("pe", o_gld[gslot], gld_n[tg])
                        pg.wait("pe", o_wcast, o_wcast.n)
                        pg.wait("pe", o_h[p2], o_h[p2].n)
                        for hlf in range(2):
                            for k in range(8):
                                pg.op("pe", lambda e, hlf=hlf, k=k, p2=p2, gslot=gslot, s=s: e.matmul(
                                    PS[:, 2 * p2 + hlf, :], lhsT=gT[gslot][:, k, s * 128:(s + 1) * 128],
                                    rhs=wo_bf[:, k, hlf * 512:(hlf + 1) * 512], start=(k == 0), stop=(k == 7)),
                                    o_y[p2] if (hlf == 1 and k == 7) else None)
                        Y = PS[:, 2 * p2:2 * p2 + 2, :].rearrange("p a b -> p (a b)")
                        pg.wait("dve", o_y[p2], o_y[p2].n)
                        pg.wait("dve", m_dve, mod_ready_dve)
                        pg.op("dve", lambda e, b=b: e.memset(small[:, b:b + 2], 0.0))
                        pg.op("dve", lambda e, p2=p2, Y=Y, c=c: e.tensor_tensor(out=ht[p2], in0=Y, in1=GATE[c], op=ALU.mult), o_h[p2])
                        pg.wait("dve", o_xld[xs_], xld_n[i])
                        pg.wait("dve", o_h[p2], o_h[p2].n)
                        pg.op("dve", lambda e, p2=p2, xs_=xs_, b=b: e.scalar_tensor_tensor(
                            out=ht[p2], in0=xt[xs_], scalar=ALPHA, in1=ht[p2], op0=ALU.mult, op1=ALU.add,
                            accum_out=small[:, b:b + 1]), o_h2[p2])
                        pg.wait("act", o_h2[p2], o_h2[p2].n)
                        pg.op("act", lambda e, p2=p2, b=b: e.activation(out=junk, in_=ht[p2], func=AF.Square,
                                                                        accum_out=small[:, b + 1:b + 2]), o_sq[p2])
                        pg.wait("dve", o_sq[p2], o_sq[p2].n)
                        ops = [
                            lambda e, b=b: e.tensor_scalar(out=small[:, b + 2:b + 3], in0=small[:, b:b + 1], scalar1=-1.0 / D, scalar2=None, op0=ALU.mult),
                            lambda e, b=b: e.tensor_tensor(out=small[:, b + 3:b + 4], in0=small[:, b + 2:b + 3], in1=small[:, b + 2:b + 3], op=ALU.mult),
                            lambda e, b=b: e.scalar_tensor_tensor(out=small[:, b + 4:b + 5], in0=small[:, b + 1:b + 2], scalar=1.0 / D, in1=small[:, b + 3:b + 4], op0=ALU.mult, op1=ALU.subtract),
                            lambda e, b=b: e.tensor_scalar(out=small[:, b + 4:b + 5], in0=small[:, b + 4:b + 5], scalar1=LN_EPS, scalar2=None, op0=ALU.add),
                        ]
                        for f in ops:
                            n = pg.op("dve", f, o_st[p2])
                            pg.wait("dve", o_st[p2], n)
                        pg.wait("act", o_st[p2], o_st[p2].n)
                        n = pg.op("act", lambda e, b=b: e.activation(out=small[:, b + 5:b + 6], in_=small[:, b + 4:b + 5], func=AF.Sqrt), o_sq[p2])
                        pg.wait("dve", o_sq[p2], n)
                        n = pg.op("dve", lambda e, b=b: e.reciprocal(out=small[:, b + 5:b + 6], in_=small[:, b + 5:b + 6]), o_st[p2])
                        pg.wait("dve", o_st[p2], n)
                        pg.wait("dve", m_ld, mod_ready_ld)
                        n = pg.op("dve", lambda e, p2=p2, b=b: e.scalar_tensor_tensor(
                            out=ht[p2], in0=ht[p2], scalar=small[:, b + 2:b + 3], in1=LNG, op0=ALU.add, op1=ALU.mult), o_xn[p2])
                        pg.wait("dve", o_xn[p2], n)
                        pg.op("dve", lambda e, p2=p2, xs_=xs_, b=b: e.scalar_tensor_tensor(
                            out=xt[xs_], in0=ht[p2], scalar=small[:, b + 5:b + 6], in1=LNB, op0=ALU.mult, op1=ALU.add), o_xo[xs_])
                        pg.wait("sp", o_xo[xs_], o_xo[xs_].n)
                        pg.dma(lambda e, xs_=xs_, row0=row0: e.dma_start(out=dst[row0:row0 + 128, :], in_=xt[xs_]), o_xst[xs_])
                        xt_free[xs_].append((o_xst[xs_], o_xst[xs_].n))
                    if not last:
                        xo_n = o_xo[xs_].n

                        def ublock(i=i, tg=tg, s=s, xs_=xs_, p2=p2, c=c, uslot=uslot, xo_n=xo_n):
                            if full:
                                pg.wait("pool", o_xo[xs_], xo_n)
                            else:
                                pg.wait("pool", o_xld[xs_], xld_n[i])
                            pg.wait("pool", m_dve, mod_ready_dve)
                            pg.wait("pool", o_tp[p2], o_tp[p2].n)
                            n = pg.op("pool", lambda e: e.tensor_tensor(out=ut[p2], in0=xt[xs_], in1=SC1[c], op=ALU.mult), o_u1[p2])
                            xt_free[xs_].append((o_u1[p2], o_u1[p2].n))
                            pg.wait("pool", o_u1[p2], n)
                            pg.op("pool", lambda e: e.tensor_tensor(out=ut[p2], in0=ut[p2], in1=SH[c], op=ALU.add), o_u2[p2])
                            pg.wait("pe", o_u2[p2], o_u2[p2].n)
                            pg.wait("pe", misc_ld, misc_ld.n)
                            pg.wait("pe", o_te[p2], o_te[p2].n)
                            for k in range(8):
                                pg.op("pe", lambda e, k=k: e.transpose(
                                    out=PS[:, 4 + 2 * p2 + k // 4, (k % 4) * 128:(k % 4 + 1) * 128],
                                    in_=ut[p2][:, k * 128:(k + 1) * 128], identity=ident[:]),
                                    o_tp[p2] if k == 7 else None)
                            pg.wait("act", o_tp[p2], o_tp[p2].n)
                            if s == 0:
                                pg.wait("act", o_ust[uslot], o_ust[uslot].n)
                            pg.op("act", lambda e: e.activation(
                                out=uTst[uslot][:, :, s * 128:(s + 1) * 128],
                                in_=PS[:, 4 + 2 * p2:6 + 2 * p2, :].rearrange("p a (b t) -> p (a b) t", t=128), func=AF.Identity), o_te[p2])
                            if s == 3:
                                pg.wait("sp", o_te[0], o_te[0].n)
                                pg.wait("sp", o_te[1], o_te[1].n)
                                J = tg // 4
                                pg.dma(lambda e: e.dma_start(
                                    out=uT_d[J, :, :, (tg % 4) * 512:(tg % 4 + 1) * 512], in_=uTst[uslot]), o_ust[uslot])
                        pend_u.append(ublock)
                        while len(pend_u) > 2:
                            pend_u.pop(0)()
            while pend_u:
                pend_u.pop(0)()
            cnt["sub"] += NSUB
            cnt["tg"] += TOK // 512
            for evs in (o_xst, o_ust):
                for e_ in evs:
                    pg.wait("sp", e_, e_.n)

        uT_sb = BFA[:, 0:16384].rearrange("p (k t) -> p k t", k=8)
        wbf = [BFA[:, 16384 + i * 4096:16384 + (i + 1) * 4096].rearrange("p (k c) -> p k c", k=8) for i in range(2)]
        fstage = [BFA[:, 24576 + i * 2048:24576 + (i + 1) * 2048] for i in range(2)]
        vstage = [BFA[:, 28672 + i * 4096:28672 + (i + 1) * 4096].rearrange("p (t a c) -> p t a c", t=4, a=4) for i in range(2)]
        wst3 = [FA[:, i * 4096:(i + 1) * 4096].rearrange("p (k c) -> p k c", k=8) for i in range(3)]
        p_uld = pg.ev("puld")
        p_wld = [pg.ev(f"pwld{i}") for i in range(3)]
        p_cast = pg.ev("pcast")
        p_mm = [pg.ev(f"pmm{i}") for i in range(2)]
        p_evf = [pg.ev(f"pevf{i}") for i in range(2)]
        p_evv = [pg.ev(f"pevv{i}") for i in range(2)]
        p_fst = [pg.ev(f"pfst{i}") for i in range(2)]
        p_vst = [pg.ev(f"pvst{i}") for i in range(2)]
        p_ones = pg.ev("pones")
        pc = dict(w=0, set=0, f=0, v=0, ev=0)
        set_free = [None, None]

        def phase_P(l):
            ltype = layer_type(l)
            t = LT[ltype]
            groups = t["groups"]
            W = w_d[l]
            ngf, ngv = t["ngrp_fm"], t["ngrp_v"]
            nfm = t["nfm"]
            pg.wait("pool", p_vst[0], p_vst[0].n)
            pg.wait("pool", p_vst[1], p_vst[1].n)
            for i in range(2):
                pg.op("pool", lambda e, i=i: e.memset(vstage[i][:, :, :, 64:192], 1.0), p_ones)
            seq = [(J, wg) for J in range(NSB) for wg in range(ngf + ngv)]
            wbase = pc["w"]
            ldn = {}

            def emit_wload(i):
                w = wbase + i
                s3 = w % 3
                pg.wait("sp", p_cast, w - 2)
                wg_ = seq[i][1]
                pg.dma(lambda e: e.dma_start(out=wst3[s3], in_=W[wg_]), p_wld[s3])
                ldn[i] = p_wld[s3].n

            emit_wload(0)
            emit_wload(1)
            for i_, (J, wg) in enumerate(seq):
                if wg == 0:
                    pg.wait("sp", p_mm[0], p_mm[0].n)
                    pg.wait("sp", p_mm[1], p_mm[1].n)
                    pg.dma(lambda e, J=J: e.dma_start(out=uT_sb, in_=uT_d[J]), p_uld)
                if i_ + 2 < len(seq):
                    emit_wload(i_ + 2)
                if True:
                    wi = wbase + i_
                    s3 = wi % 3
                    ws = wi % 2
                    pg.wait("pool", p_wld[s3], ldn[i_])
                    pg.wait("pool", p_mm[0], pc.get(("mm0", ws), 0))
                    pg.wait("pool", p_mm[1], pc.get(("mm1", ws), 0))
                    pg.op("pool", lambda e, ws=ws, s3=s3: e.tensor_copy(out=wbf[ws], in_=wst3[s3]), p_cast)
                    assert p_cast.n == wi + 1
                    pg.wait("pe", p_cast, wi + 1)
                    pg.wait("pe", p_uld, p_uld.n)
                    if wg < ngf:
                        for b in range(4):
                            bi = wg * 4 + b
                            ty, pair = bi // 8, bi % 8
                            if ltype == "A":
                                is_z = (ty == 6)
                                d = 1 if is_z else groups[ty // 2][0]
                            else:
                                is_z = (ty == 2)
                                d = 1
                            si = pc["set"]
                            pc["set"] += 1
                            S_ = si % 2
                            if set_free[S_] is not None:
                                pg.wait("pe", set_free[S_][0], set_free[S_][1])
                            for k in range(8):
                                for s in range(4):
                                    pg.op("pe", lambda e, S_=S_, k=k, s=s, ws=ws, b=b: e.matmul(
                                        PS[:, 4 * S_ + s, :], lhsT=wbf[ws][:, k, b * 128:(b + 1) * 128],
                                        rhs=uT_sb[:, k, s * 512:(s + 1) * 512], start=(k == 0), stop=(k == 7)),
                                        p_mm[S_] if (k == 7 and s == 3) else None)
                            fi = pc["f"]
                            pc["f"] += 1
                            fs = fi % 2
                            eng = "act" if (is_z or fi % 2 == 0) else "dve"
                            pg.wait(eng, p_mm[S_], p_mm[S_].n)
                            pg.wait(eng, p_fst[fs], p_fst[fs].n)
                            src_ap = PS[:, 4 * S_:4 * S_ + 4, :].rearrange("p a b -> p (a b)")
                            if d > 1:
                                src_ap = src_ap.rearrange("p (u r) -> p r u", r=d)
                                dst_ap = fstage[fs].rearrange("p (r u) -> p r u", r=d)
                            else:
                                dst_ap = fstage[fs]
                            if eng == "act":
                                fn = AF.Silu if is_z else AF.Identity
                                pg.op("act", lambda e, dst_ap=dst_ap, src_ap=src_ap, fn=fn: e.activation(out=dst_ap, in_=src_ap, func=fn), p_evf[fs])
                            else:
                                pg.op("dve", lambda e, dst_ap=dst_ap, src_ap=src_ap: e.tensor_copy(out=dst_ap, in_=src_ap), p_evf[fs])
                            set_free[S_] = (p_evf[fs], p_evf[fs].n)
                            n = SB // d
                            pg.wait("sp", p_evf[fs], p_evf[fs].n)
                            pg.dma(lambda e, ty=ty, pair=pair, fs=fs, d=d, n=n, J=J: e.dma_start(
                                out=fm_d[ty, pair].rearrange("p (r u) -> p r u", r=d)[:, :, J * n:(J + 1) * n],
                                in_=fstage[fs].rearrange("p (r u) -> p r u", r=d)), p_fst[fs])
                    else:
                        vg = wg - ngf
                        g, ph = vg // 2, vg % 2
                        d = groups[g][0]
                        for sidx in range(4):
                            if d == 1:
                                tiles = [(0, 4 * sidx + i) for i in range(4)]
                            elif d == 4:
                                tiles = [(sidx, i) for i in range(4)]
                            else:
                                tiles = [(4 * sidx + i, 0) for i in range(4)]
                            si = pc["set"]
                            pc["set"] += 1
                            S_ = si % 2
                            if set_free[S_] is not None:
                                pg.wait("pe", set_free[S_][0], set_free[S_][1])
                            for ti, (r, mp) in enumerate(tiles):
                                for k in range(8):
                                    if d > 1:
                                        lh = uT_sb[:, k, :].rearrange("p (u r) -> p r u", r=d)[:, r, mp * 128:(mp + 1) * 128]
                                    else:
                                        lh = uT_sb[:, k, mp * 128:(mp + 1) * 128]
                                    pg.op("pe", lambda e, S_=S_, ti=ti, k=k, ws=ws, lh=lh: e.matmul(
                                        PS[:, 4 * S_ + ti, :], lhsT=lh, rhs=wbf[ws][:, k, :], start=(k == 0), stop=(k == 7)),
                                        p_mm[S_] if (k == 7 and ti == 3) else None)
                            vi = pc["v"]
                            pc["v"] += 1
                            vs = vi % 2
                            veng = "act" if vi % 2 == 0 else "dve"
                            pg.wait(veng, p_mm[S_], p_mm[S_].n)
                            pg.wait(veng, p_vst[vs], p_vst[vs].n)
                            pg.wait(veng, p_ones, p_ones.n)
                            srcv = PS[:, 4 * S_:4 * S_ + 4, :].rearrange("p t (a j e) -> p t a j e", a=4, j=2)
                            for j_, c0 in ((0, 0), (1, 192)):
                                if veng == "act":
                                    pg.op("act", lambda e, vs=vs, srcv=srcv, j_=j_, c0=c0: e.activation(
                                        out=vstage[vs][:, :, :, c0:c0 + 64], in_=srcv[:, :, :, j_, :], func=AF.Identity), p_evv[vs])
                                else:
                                    pg.op("dve", lambda e, vs=vs, srcv=srcv, j_=j_, c0=c0: e.tensor_copy(
                                        out=vstage[vs][:, :, :, c0:c0 + 64], in_=srcv[:, :, :, j_, :]), p_evv[vs])
                            set_free[S_] = (p_evv[vs], p_evv[vs].n)
                            pg.wait("sp", p_evv[vs], p_evv[vs].n)
                            ntr = 64 // d
                            for ti, (r, mp) in enumerate(tiles):
                                kt = r * ntr + J * (16 // d) + mp
                                pg.dma(lambda e, g=g, ph=ph, kt=kt, vs=vs, ti=ti: e.dma_start(
                                    out=Vs_d[g, 4 * ph:4 * ph + 4, :, kt, :].rearrange("a k c -> k a c"),
                                    in_=vstage[vs][:, ti, :, :]), p_vst[vs])
                    pc[("mm0", ws)] = p_mm[0].n
                    pc[("mm1", ws)] = p_mm[1].n
            pc["w"] += len(seq)
            for evs in (p_fst, p_vst):
                for e_ in evs:
                    pg.wait("sp", e_, e_.n)

        KT = [BFA[:, i * 3072:(i + 1) * 3072] for i in range(3)]
        QT = [BFA[:, 9216 + i * 2048:9216 + (i + 1) * 2048] for i in range(3)]
        VT = [BFA[:, 15360 + i * 6144:15360 + (i + 1) * 6144].rearrange("p (t c) -> p t c", c=256) for i in range(3)]
        ZT = [BFA[:, 33792 + i * 2048:33792 + (i + 1) * 2048] for i in range(2)]
        GT = [BFA[:, 37888 + i * 2048:37888 + (i + 1) * 2048] for i in range(2)]
        EP = [BFA[:, 41984 + i * 1024:41984 + (i + 1) * 1024].rearrange("p (h n) -> p h n", h=2) for i in range(4)]
        DTB = BFA[:, 46080:46080 + 9216]
        ACC2 = [FA[:, i * 4096:(i + 1) * 4096].rearrange("p (h t) -> p h t", h=2) for i in range(2)]
        RS = FA[:, 8192:10240]
        DTS = [FA[:, 10240 + i * 1024:10240 + (i + 1) * 1024] for i in range(2)]

        a_ld = [pg.ev(f"ald{i}") for i in range(3)]
        a_zld = [pg.ev(f"azld{i}") for i in range(2)]
        a_dld = [pg.ev(f"adld{i}") for i in range(2)]
        a_dcast = pg.ev("adcast")
        a_ln = pg.ev("aln")
        a_s = [pg.ev(f"as{i}") for i in range(2)]
        a_e = [pg.ev(f"ae{i}") for i in range(4)]
        a_p = [pg.ev(f"ap{i}") for i in range(4)]
        a_pv = [pg.ev(f"apv{i}") for i in range(4)]
        a_evac = [pg.ev(f"aevac{i}") for i in range(2)]
        a_fin = pg.ev("afin")
        a_gst = [pg.ev(f"agst{i}") for i in range(2)]
        ac = dict(unit=0, batch=0, pj=0, dst=0)
        unit_end = {}
        fin_hist = []
        exp_n = {}

        def phase_A(l):
            ltype = layer_type(l)
            t = LT[ltype]
            groups = t["groups"]
            PW = 512 if ltype == "A" else 384
            dtab = dA_d if ltype == "A" else dB_d
            ng = len(groups)
            zty = fm_types(ltype) - 1
            KEEP = 2
            units = []
            for pair in range(8):
                for J in range(NSB):
                    for g, (d, ns) in enumerate(groups):
                        parts = 2 if d == 16 else 1
                        for part in range(parts):
                            nr = d // parts
                            units.append(dict(pair=pair, J=J, g=g, d=d, ns=ns, r0=part * nr, nr=nr,
                                              first=(g == 0 and part == 0), lastu=(g == ng - 1 and part == parts - 1)))
            ubase = ac["unit"]

            def emit_loads(ui):
                u = units[ui]
                ug = ubase + ui
                slot = ug % 3
                d, ns, J, pair, g = u["d"], u["ns"], u["J"], u["pair"], u["g"]
                Lr = TOK // d
                n = SB // d
                P0 = J * n
                KW = n + 256
                klo, khi = max(0, P0 - 128), min(Lr, P0 + n + 128)
                ntw = n // 128 + 2
                mlo, mhi = max(0, P0 // 128 - 1), min(Lr // 128 - 1, (P0 + n) // 128)
                if ltype == "A":
                    kty, qty = 2 * g + 1, 2 * g
                else:
                    kty, qty = 1, 0
                r0, nr = u["r0"], u["nr"]
                if (ug - 3) in unit_end:
                    pg.wait("sp", a_evac[0], unit_end[ug - 3][0])
                    pg.wait("sp", a_evac[1], unit_end[ug - 3][1])
                else:
                    assert ug - 3 < ubase or ug < 3, (ug, ubase)
                pg.dma(lambda e: e.dma_start(
                    out=KT[slot][:, 0:nr * KW].rearrange("p (r w) -> p r w", r=nr)[:, :, klo - (P0 - 128):khi - (P0 - 128)],
                    in_=fm_d[kty, pair].rearrange("p (r u) -> p r u", r=d)[:, r0:r0 + nr, klo:khi]), a_ld[slot])
                pg.dma(lambda e: e.dma_start(
                    out=QT[slot][:, 0:nr * n].rearrange("p (r w) -> p r w", r=nr),
                    in_=fm_d[qty, pair].rearrange("p (r u) -> p r u", r=d)[:, r0:r0 + nr, P0:P0 + n]), a_ld[slot])
                t0 = mlo - (P0 // 128 - 1)
                t1 = mhi + 1 - (P0 // 128 - 1)
                pg.dma(lambda e: e.dma_start(
                    out=VT[slot][:, 0:nr * ntw, :].rearrange("p (r t) c -> p r t c", r=nr)[:, :, t0:t1, :],
                    in_=Vs_d[g, pair].rearrange("k (r t) c -> k r t c", r=d)[:, r0:r0 + nr, mlo:mhi + 1, :]), a_ld[slot])
                u["slot"] = slot
                u["ldn"] = a_ld[slot].n

            pending = []

            def flush(keep):
                while len(pending) > keep:
                    pending.pop(0)[1]()

            cur_zs = 0
            emit_loads(0)
            loaded = 0
            for ui, u in enumerate(units):
                if loaded == ui and ui + 1 < len(units):
                    assert all(tag >= ui - 1 for tag, _ in pending)
                    emit_loads(ui + 1)
                    loaded = ui + 1
                slot = u["slot"]
                d, ns, J, pair, g = u["d"], u["ns"], u["J"], u["pair"], u["g"]
                Lr = TOK // d
                n = SB // d
                P0 = J * n
                KW = n + 256
                ntw = n // 128 + 2
                mid = Lr // 2
                r0, nr = u["r0"], u["nr"]
                ug = ubase + ui
                if u["first"]:
                    pj = ac["pj"]
                    ac["pj"] += 1
                    cur_zs = pj % 2
                    zs = cur_zs
                    fin_need = fin_hist[-2] if len(fin_hist) >= 2 else 0
                    if len(fin_hist) >= 2:
                        pg.wait("sp", a_fin, fin_hist[-2])
                    pg.dma(lambda e, zs=zs, pair=pair, J=J: e.dma_start(
                        out=ZT[zs], in_=fm_d[zty, pair][:, J * SB:(J + 1) * SB]), a_zld[zs])
                    if J == 0:
                        for i_ in range(4):
                            pg.wait("pool", a_p[i_], a_p[i_].n)
                        for gg in range(ng):
                            for vv in range(3):
                                di = ac["dst"]
                                ac["dst"] += 1
                                hs = di % 2
                                pg.wait("sp", a_dcast, max(0, di - 1))
                                pg.dma(lambda e, gg=gg, vv=vv, pair=pair, hs=hs: e.dma_start(
                                    out=DTS[hs][:, 0:2 * PW].rearrange("p (h w) -> p h w", h=2),
                                    in_=dtab[pair, gg, vv]), a_dld[hs])
                                pg.wait("pool", a_dld[hs], a_dld[hs].n)
                                pg.op("pool", lambda e, gg=gg, vv=vv, hs=hs: e.tensor_copy(
                                    out=DTB[:, (gg * 3 + vv) * 2 * PW:(gg * 3 + vv + 1) * 2 * PW], in_=DTS[hs][:, 0:2 * PW]), a_dcast)
                                assert a_dcast.n == di + 1
                packs = []
                if ltype == "B":
                    qsz = 128
                    for qt_i in range(n // qsz):
                        q0 = P0 + qt_i * qsz
                        segs = []
                        for j in range(3):
                            m = q0 // 128 - 1 + j
                            segs.append(dict(ri=0, m=m, col=128 * j, N=128, qlo=q0, ocol=0))
                        packs.append(dict(q0=q0, segs=segs, nruns=1, rr=r0, ocols=qsz))
                elif d < 16:
                    qsz = 256
                    offs, Ns, qoff = [0, 64, 256, 448], [64, 192, 192, 64], [0, 0, 64, 192]
                    for ri in range(nr):
                        for qt_i in range(n // qsz):
                            q0 = P0 + qt_i * qsz
                            segs = []
                            for j in range(4):
                                m = q0 // 128 - 1 + j
                                segs.append(dict(ri=ri, m=m, col=offs[j], N=Ns[j], qlo=q0 + qoff[j], ocol=qoff[j]))
                            packs.append(dict(q0=q0, segs=segs, nruns=1, rr=r0 + ri, ocols=qsz))
                else:
                    qsz = 128
                    offs, Ns, qoff = [0, 64, 192], [64, 128, 64], [0, 0, 64]
                    q0 = P0
                    for rp in range(nr // 2):
                        segs = []
                        for a_ in range(2):
                            ri = 2 * rp + a_
                            for j in range(3):
                                m = q0 // 128 - 1 + j
                                segs.append(dict(ri=ri, m=m, col=256 * a_ + offs[j], N=Ns[j], qlo=q0 + qoff[j], ocol=128 * a_ + qoff[j]))
                        packs.append(dict(q0=q0, segs=segs, nruns=2, rr=r0 + 2 * rp, ocols=256))
                for pi, pk in enumerate(packs):
                    q0 = pk["q0"]
                    segs = [sg for sg in pk["segs"] if 0 <= sg["m"] < Lr // 128]
                    var = 0
                    if q0 == mid - qsz:
                        var = 1
                    elif q0 == mid:
                        var = 2
                    bi = ac["batch"]
                    ac["batch"] += 1
                    bs = bi % 4
                    ss = bi % 2
                    oset = bi % 2
                    pg.wait("pe", a_ld[slot], u["ldn"])
                    if (bi - 2) in exp_n:
                        pg.wait("pe", a_e[(bi - 2) % 4], exp_n[bi - 2])
                    for si_, sg in enumerate(segs):
                        kcol = sg["ri"] * KW + (128 * sg["m"] - (P0 - 128))
                        qcol = sg["ri"] * n + (sg["qlo"] - P0)
                        for h in range(2):
                            pg.op("pe", lambda e, ss=ss, h=h, slot=slot, kcol=kcol, qcol=qcol, sg=sg: e.matmul(
                                PS[:, 2 * ss + h, sg["col"]:sg["col"] + sg["N"]],
                                lhsT=KT[slot][h * 64:(h + 1) * 64, kcol:kcol + 128],
                                rhs=QT[slot][h * 64:(h + 1) * 64, qcol:qcol + sg["N"]], start=True, stop=True),
                                a_s[ss] if (h == 1 and si_ == len(segs) - 1) else None)
                    pg.wait("act", a_s[ss], a_s[ss].n)
                    pg.wait("act", a_pv[bs], a_pv[bs].n)
                    exp_n[bi] = pg.op("act", lambda e, bs=bs, ss=ss: e.activation(
                        out=EP[bs][:, :, 0:PW], in_=PS[:, 2 * ss:2 * ss + 2, 0:PW], func=AF.Exp, scale=0.125), a_e[bs])
                    pg.wait("dve", a_e[bs], a_e[bs].n)
                    pg.wait("dve", a_dcast, a_dcast.n)
                    dview = DTB[:, (g * 3 + var) * 2 * PW:(g * 3 + var + 1) * 2 * PW].rearrange("p (h w) -> p h w", h=2)
                    pg.op("dve", lambda e, bs=bs, dview=dview: e.tensor_tensor(
                        out=EP[bs][:, :, 0:PW], in0=EP[bs][:, :, 0:PW], in1=dview, op=ALU.mult), a_p[bs])
                    p_need = a_p[bs].n
                    ufirst = u["first"]
                    last_of_unit = (pi == len(packs) - 1)
                    tok0 = q0 - P0

                    ACC = ACC2[cur_zs]

                    def pv(bs=bs, oset=oset, slot=slot, segs=segs, pk=pk, p_need=p_need, ufirst=ufirst,
                           last_of_unit=last_of_unit, ug=ug, d=d, ntw=ntw, P0=P0, tok0=tok0, qsz=qsz, ACC=ACC, fin_need=fin_need):
                        pg.wait("pe", a_p[bs], p_need)
                        pg.wait("pe", a_evac[oset], a_evac[oset].n)
                        for h in range(2):
                            for si_, sg in enumerate(segs):
                                vtile = sg["ri"] * ntw + (sg["m"] - (P0 // 128 - 1))
                                pg.op("pe", lambda e, h=h, sg=sg, vtile=vtile, si_=si_: e.matmul(
                                    PS[:, 4 + 2 * oset + h, sg["ocol"]:sg["ocol"] + sg["N"]],
                                    lhsT=VT[slot][:, vtile, h * 128:(h + 1) * 128],
                                    rhs=EP[bs][:, h, sg["col"]:sg["col"] + sg["N"]],
                                    start=(si_ == 0), stop=(si_ == len(segs) - 1), skip_group_check=True),
                                    a_pv[bs] if (h == 1 and si_ == len(segs) - 1) else None)
                        pg.wait("dve", a_pv[bs], a_pv[bs].n)
                        oc = pk["ocols"]
                        osrc = PS[:, 4 + 2 * oset:6 + 2 * oset, 0:oc]
                        if d == 1:
                            accv = ACC[:, :, tok0:tok0 + oc]
                        elif pk["nruns"] == 1:
                            accv = ACC.rearrange("p h (u r) -> p h r u", r=d)[:, :, pk["rr"], tok0:tok0 + oc]
                        else:
                            accv = ACC.rearrange("p h (u r) -> p h r u", r=d)[:, :, pk["rr"]:pk["rr"] + 2, tok0:tok0 + qsz]
                            osrc = osrc.rearrange("p h (a u) -> p h a u", a=2)
                        if ufirst:
                            pg.wait("dve", a_fin, fin_need)
                            pg.op("dve", lambda e: e.tensor_copy(out=accv, in_=osrc), a_evac[oset])
                        else:
                            pg.op("dve", lambda e: e.tensor_tensor(out=accv, in0=osrc, in1=accv, op=ALU.add), a_evac[oset])
                        if last_of_unit:
                            unit_end[ug] = (a_evac[0].n, a_evac[1].n)
                    pending.append((ui, pv))
                    flush(KEEP)
                    if loaded == ui and ui + 1 < len(units) and all(tag != ui - 1 for tag, _ in pending):
                        emit_loads(ui + 1)
                        loaded = ui + 1
                if u["lastu"]:
                    flush(0)
                if loaded == ui and ui + 1 < len(units):
                    flush(0)
                    emit_loads(ui + 1)
                    loaded = ui + 1
                if u["lastu"]:
                    zs = cur_zs
                    ACC = ACC2[cur_zs]
                    a_den, b_den = ACC[64:128, 0, :], ACC[0:64, 1, :]
                    a_num, b_num = ACC[0:64, 0, :], ACC[64:128, 1, :]
                    fin_before = a_fin.n
                    if ltype == "B":
                        pg.wait("dve", a_evac[0], a_evac[0].n)
                        pg.wait("dve", a_evac[1], a_evac[1].n)
                        pg.wait("dve", pro_ev, pro_ev.n)
                        li = l // 2
                        ca, cb = li * 16 + 2 * pair, li * 16 + 2 * pair + 1
                        pg.op("dve", lambda e, ca=ca, a_den=a_den: e.tensor_scalar(out=a_den, in0=a_den,
                                                                      scalar1=esink[64:128, ca:ca + 1], scalar2=None, op0=ALU.add), a_evac[0])
                        pg.op("dve", lambda e, cb=cb, b_den=b_den: e.tensor_scalar(out=b_den, in0=b_den,
                                                                      scalar1=esink[0:64, cb:cb + 1], scalar2=None, op0=ALU.add), a_evac[1])
                    pg.wait("act", a_evac[0], a_evac[0].n)
                    pg.wait("act", a_evac[1], a_evac[1].n)
                    pg.wait("act", a_fin, fin_before)
                    pg.op("act", lambda e, a_den=a_den: e.activation(out=RS[0:64, :], in_=a_den, func=AF.Ln), a_ln)
                    n_ = pg.op("act", lambda e, b_den=b_den: e.activation(out=RS[64:128, :], in_=b_den, func=AF.Ln), a_ln)
                    pg.wait("act", a_ln, n_)
                    pg.op("act", lambda e: e.activation(out=RS, in_=RS, func=AF.Exp, scale=-1.0), a_ln)
                    pg.wait("pool", a_ln, a_ln.n)
                    pg.wait("pool", a_zld[zs], a_zld[zs].n)
                    pg.wait("pool", a_gst[zs], a_gst[zs].n)
                    n_ = pg.op("pool", lambda e, a_num=a_num: e.tensor_tensor(out=RS[0:64, :], in0=a_num, in1=RS[0:64, :], op=ALU.mult), a_fin)
                    n_ = pg.op("pool", lambda e, b_num=b_num: e.tensor_tensor(out=RS[64:128, :], in0=b_num, in1=RS[64:128, :], op=ALU.mult), a_fin)
                    pg.wait("pool", a_fin, n_)
                    pg.op("pool", lambda e, zs=zs: e.tensor_tensor(out=GT[zs], in0=RS, in1=ZT[zs], op=ALU.mult), a_fin)
                    fin_hist.append(a_fin.n)
                    pg.wait("sp", a_fin, a_fin.n)
                    pg.dma(lambda e, zs=zs, pair=pair, J=J: e.dma_start(out=gs_d[pair, :, J * SB:(J + 1) * SB], in_=GT[zs]), a_gst[zs])
            ac["unit"] += len(units)
            for e_ in a_gst:
                pg.wait("sp", e_, e_.n)

        phase_M(-1)
        phase_O(-1)
        for l in range(nlayers):
            phase_P(l)
            phase_A(l)
            phase_M(l)
            phase_O(l)
        pg.wait("sp", o_xst[0], o_xst[0].n)
        pg.wait("sp", o_xst[1], o_xst[1].n)
        pg.wait("sp", o_xst[2], o_xst[2].n)

        with nc.Block() as block:
            @block.sync
            def _(e):
                pg.replay("sp", e)

            @block.tensor
            def _(e):
                pg.replay("pe", e)

            @block.scalar
            def _(e):
                pg.replay("act", e)

            @block.vector
            def _(e):
                pg.replay("dve", e)

            @block.gpsimd
            def _(e):
                pg.replay("pool", e)
    return nc


def _group_layout(wcat):
    nf = wcat.shape[1]
    assert nf % 512 == 0
    return np.ascontiguousarray(wcat.reshape(8, 128, nf // 512, 512).transpose(2, 1, 0, 3))


def _layer_weights(l, w_in_a, w_in_b):
    if l % 2 == 0:
        w = w_in_a[l // 2]
        cols = []
        for g in range(3):
            cols.append(w[:, g * 3072:g * 3072 + 1024])
            cols.append(w[:, g * 3072 + 1024:g * 3072 + 2048])
        cols.append(w[:, 9216:10240])
        for g in range(3):
            cols.append(w[:, g * 3072 + 2048:g * 3072 + 3072])
        return _group_layout(np.concatenate(cols, axis=1))
    w = w_in_b[l // 2]
    q = w[:, 0:1024]
    k = w[:, 1024:1280].reshape(1024, 4, 64)
    v = w[:, 1280:1536].reshape(1024, 4, 64)
    z = w[:, 1536:2560]
    kk = np.repeat(k, 4, axis=1).reshape(1024, 1024)
    vv = np.repeat(v, 4, axis=1).reshape(1024, 1024)
    return _group_layout(np.concatenate([q, kk, z, vv], axis=1))


def _dtable(groups, W, join, ltype):
    ng = len(groups)
    PW = 512 if ltype == "A" else 384
    out = np.zeros((8, ng, 3, 128, 2, PW), np.float32)
    slopes = 2.0 ** (-8.0 * np.arange(1, 17) / 16.0)
    k = np.arange(128)[:, None]
    j = np.arange(W)[None, :]
    for gi, (d, ns) in enumerate(groups):
        rel = k + ns - j
        valid = (np.abs(rel) <= ns)
        if ltype == "B":
            sl_ = [(256, 384), (128, 256), (0, 128)]
            cross_last, cross_first = (256, 384), (0, 128)
            reps = 1
        elif d < 16:
            sl_ = [(192, 256), (64, 256), (0, 192), (0, 64)]
            cross_last, cross_first = (448, 512), (0, 64)
            reps = 1
        else:
            sl_ = [(192, 256), (64, 192), (0, 64)]
            cross_last, cross_first = (192, 256), (0, 64)
            reps = 2
        for pair in range(8):
            for h in range(2):
                sl = slopes[2 * pair + h]
                base = np.where(valid, np.exp(-sl * d * np.abs(rel).astype(np.float64)), 0.0).astype(np.float32)
                one = np.concatenate([base[:, a:b] for a, b in sl_], axis=1)
                v1 = one.copy()
                v2 = one.copy()
                if not join:
                    v1[:, cross_last[0]:cross_last[1]] = 0.0
                    v2[:, cross_first[0]:cross_first[1]] = 0.0
                for vi, tb in enumerate((one, v1, v2)):
                    out[pair, gi, vi, :, h, :] = np.concatenate([tb] * reps, axis=1)
    return out


_CACHE = {}


def _get_program(nlayers):
    if nlayers not in _CACHE:
        _CACHE[nlayers] = build_program(nlayers)
    return _CACHE[nlayers]


def make_in_maps(x_prompt, x_sample, c_prompt, c_sample, w_mod, b_mod, ln_g, ln_b,
                 w_in_a, w_out_a, w_in_b, w_out_b, sink_b, nlayers=DEPTH):
    f = lambda a: np.ascontiguousarray(np.asarray(a, dtype=np.float32))
    x_prompt, x_sample, c_prompt, c_sample = f(x_prompt), f(x_sample), f(c_prompt), f(c_sample)
    w_mod, b_mod, ln_g, ln_b = f(w_mod), f(b_mod), f(ln_g), f(ln_b)
    w_in_a, w_out_a, w_in_b, w_out_b, sink_b = f(w_in_a), f(w_out_a), f(w_in_b), f(w_out_b), f(sink_b)
    shared = {}
    for l in range(nlayers):
        shared[f"w{l}"] = _layer_weights(l, w_in_a, w_in_b)
    wo = np.stack([(w_out_a if l % 2 == 0 else w_out_b)[l // 2] for l in range(DEPTH)])
    shared["wo"] = np.ascontiguousarray(wo.reshape(DEPTH, 8, 128, D).transpose(0, 2, 1, 3))
    shared["wm"] = np.ascontiguousarray(w_mod.reshape(DEPTH, 8, 128, 6, 512).transpose(0, 3, 2, 1, 4))
    shared["bmod"] = b_mod
    shared["lng"] = ln_g
    shared["lnb"] = ln_b
    shared["sink"] = sink_b
    shared["ident"] = np.eye(128, dtype=np.float32)
    dA = {j: _dtable(LT["A"]["groups"], 256, j, "A") for j in (False, True)}
    dB = {j: _dtable(LT["B"]["groups"], 384, j, "B") for j in (False, True)}
    in_maps = []
    for i in range(NCORES):
        m = dict(shared)
        if i < 4:
            m["x"] = x_prompt[i]
            m["cvec"] = np.ascontiguousarray(np.stack([c_prompt[i], c_prompt[i]]))
            m["dA"], m["dB"] = dA[True], dB[True]
        else:
            j = 2 * (i - 4)
            m["x"] = np.ascontiguousarray(x_sample[j:j + 2].reshape(TOK, D))
            m["cvec"] = np.ascontiguousarray(c_sample[j:j + 2])
            m["dA"], m["dB"] = dA[False], dB[False]
        in_maps.append(m)
    return in_maps


def kernel(x_prompt, x_sample, c_prompt, c_sample, w_mod, b_mod, ln_g, ln_b,
           w_in_a, w_out_a, w_in_b, w_out_b, sink_b):
    in_maps = make_in_maps(x_prompt, x_sample, c_prompt, c_sample, w_mod, b_mod, ln_g, ln_b,
                           w_in_a, w_out_a, w_in_b, w_out_b, sink_b)
    nc = _get_program(DEPTH)
    res = run_bass_kernel_spmd(nc, in_maps, core_ids=list(range(NCORES)))
    ys = [np.asarray(r["y"], dtype=np.float32) for r in res.results]
    y_prompt = np.stack(ys[0:4]).reshape(4, TOK, D)
    y_sample = np.concatenate([y.reshape(2, TOK // 2, D) for y in ys[4:8]], axis=0)
    return (y_prompt, y_sample)
```

```python
import contextlib
import numpy as np
import concourse.bass as bass
import concourse.mybir as mybir
from concourse.bass_utils import run_bass_kernel_spmd

F32 = mybir.dt.float32
BF16 = mybir.dt.bfloat16
ALU = mybir.AluOpType
AF = mybir.ActivationFunctionType

NCORES = 8
D = 1024
TOK = 8192
SB = 2048
NSB = TOK // SB
DEPTH = 4
ALPHA = (2.0 * DEPTH) ** 0.25
LN_EPS = 1e-5
A_DILS = (1, 4, 16)

LT = {
    "A": dict(groups=[(1, 64), (4, 64), (16, 64)], nfm=7, ngrp_fm=14, ngrp_v=6, W=256),
    "B": dict(groups=[(1, 128)], nfm=3, ngrp_fm=6, ngrp_v=2, W=384),
}
def fm_types(t):
    return 7 if t == "A" else 3


def layer_type(l):
    return "A" if l % 2 == 0 else "B"


class Ev:
    def __init__(self, nc, stack, name):
        self.h = stack.enter_context(nc.semaphore(name))
        self.n = 0


class Prog:
    def __init__(self, nc, stack):
        self.nc = nc
        self.stack = stack
        self.q = {k: [] for k in ("pe", "act", "dve", "pool", "sp")}
        self.nev = 0

    def ev(self, name):
        self.nev += 1
        return Ev(self.nc, self.stack, f"{name}_{self.nev}")

    def op(self, eng, fn, ev=None, amt=1):
        self.q[eng].append((fn, ev, amt))
        if ev is not None:
            ev.n += amt
            return ev.n
        return None

    def dma(self, fn, ev):
        return self.op("sp", fn, ev, 16)

    def wait(self, eng, ev, val):
        if val <= 0:
            return
        self.q[eng].append(("wait", ev, val))

    def replay(self, eng_name, eng):
        last_wait = {}
        for item in self.q[eng_name]:
            if item[0] == "wait":
                _, ev, val = item
                if last_wait.get(id(ev), 0) >= val:
                    continue
                last_wait[id(ev)] = val
                eng.wait_ge(ev.h, val)
            else:
                fn, ev, amt = item
                ins = fn(eng)
                if ev is not None:
                    ins.then_inc(ev.h, amt)


def build_program(nlayers=DEPTH):
    nc = bass.Bass("TRN2", target_bir_lowering=False)
    dt = nc.dram_tensor
    x_in = dt("x", [TOK, D], F32, kind="ExternalInput").ap()
    cvec = dt("cvec", [2, D], F32, kind="ExternalInput").ap()
    ident_d = dt("ident", [128, 128], F32, kind="ExternalInput").ap()
    w_d = []
    for l in range(nlayers):
        t = LT[layer_type(l)]
        w_d.append(dt(f"w{l}", [t["ngrp_fm"] + t["ngrp_v"], 128, 8, 512], F32, kind="ExternalInput").ap())
    wo_d = dt("wo", [DEPTH, 128, 8, D], F32, kind="ExternalInput").ap()
    wm_d = dt("wm", [DEPTH, 6, 128, 8, 512], F32, kind="ExternalInput").ap()
    bmod_d = dt("bmod", [DEPTH, 3 * D], F32, kind="ExternalInput").ap()
    lng_d = dt("lng", [DEPTH, D], F32, kind="ExternalInput").ap()
    lnb_d = dt("lnb", [DEPTH, D], F32, kind="ExternalInput").ap()
    sink_d = dt("sink", [2, 16], F32, kind="ExternalInput").ap()
    dA_d = dt("dA", [8, 3, 3, 128, 2, 512], F32, kind="ExternalInput").ap()
    dB_d = dt("dB", [8, 1, 3, 128, 2, 384], F32, kind="ExternalInput").ap()
    y_out = dt("y", [TOK, D], F32, kind="ExternalOutput").ap()
    xs = [dt(f"xs{i}", [TOK, D], F32).ap() for i in range(2)]
    uT_d = dt("uT", [NSB, 128, 8, SB], BF16).ap()
    fm_d = dt("fm", [7, 8, 128, TOK], BF16).ap()
    Vs_d = dt("Vs", [3, 8, 128, 64, 256], BF16).ap()
    gs_d = dt("gs", [8, 128, TOK], BF16).ap()

    stack = contextlib.ExitStack()
    with stack:
        sb = lambda name, shape, dtp: stack.enter_context(nc.sbuf_tensor(name, shape, dtp))
        BFA = sb("bfa", [128, 44032], BF16)
        FA = sb("fa", [128, 12288], F32)
        MOD = sb("modt", [128, 8, D], F32)
        screp = sb("screp", [128, 2, 8, 128], F32)
        ident = sb("identt", [128, 128], F32)
        small = sb("smallt", [128, 64], F32)
        esink = sb("esink", [128, 32], F32)
        csb = sb("csb", [128, 16], F32)
        biasb = sb("biasb", [128, 2, 512], F32)
        PS = stack.enter_context(nc.psum_tensor("ps", [128, 8, 512], F32))
        pg = Prog(nc, stack)

        GATE = [MOD[:, 0, :], MOD[:, 1, :]]
        SC1 = [MOD[:, 2, :], MOD[:, 3, :]]
        SH = [MOD[:, 4, :], MOD[:, 5, :]]
        LNG = MOD[:, 6, :]
        LNB = MOD[:, 7, :]

        misc_ld = pg.ev("miscld")

        pg.dma(lambda e: e.dma_start(out=ident[:], in_=ident_d[:, :]), misc_ld)
        pg.dma(lambda e: e.dma_start(out=csb[:, :].rearrange("p (c k) -> p c k", c=2),
                                     in_=cvec.rearrange("c (k p) -> p c k", p=128),
                                     allow_slow_non_contiguous=True), misc_ld)
        pg.dma(lambda e: e.dma_start(out=esink[:, :], in_=sink_d.rearrange("a h -> (a h)").partition_broadcast(128)), misc_ld)
        pro_ev = pg.ev("pro")
        pg.wait("act", misc_ld, misc_ld.n)
        pg.op("act", lambda e: e.activation(out=csb[:, :], in_=csb[:, :], func=AF.Silu), pro_ev)
        pg.op("act", lambda e: e.activation(out=esink[:, :], in_=esink[:, :], func=AF.Exp), pro_ev)
        pg.op("pool", lambda e: e.memset(screp[:], 1.0), pro_ev)
        pg.wait("dve", pro_ev, pro_ev.n)
        pro2 = pg.ev("pro2")
        for c in range(2):
            for k in range(8):
                pg.op("dve", lambda e, c=c, k=k: e.tensor_scalar(
                    out=screp[:, c, k, :], in0=screp[:, c, k, :], scalar1=csb[:, c * 8 + k:c * 8 + k + 1],
                    scalar2=None, op0=ALU.mult), pro2)

        wst_views = [FA[:, 0:4096].rearrange("p (k c) -> p k c", k=8), FA[:, 4096:8192].rearrange("p (k c) -> p k c", k=8)]
        m_ld = pg.ev("mld")
        m_pe = pg.ev("mpe")
        m_dve = pg.ev("mdve")

        def phase_M(l):
            jobs = []
            if l >= 0:
                jobs += [(l, 4, GATE, 0, 0.0), (l, 5, GATE, 1, 0.0)]
            if l + 1 < nlayers:
                jobs += [(l + 1, 0, SH, 0, 0.0), (l + 1, 1, SH, 1, 0.0), (l + 1, 2, SC1, 0, 1.0), (l + 1, 3, SC1, 1, 1.0)]
            if l >= 0:
                pg.wait("sp", m_dve, m_dve.n)
                pg.dma(lambda e: e.dma_start(out=LNG, in_=lng_d[l, :].partition_broadcast(128)), m_ld)
                pg.dma(lambda e: e.dma_start(out=LNB, in_=lnb_d[l, :].partition_broadcast(128)), m_ld)
            for ji, (ll, cg, tgt, half, add1) in enumerate(jobs):
                slot = ji % 2
                pg.wait("sp", m_pe, m_pe.n)
                pg.wait("sp", m_dve, m_dve.n)
                pg.dma(lambda e, ll=ll, cg=cg, slot=slot: e.dma_start(out=wst_views[slot], in_=wm_d[ll, cg]), m_ld)
                pg.dma(lambda e, ll=ll, cg=cg, slot=slot: e.dma_start(
                    out=biasb[:, slot, :], in_=bmod_d[ll, cg * 512:(cg + 1) * 512].partition_broadcast(128)), m_ld)
                pg.wait("pe", m_ld, m_ld.n)
                pg.wait("pe", pro2, pro2.n)
                pg.wait("pe", m_dve, m_dve.n)
                for c in range(2):
                    for k in range(8):
                        pg.op("pe", lambda e, c=c, k=k, slot=slot: e.matmul(
                            PS[:, c, :], lhsT=screp[:, c, k, :], rhs=wst_views[slot][:, k, :],
                            start=(k == 0), stop=(k == 7)), m_pe if (c == 1 and k == 7) else None)
                pg.wait("dve", m_pe, m_pe.n)
                for c in range(2):
                    pg.op("dve", lambda e, c=c, tgt=tgt, half=half, add1=add1, slot=slot: e.scalar_tensor_tensor(
                        out=tgt[c][:, half * 512:(half + 1) * 512], in0=PS[:, c, :], scalar=add1,
                        in1=biasb[:, slot, :], op0=ALU.add, op1=ALU.add), m_dve)
            return

        gT = [BFA[:, i * 4096:(i + 1) * 4096].rearrange("p (k t) -> p k t", k=8) for i in range(2)]
        wo_bf = BFA[:, 8192:16384].rearrange("p (k c) -> p k c", k=8)
        uTst = [BFA[:, 16384 + i * 4096:16384 + (i + 1) * 4096].rearrange("p (k t) -> p k t", k=8) for i in range(2)]
        xt = [FA[:, i * 1024:(i + 1) * 1024] for i in range(4)]
        ht = [FA[:, 4096 + i * 1024:4096 + (i + 1) * 1024] for i in range(2)]
        ut = [FA[:, 6144 + i * 1024:6144 + (i + 1) * 1024] for i in range(2)]
        junk = BFA[:, 24576:25600]
        wo_st = FA[:, 8192:12288].rearrange("p (k c) -> p k c", k=8)

        o_gld = [pg.ev(f"ogld{i}") for i in range(2)]
        o_xld = [pg.ev(f"oxld{i}") for i in range(4)]
        o_wld = pg.ev("owld")
        o_wcast = pg.ev("owcast")
        o_y = [pg.ev(f"oy{i}") for i in range(2)]
        o_h = [pg.ev(f"oh{i}") for i in range(2)]
        o_h2 = [pg.ev(f"oh2{i}") for i in range(2)]
        o_sq = [pg.ev(f"osq{i}") for i in range(2)]
        o_st = [pg.ev(f"ost{i}") for i in range(2)]
        o_xn = [pg.ev(f"oxn{i}") for i in range(2)]
        o_xo = [pg.ev(f"oxo{i}") for i in range(4)]
        o_xst = [pg.ev(f"oxst{i}") for i in range(4)]
        o_u1 = [pg.ev(f"ou1{i}") for i in range(2)]
        o_u2 = [pg.ev(f"ou2{i}") for i in range(2)]
        o_tp = [pg.ev(f"otp{i}") for i in range(2)]
        o_te = [pg.ev(f"ote{i}") for i in range(2)]
        o_ust = [pg.ev(f"oust{i}") for i in range(2)]
        cnt = dict(sub=0, tg=0)
        xt_free = [[], [], [], []]
        ht_free = [None, None]

        def phase_O(l):
            full = l >= 0
            last = (l == nlayers - 1)
            src = x_in if l <= 0 else xs[(l - 1) % 2]
            dst = y_out if last else xs[l % 2]
            mod_ready_dve = m_dve.n
            mod_ready_ld = m_ld.n
            pg.wait("sp", m_pe, m_pe.n)
            pg.wait("sp", m_dve, m_dve.n)
            if full:
                for hlf in range(2):
                    pg.wait("sp", o_wcast, o_wcast.n)
                    pg.dma(lambda e, hlf=hlf: e.dma_start(out=wo_st, in_=wo_d[l, :, :, hlf * 512:(hlf + 1) * 512]), o_wld)
                    pg.wait("pool", o_wld, o_wld.n)
                    pg.op("pool", lambda e, hlf=hlf: e.tensor_copy(out=wo_bf[:, :, hlf * 512:(hlf + 1) * 512], in_=wo_st), o_wcast)
            base_sub = cnt["sub"]
            base_tg = cnt["tg"]
            NSUB = TOK // 128

            def emit_xload(i):
                si = base_sub + i
                xs_ = si % 4
                for (ev_, n_) in xt_free[xs_]:
                    pg.wait("sp", ev_, n_)
                xt_free[xs_] = []
                row0 = i * 128
                pg.dma(lambda e: e.dma_start(out=xt[xs_], in_=src[row0:row0 + 128, :]), o_xld[xs_])
                return o_xld[xs_].n

            def emit_gload(tg):
                gslot = (base_tg + tg) % 2
                pg.wait("sp", o_y[0], o_y[0].n)
                pg.wait("sp", o_y[1], o_y[1].n)
                pg.dma(lambda e: e.dma_start(
                    out=gT[gslot], in_=gs_d[:, :, tg * 512:(tg + 1) * 512].rearrange("k p t -> p k t")), o_gld[gslot])
                return o_gld[gslot].n

            pend_u = []
            xld_n = {0: emit_xload(0)}
            gld_n = {}
            if full:
                gld_n[0] = emit_gload(0)
            for tg in range(TOK // 512):
                c = tg // 8
                gi = base_tg + tg
                gslot = gi % 2
                uslot = gi % 2
                if full and tg + 1 < TOK // 512:
                    gld_n[tg + 1] = emit_gload(tg + 1)
                for s in range(4):
                    i = tg * 4 + s
                    si = base_sub + i
                    xs_ = si % 4
                    p2 = si % 2
                    row0 = i * 128
                    if i + 1 < NSUB:
                        xld_n[i + 1] = emit_xload(i + 1)
                    b = 8 * p2
                    if full:
                        pg.wait("pe", o_gld[gslot], gld_n[tg])
                        pg.wait("pe", o_wcast, o_wcast.n)
                        pg.wait("pe", o_h[p2], o_h[p2].n)
                        for hlf in range(2):
                            for k in range(8):
                                pg.op("pe", lambda e, hlf=hlf, k=k, p2=p2, gslot=gslot, s=s: e.matmul(
                                    PS[:, 2 * p2 + hlf, :], lhsT=gT[gslot][:, k, s * 128:(s + 1) * 128],
                                    rhs=wo_bf[:, k, hlf * 512:(hlf + 1) * 512], start=(k == 0), stop=(k == 7)),
                                    o_y[p2] if (hlf == 1 and k == 7) else None)
                        Y = PS[:, 2 * p2:2 * p2 + 2, :].rearrange("p a b -> p (a b)")
                        pg.wait("dve", o_y[p2], o_y[p2].n)
                        pg.wait("dve", m_dve, mod_ready_dve)
                        pg.op("dve", lambda e, b=b: e.memset(small[:, b:b + 2], 0.0))
                        pg.op("dve", lambda e, p2=p2, Y=Y, c=c: e.tensor_tensor(out=ht[p2], in0=Y, in1=GATE[c], op=ALU.mult), o_h[p2])
                        pg.wait("dve", o_xld[xs_], xld_n[i])
                        pg.wait("dve", o_h[p2], o_h[p2].n)
                        pg.op("dve", lambda e, p2=p2, xs_=xs_, b=b: e.scalar_tensor_tensor(
                            out=ht[p2], in0=xt[xs_], scalar=ALPHA, in1=ht[p2], op0=ALU.mult, op1=ALU.add,
                            accum_out=small[:, b:b + 1]), o_h2[p2])
                        pg.wait("act", o_h2[p2], o_h2[p2].n)
                        pg.op("act", lambda e, p2=p2, b=b: e.activation(out=junk, in_=ht[p2], func=AF.Square,
                                                                        accum_out=small[:, b + 1:b + 2]), o_sq[p2])
                        pg.wait("dve", o_sq[p2], o_sq[p2].n)
                        ops = [
                            lambda e, b=b: e.tensor_scalar(out=small[:, b + 2:b + 3], in0=small[:, b:b + 1], scalar1=-1.0 / D, scalar2=None, op0=ALU.mult),
                            lambda e, b=b: e.tensor_tensor(out=small[:, b + 3:b + 4], in0=small[:, b + 2:b + 3], in1=small[:, b + 2:b + 3], op=ALU.mult),
                            lambda e, b=b: e.scalar_tensor_tensor(out=small[:, b + 4:b + 5], in0=small[:, b + 1:b + 2], scalar=1.0 / D, in1=small[:, b + 3:b + 4], op0=ALU.mult, op1=ALU.subtract),
                            lambda e, b=b: e.tensor_scalar(out=small[:, b + 4:b + 5], in0=small[:, b + 4:b + 5], scalar1=LN_EPS, scalar2=None, op0=ALU.add),
                        ]
                        for f in ops:
                            n = pg.op("dve", f, o_st[p2])
                            pg.wait("dve", o_st[p2], n)
                        pg.wait("act", o_st[p2], o_st[p2].n)
                        n = pg.op("act", lambda e, b=b: e.activation(out=small[:, b + 5:b + 6], in_=small[:, b + 4:b + 5], func=AF.Sqrt), o_sq[p2])
                        pg.wait("dve", o_sq[p2], n)
                        n = pg.op("dve", lambda e, b=b: e.reciprocal(out=small[:, b + 5:b + 6], in_=small[:, b + 5:b + 6]), o_st[p2])
                        pg.wait("dve", o_st[p2], n)
                        pg.wait("dve", m_ld, mod_ready_ld)
                        n = pg.op("dve", lambda e, p2=p2, b=b: e.scalar_tensor_tensor(
                            out=ht[p2], in0=ht[p2], scalar=small[:, b + 2:b + 3], in1=LNG, op0=ALU.add, op1=ALU.mult), o_xn[p2])
                        pg.wait("dve", o_xn[p2], n)
                        pg.op("dve", lambda e, p2=p2, xs_=xs_, b=b: e.scalar_tensor_tensor(
                            out=xt[xs_], in0=ht[p2], scalar=small[:, b + 5:b + 6], in1=LNB, op0=ALU.mult, op1=ALU.add), o_xo[xs_])
                        pg.wait("sp", o_xo[xs_], o_xo[xs_].n)
                        pg.dma(lambda e, xs_=xs_, row0=row0: e.dma_start(out=dst[row0:row0 + 128, :], in_=xt[xs_]), o_xst[xs_])
                        xt_free[xs_].append((o_xst[xs_], o_xst[xs_].n))
                    if not last:
                        xo_n = o_xo[xs_].n

                        def ublock(i=i, tg=tg, s=s, xs_=xs_, p2=p2, c=c, uslot=uslot, xo_n=xo_n):
                            if full:
                                pg.wait("pool", o_xo[xs_], xo_n)
                            else:
                                pg.wait("pool", o_xld[xs_], xld_n[i])
                            pg.wait("pool", m_dve, mod_ready_dve)
                            pg.wait("pool", o_tp[p2], o_tp[p2].n)
                            n = pg.op("pool", lambda e: e.tensor_tensor(out=ut[p2], in0=xt[xs_], in1=SC1[c], op=ALU.mult), o_u1[p2])
                            xt_free[xs_].append((o_u1[p2], o_u1[p2].n))
                            pg.wait("pool", o_u1[p2], n)
                            pg.op("pool", lambda e: e.tensor_tensor(out=ut[p2], in0=ut[p2], in1=SH[c], op=ALU.add), o_u2[p2])
                            pg.wait("pe", o_u2[p2], o_u2[p2].n)
                            pg.wait("pe", misc_ld, misc_ld.n)
                            pg.wait("pe", o_te[p2], o_te[p2].n)
                            for k in range(8):
                                pg.op("pe", lambda e, k=k: e.transpose(
                                    out=PS[:, 4 + 2 * p2 + k // 4, (k % 4) * 128:(k % 4 + 1) * 128],
                                    in_=ut[p2][:, k * 128:(k + 1) * 128], identity=ident[:]),
                                    o_tp[p2] if k == 7 else None)
                            pg.wait("act", o_tp[p2], o_tp[p2].n)
                            if s == 0:
                                pg.wait("act", o_ust[uslot], o_ust[uslot].n)
                            pg.op("act", lambda e: e.activation(
                                out=uTst[uslot][:, :, s * 128:(s + 1) * 128],
                                in_=PS[:, 4 + 2 * p2:6 + 2 * p2, :].rearrange("p a (b t) -> p (a b) t", t=128), func=AF.Identity), o_te[p2])
                            if s == 3:
                                pg.wait("sp", o_te[0], o_te[0].n)
                                pg.wait("sp", o_te[1], o_te[1].n)
                                J = tg // 4
                                pg.dma(lambda e: e.dma_start(
                                    out=uT_d[J, :, :, (tg % 4) * 512:(tg % 4 + 1) * 512], in_=uTst[uslot]), o_ust[uslot])
                        pend_u.append(ublock)
                        while len(pend_u) > 2:
                            pend_u.pop(0)()
            while pend_u:
                pend_u.pop(0)()
            cnt["sub"] += NSUB
            cnt["tg"] += TOK // 512
            for evs in (o_xst, o_ust):
                for e_ in evs:
                    pg.wait("sp", e_, e_.n)

        uT_sb = BFA[:, 0:16384].rearrange("p (k t) -> p k t", k=8)
        wbf = [BFA[:, 16384 + i * 4096:16384 + (i + 1) * 4096].rearrange("p (k c) -> p k c", k=8) for i in range(2)]
        fstage = [BFA[:, 24576 + i * 2048:24576 + (i + 1) * 2048] for i in range(2)]
        vstage = [BFA[:, 28672 + i * 4096:28672 + (i + 1) * 4096].rearrange("p (t a c) -> p t a c", t=4, a=4) for i in range(2)]
        wst3 = [FA[:, i * 4096:(i + 1) * 4096].rearrange("p (k c) -> p k c", k=8) for i in range(3)]
        p_uld = pg.ev("puld")
        p_wld = [pg.ev(f"pwld{i}") for i in range(3)]
        p_cast = pg.ev("pcast")
        p_mm = [pg.ev(f"pmm{i}") for i in range(2)]
        p_evf = [pg.ev(f"pevf{i}") for i in range(2)]
        p_evv = [pg.ev(f"pevv{i}") for i in range(2)]
        p_fst = [pg.ev(f"pfst{i}") for i in range(2)]
        p_vst = [pg.ev(f"pvst{i}") for i in range(2)]
        p_ones = pg.ev("pones")
        pc = dict(w=0, set=0, f=0, v=0, ev=0)
        set_free = [None, None]

        def phase_P(l):
            ltype = layer_type(l)
            t = LT[ltype]
            groups = t["groups"]
            W = w_d[l]
            ngf, ngv = t["ngrp_fm"], t["ngrp_v"]
            nfm = t["nfm"]
            pg.wait("pool", p_vst[0], p_vst[0].n)
            pg.wait("pool", p_vst[1], p_vst[1].n)
            for i in range(2):
                pg.op("pool", lambda e, i=i: e.memset(vstage[i][:, :, :, 64:192], 1.0), p_ones)
            seq = [(J, wg) for J in range(NSB) for wg in range(ngf + ngv)]
            wbase = pc["w"]
            ldn = {}

            def emit_wload(i):
                w = wbase + i
                s3 = w % 3
                pg.wait("sp", p_cast, w - 2)
                wg_ = seq[i][1]
                pg.dma(lambda e: e.dma_start(out=wst3[s3], in_=W[wg_]), p_wld[s3])
                ldn[i] = p_wld[s3].n

            emit_wload(0)
            emit_wload(1)
            for i_, (J, wg) in enumerate(seq):
                if wg == 0:
                    pg.wait("sp", p_mm[0], p_mm[0].n)
                    pg.wait("sp", p_mm[1], p_mm[1].n)
                    pg.dma(lambda e, J=J: e.dma_start(out=uT_sb, in_=uT_d[J]), p_uld)
                if i_ + 2 < len(seq):
                    emit_wload(i_ + 2)
                if True:
                    wi = wbase + i_
                    s3 = wi % 3
                    ws = wi % 2
                    pg.wait("pool", p_wld[s3], ldn[i_])
                    pg.wait("pool", p_mm[0], pc.get(("mm0", ws), 0))
                    pg.wait("pool", p_mm[1], pc.get(("mm1", ws), 0))
                    pg.op("pool", lambda e, ws=ws, s3=s3: e.tensor_copy(out=wbf[ws], in_=wst3[s3]), p_cast)
                    assert p_cast.n == wi + 1
                    pg.wait("pe", p_cast, wi + 1)
                    pg.wait("pe", p_uld, p_uld.n)
                    if wg < ngf:
                        for b in range(4):
                            bi = wg * 4 + b
                            ty, pair = bi // 8, bi % 8
                            if ltype == "A":
                                is_z = (ty == 6)
                                d = 1 if is_z else groups[ty // 2][0]
                            else:
                                is_z = (ty == 2)
                                d = 1
                            si = pc["set"]
                            pc["set"] += 1
                            S_ = si % 2
                            if set_free[S_] is not None:
                                pg.wait("pe", set_free[S_][0], set_free[S_][1])
                            for k in range(8):
                                for s in range(4):
                                    pg.op("pe", lambda e, S_=S_, k=k, s=s, ws=ws, b=b: e.matmul(
                                        PS[:, 4 * S_ + s, :], lhsT=wbf[ws][:, k, b * 128:(b + 1) * 128],
                                        rhs=uT_sb[:, k, s * 512:(s + 1) * 512], start=(k == 0), stop=(k == 7)),
                                        p_mm[S_] if (k == 7 and s == 3) else None)
                            fi = pc["f"]
                            pc["f"] += 1
                            fs = fi % 2
                            eng = "act" if (is_z or fi % 2 == 0) else "dve"
                            pg.wait(eng, p_mm[S_], p_mm[S_].n)
                            pg.wait(eng, p_fst[fs], p_fst[fs].n)
                            src_ap = PS[:, 4 * S_:4 * S_ + 4, :].rearrange("p a b -> p (a b)")
                            if d > 1:
                                src_ap = src_ap.rearrange("p (u r) -> p r u", r=d)
                                dst_ap = fstage[fs].rearrange("p (r u) -> p r u", r=d)
                            else:
                                dst_ap = fstage[fs]
                            if eng == "act":
                                fn = AF.Silu if is_z else AF.Identity
                                pg.op("act", lambda e, dst_ap=dst_ap, src_ap=src_ap, fn=fn: e.activation(out=dst_ap, in_=src_ap, func=fn), p_evf[fs])
                            else:
                                pg.op("dve", lambda e, dst_ap=dst_ap, src_ap=src_ap: e.tensor_copy(out=dst_ap, in_=src_ap), p_evf[fs])
                            set_free[S_] = (p_evf[fs], p_evf[fs].n)
                            n = SB // d
                            pg.wait("sp", p_evf[fs], p_evf[fs].n)
                            pg.dma(lambda e, ty=ty, pair=pair, fs=fs, d=d, n=n, J=J: e.dma_start(
                                out=fm_d[ty, pair].rearrange("p (r u) -> p r u", r=d)[:, :, J * n:(J + 1) * n],
                                in_=fstage[fs].rearrange("p (r u) -> p r u", r=d)), p_fst[fs])
                    else:
                        vg = wg - ngf
                        g, ph = vg // 2, vg % 2
                        d = groups[g][0]
                        for sidx in range(4):
                            if d == 1:
                                tiles = [(0, 4 * sidx + i) for i in range(4)]
                            elif d == 4:
                                tiles = [(sidx, i) for i in range(4)]
                            else:
                                tiles = [(4 * sidx + i, 0) for i in range(4)]
                            si = pc["set"]
                            pc["set"] += 1
                            S_ = si % 2
                            if set_free[S_] is not None:
                                pg.wait("pe", set_free[S_][0], set_free[S_][1])
                            for ti, (r, mp) in enumerate(tiles):
                                for k in range(8):
                                    if d > 1:
                                        lh = uT_sb[:, k, :].rearrange("p (u r) -> p r u", r=d)[:, r, mp * 128:(mp + 1) * 128]
                                    else:
                                        lh = uT_sb[:, k, mp * 128:(mp + 1) * 128]
                                    pg.op("pe", lambda e, S_=S_, ti=ti, k=k, ws=ws, lh=lh: e.matmul(
                                        PS[:, 4 * S_ + ti, :], lhsT=lh, rhs=wbf[ws][:, k, :], start=(k == 0), stop=(k == 7)),
                                        p_mm[S_] if (k == 7 and ti == 3) else None)
                            vi = pc["v"]
                            pc["v"] += 1
                            vs = vi % 2
                            veng = "act" if vi % 2 == 0 else "dve"
                            pg.wait(veng, p_mm[S_], p_mm[S_].n)
                            pg.wait(veng, p_vst[vs], p_vst[vs].n)
                            pg.wait(veng, p_ones, p_ones.n)
                            srcv = PS[:, 4 * S_:4 * S_ + 4, :].rearrange("p t (a j e) -> p t a j e", a=4, j=2)
                            for j_, c0 in ((0, 0), (1, 192)):
                                if veng == "act":
                                    pg.op("act", lambda e, vs=vs, srcv=srcv, j_=j_, c0=c0: e.activation(
                                        out=vstage[vs][:, :, :, c0:c0 + 64], in_=srcv[:, :, :, j_, :], func=AF.Identity), p_evv[vs])
                                else:
                                    pg.op("dve", lambda e, vs=vs, srcv=srcv, j_=j_, c0=c0: e.tensor_copy(
                                        out=vstage[vs][:, :, :, c0:c0 + 64], in_=srcv[:, :, :, j_, :]), p_evv[vs])
                            set_free[S_] = (p_evv[vs], p_evv[vs].n)
                            pg.wait("sp", p_evv[vs], p_evv[vs].n)
                            ntr = 64 // d
                            for ti, (r, mp) in enumerate(tiles):
                                kt = r * ntr + J * (16 // d) + mp
                                pg.dma(lambda e, g=g, ph=ph, kt=kt, vs=vs, ti=ti: e.dma_start(
                                    out=Vs_d[g, 4 * ph:4 * ph + 4, :, kt, :].rearrange("a k c -> k a c"),
                                    in_=vstage[vs][:, ti, :, :]), p_vst[vs])
                    pc[("mm0", ws)] = p_mm[0].n
                    pc[("mm1", ws)] = p_mm[1].n
            pc["w"] += len(seq)
            for evs in (p_fst, p_vst):
                for e_ in evs:
                    pg.wait("sp", e_, e_.n)

        KT = [BFA[:, i * 3072:(i + 1) * 3072] for i in range(2)]
        QT = [BFA[:, 6144 + i * 2048:6144 + (i + 1) * 2048] for i in range(2)]
        VT = [BFA[:, 10240 + i * 6144:10240 + (i + 1) * 6144].rearrange("p (t c) -> p t c", c=256) for i in range(2)]
        ZT = [BFA[:, 22528 + i * 2048:22528 + (i + 1) * 2048] for i in range(2)]
        GT = [BFA[:, 26624 + i * 2048:26624 + (i + 1) * 2048] for i in range(2)]
        EP = [BFA[:, 30720 + i * 1024:30720 + (i + 1) * 1024].rearrange("p (h n) -> p h n", h=2) for i in range(4)]
        DTB = BFA[:, 34816:34816 + 9216]
        ACC2 = [FA[:, i * 4096:(i + 1) * 4096].rearrange("p (h t) -> p h t", h=2) for i in range(2)]
        RS = FA[:, 8192:10240]
        DTS = [FA[:, 10240 + i * 1024:10240 + (i + 1) * 1024] for i in range(2)]

        a_ld = [pg.ev(f"ald{i}") for i in range(2)]
        a_zld = [pg.ev(f"azld{i}") for i in range(2)]
        a_dld = [pg.ev(f"adld{i}") for i in range(2)]
        a_dcast = pg.ev("adcast")
        a_ln = pg.ev("aln")
        a_s = [pg.ev(f"as{i}") for i in range(2)]
        a_e = [pg.ev(f"ae{i}") for i in range(4)]
        a_p = [pg.ev(f"ap{i}") for i in range(4)]
        a_pv = [pg.ev(f"apv{i}") for i in range(4)]
        a_evac = [pg.ev(f"aevac{i}") for i in range(2)]
        a_fin = pg.ev("afin")
        a_gst = [pg.ev(f"agst{i}") for i in range(2)]
        ac = dict(unit=0, batch=0, pj=0, dst=0)
        unit_end = {}
        fin_hist = []
        fin_cnt = {}
        exp_n = {}

        def phase_A(l):
            ltype = layer_type(l)
            t = LT[ltype]
            groups = t["groups"]
            PW = 512 if ltype == "A" else 384
            dtab = dA_d if ltype == "A" else dB_d
            ng = len(groups)
            zty = fm_types(ltype) - 1
            KEEP = 2
            units = []
            for pair in range(8):
                for J in range(NSB):
                    for g, (d, ns) in enumerate(groups):
                        parts = 2 if d == 16 else 1
                        for part in range(parts):
                            nr = d // parts
                            units.append(dict(pair=pair, J=J, g=g, d=d, ns=ns, r0=part * nr, nr=nr,
                                              first=(g == 0 and part == 0), lastu=(g == ng - 1 and part == parts - 1)))
            ubase = ac["unit"]

            def emit_loads(ui):
                u = units[ui]
                ug = ubase + ui
                slot = ug % 2
                d, ns, J, pair, g = u["d"], u["ns"], u["J"], u["pair"], u["g"]
                Lr = TOK // d
                n = SB // d
                P0 = J * n
                KW = n + 256
                klo, khi = max(0, P0 - 128), min(Lr, P0 + n + 128)
                ntw = n // 128 + 2
                mlo, mhi = max(0, P0 // 128 - 1), min(Lr // 128 - 1, (P0 + n) // 128)
                if ltype == "A":
                    kty, qty = 2 * g + 1, 2 * g
                else:
                    kty, qty = 1, 0
                r0, nr = u["r0"], u["nr"]
                if (ug - 2) in unit_end:
                    pg.wait("sp", a_evac[0], unit_end[ug - 2][0])
                    pg.wait("sp", a_evac[1], unit_end[ug - 2][1])
                else:
                    assert ug - 2 < ubase or ug < 2, (ug, ubase)
                pg.dma(lambda e: e.dma_start(
                    out=KT[slot][:, 0:nr * KW].rearrange("p (r w) -> p r w", r=nr)[:, :, klo - (P0 - 128):khi - (P0 - 128)],
                    in_=fm_d[kty, pair].rearrange("p (r u) -> p r u", r=d)[:, r0:r0 + nr, klo:khi]), a_ld[slot])
                pg.dma(lambda e: e.dma_start(
                    out=QT[slot][:, 0:nr * n].rearrange("p (r w) -> p r w", r=nr),
                    in_=fm_d[qty, pair].rearrange("p (r u) -> p r u", r=d)[:, r0:r0 + nr, P0:P0 + n]), a_ld[slot])
                t0 = mlo - (P0 // 128 - 1)
                t1 = mhi + 1 - (P0 // 128 - 1)
                pg.dma(lambda e: e.dma_start(
                    out=VT[slot][:, 0:nr * ntw, :].rearrange("p (r t) c -> p r t c", r=nr)[:, :, t0:t1, :],
                    in_=Vs_d[g, pair].rearrange("k (r t) c -> k r t c", r=d)[:, r0:r0 + nr, mlo:mhi + 1, :]), a_ld[slot])
                u["slot"] = slot
                u["ldn"] = a_ld[slot].n

            pending = []
            deferred_fin = []
            fin_countdown = [0]

            def flush(keep):
                while len(pending) > keep:
                    pending.pop(0)[1]()

            cur_zs = 0
            emit_loads(0)
            loaded = 0
            for ui, u in enumerate(units):
                slot = u["slot"]
                d, ns, J, pair, g = u["d"], u["ns"], u["J"], u["pair"], u["g"]
                Lr = TOK // d
                n = SB // d
                P0 = J * n
                KW = n + 256
                ntw = n // 128 + 2
                mid = Lr // 2
                r0, nr = u["r0"], u["nr"]
                ug = ubase + ui
                if u["first"]:
                    pj = ac["pj"]
                    ac["pj"] += 1
                    cur_zs = pj % 2
                    zs = cur_zs
                    cur_pj = pj
                    if (pj - 2) in fin_cnt:
                        pg.wait("sp", a_fin, fin_cnt[pj - 2])
                    pg.dma(lambda e, zs=zs, pair=pair, J=J: e.dma_start(
                        out=ZT[zs], in_=fm_d[zty, pair][:, J * SB:(J + 1) * SB]), a_zld[zs])
                    if J == 0:
                        for i_ in range(4):
                            pg.wait("act", a_p[i_], a_p[i_].n)
                        for gg in range(ng):
                            for vv in range(3):
                                di = ac["dst"]
                                ac["dst"] += 1
                                hs = di % 2
                                pg.wait("sp", a_dcast, max(0, di - 1))
                                pg.dma(lambda e, gg=gg, vv=vv, pair=pair, hs=hs: e.dma_start(
                                    out=DTS[hs][:, 0:2 * PW].rearrange("p (h w) -> p h w", h=2),
                                    in_=dtab[pair, gg, vv]), a_dld[hs])
                                pg.wait("act", a_dld[hs], a_dld[hs].n)
                                pg.op("act", lambda e, gg=gg, vv=vv, hs=hs: e.activation(
                                    out=DTB[:, (gg * 3 + vv) * 2 * PW:(gg * 3 + vv + 1) * 2 * PW], in_=DTS[hs][:, 0:2 * PW], func=AF.Identity), a_dcast)
                                assert a_dcast.n == di + 1
                packs = []
                if ltype == "B":
                    qsz = 128
                    for qt_i in range(n // qsz):
                        q0 = P0 + qt_i * qsz
                        segs = []
                        for j in range(3):
                            m = q0 // 128 - 1 + j
                            segs.append(dict(ri=0, m=m, col=128 * j, N=128, qlo=q0, ocol=0))
                        packs.append(dict(q0=q0, segs=segs, nruns=1, rr=r0, ocols=qsz))
                elif d < 16:
                    qsz = 256
                    offs, Ns, qoff = [0, 64, 256, 448], [64, 192, 192, 64], [0, 0, 64, 192]
                    for ri in range(nr):
                        for qt_i in range(n // qsz):
                            q0 = P0 + qt_i * qsz
                            segs = []
                            for j in range(4):
                                m = q0 // 128 - 1 + j
                                segs.append(dict(ri=ri, m=m, col=offs[j], N=Ns[j], qlo=q0 + qoff[j], ocol=qoff[j]))
                            packs.append(dict(q0=q0, segs=segs, nruns=1, rr=r0 + ri, ocols=qsz))
                else:
                    qsz = 128
                    offs, Ns, qoff = [0, 64, 192], [64, 128, 64], [0, 0, 64]
                    q0 = P0
                    for rp in range(nr // 2):
                        segs = []
                        for a_ in range(2):
                            ri = 2 * rp + a_
                            for j in range(3):
                                m = q0 // 128 - 1 + j
                                segs.append(dict(ri=ri, m=m, col=256 * a_ + offs[j], N=Ns[j], qlo=q0 + qoff[j], ocol=128 * a_ + qoff[j]))
                        packs.append(dict(q0=q0, segs=segs, nruns=2, rr=r0 + 2 * rp, ocols=256))
                for pi, pk in enumerate(packs):
                    q0 = pk["q0"]
                    segs = [sg for sg in pk["segs"] if 0 <= sg["m"] < Lr // 128]
                    var = 0
                    if q0 == mid - qsz:
                        var = 1
                    elif q0 == mid:
                        var = 2
                    bi = ac["batch"]
                    ac["batch"] += 1
                    bs = bi % 4
                    ss = bi % 2
                    oset = bi % 2
                    pg.wait("pe", a_ld[slot], u["ldn"])
                    if (bi - 2) in exp_n:
                        pg.wait("pe", a_e[(bi - 2) % 4], exp_n[bi - 2])
                    for si_, sg in enumerate(segs):
                        kcol = sg["ri"] * KW + (128 * sg["m"] - (P0 - 128))
                        qcol = sg["ri"] * n + (sg["qlo"] - P0)
                        for h in range(2):
                            pg.op("pe", lambda e, ss=ss, h=h, slot=slot, kcol=kcol, qcol=qcol, sg=sg: e.matmul(
                                PS[:, 2 * ss + h, sg["col"]:sg["col"] + sg["N"]],
                                lhsT=KT[slot][h * 64:(h + 1) * 64, kcol:kcol + 128],
                                rhs=QT[slot][h * 64:(h + 1) * 64, qcol:qcol + sg["N"]], start=True, stop=True),
                                a_s[ss] if (h == 1 and si_ == len(segs) - 1) else None)
                    pg.wait("act", a_s[ss], a_s[ss].n)
                    pg.wait("act", a_pv[bs], a_pv[bs].n)
                    exp_n[bi] = pg.op("act", lambda e, bs=bs, ss=ss: e.activation(
                        out=EP[bs][:, :, 0:PW], in_=PS[:, 2 * ss:2 * ss + 2, 0:PW], func=AF.Exp, scale=0.125), a_e[bs])
                    pg.wait("dve", a_e[bs], a_e[bs].n)
                    pg.wait("dve", a_dcast, a_dcast.n)
                    dview = DTB[:, (g * 3 + var) * 2 * PW:(g * 3 + var + 1) * 2 * PW].rearrange("p (h w) -> p h w", h=2)
                    pg.op("dve", lambda e, bs=bs, dview=dview: e.tensor_tensor(
                        out=EP[bs][:, :, 0:PW], in0=EP[bs][:, :, 0:PW], in1=dview, op=ALU.mult), a_p[bs])
                    p_need = a_p[bs].n
                    ufirst = u["first"]
                    last_of_unit = (pi == len(packs) - 1)
                    tok0 = q0 - P0

                    ACC = ACC2[cur_zs]

                    def pv(bs=bs, oset=oset, slot=slot, segs=segs, pk=pk, p_need=p_need, ufirst=ufirst,
                           last_of_unit=last_of_unit, ug=ug, d=d, ntw=ntw, P0=P0, tok0=tok0, qsz=qsz, ACC=ACC, cur_pj=cur_pj):
                        pg.wait("pe", a_p[bs], p_need)
                        pg.wait("pe", a_evac[oset], a_evac[oset].n)
                        for h in range(2):
                            for si_, sg in enumerate(segs):
                                vtile = sg["ri"] * ntw + (sg["m"] - (P0 // 128 - 1))
                                pg.op("pe", lambda e, h=h, sg=sg, vtile=vtile, si_=si_: e.matmul(
                                    PS[:, 4 + 2 * oset + h, sg["ocol"]:sg["ocol"] + sg["N"]],
                                    lhsT=VT[slot][:, vtile, h * 128:(h + 1) * 128],
                                    rhs=EP[bs][:, h, sg["col"]:sg["col"] + sg["N"]],
                                    start=(si_ == 0), stop=(si_ == len(segs) - 1), skip_group_check=True),
                                    a_pv[bs] if (h == 1 and si_ == len(segs) - 1) else None)
                        pg.wait("dve", a_pv[bs], a_pv[bs].n)
                        oc = pk["ocols"]
                        osrc = PS[:, 4 + 2 * oset:6 + 2 * oset, 0:oc]
                        if d == 1:
                            accv = ACC[:, :, tok0:tok0 + oc]
                        elif pk["nruns"] == 1:
                            accv = ACC.rearrange("p h (u r) -> p h r u", r=d)[:, :, pk["rr"], tok0:tok0 + oc]
                        else:
                            accv = ACC.rearrange("p h (u r) -> p h r u", r=d)[:, :, pk["rr"]:pk["rr"] + 2, tok0:tok0 + qsz]
                            osrc = osrc.rearrange("p h (a u) -> p h a u", a=2)
                        if ufirst:
                            pg.wait("dve", a_fin, fin_cnt.get(cur_pj - 2, 0))
                            pg.op("dve", lambda e: e.tensor_copy(out=accv, in_=osrc), a_evac[oset])
                        else:
                            pg.op("dve", lambda e: e.tensor_tensor(out=accv, in0=osrc, in1=accv, op=ALU.add), a_evac[oset])
                        if last_of_unit:
                            unit_end[ug] = (a_evac[0].n, a_evac[1].n)
                    pending.append((ui, pv))
                    flush(KEEP)
                    if deferred_fin:
                        fin_countdown[0] -= 1
                        if fin_countdown[0] <= 0:
                            deferred_fin.pop(0)()
                    if loaded == ui and ui + 1 < len(units) and all(tag != ui - 1 for tag, _ in pending):
                        emit_loads(ui + 1)
                        loaded = ui + 1
                if u["lastu"]:
                    flush(0)
                if loaded == ui and ui + 1 < len(units):
                    flush(0)
                    emit_loads(ui + 1)
                    loaded = ui + 1
                if u["lastu"]:
                    zs = cur_zs
                    ACC = ACC2[cur_zs]
                    a_den, b_den = ACC[64:128, 0, :], ACC[0:64, 1, :]
                    a_num, b_num = ACC[0:64, 0, :], ACC[64:128, 1, :]
                    fin_before = a_fin.n
                    if ltype == "B":
                        pg.wait("dve", a_evac[0], a_evac[0].n)
                        pg.wait("dve", a_evac[1], a_evac[1].n)
                        pg.wait("dve", pro_ev, pro_ev.n)
                        li = l // 2
                        ca, cb = li * 16 + 2 * pair, li * 16 + 2 * pair + 1
                        pg.op("dve", lambda e, ca=ca, a_den=a_den: e.tensor_scalar(out=a_den, in0=a_den,
                                                                      scalar1=esink[64:128, ca:ca + 1], scalar2=None, op0=ALU.add), a_evac[0])
                        pg.op("dve", lambda e, cb=cb, b_den=b_den: e.tensor_scalar(out=b_den, in0=b_den,
                                                                      scalar1=esink[0:64, cb:cb + 1], scalar2=None, op0=ALU.add), a_evac[1])
                    pg.wait("act", a_evac[0], a_evac[0].n)
                    pg.wait("act", a_evac[1], a_evac[1].n)
                    while deferred_fin:
                        deferred_fin.pop(0)()
                    pg.wait("act", a_fin, a_fin.n)
                    pg.op("act", lambda e, a_den=a_den: e.activation(out=RS[0:64, :], in_=a_den, func=AF.Ln), a_ln)
                    n_ = pg.op("act", lambda e, b_den=b_den: e.activation(out=RS[64:128, :], in_=b_den, func=AF.Ln), a_ln)
                    pg.wait("act", a_ln, n_)
                    pg.op("act", lambda e: e.activation(out=RS, in_=RS, func=AF.Exp, scale=-1.0), a_ln)
                    ln_need = a_ln.n

                    def fin_dve(zs=zs, pair=pair, J=J, a_num=a_num, b_num=b_num, ln_need=ln_need, cur_pj=cur_pj):
                        pg.wait("dve", a_ln, ln_need)
                        pg.wait("dve", a_zld[zs], a_zld[zs].n)
                        pg.wait("dve", a_gst[zs], a_gst[zs].n)
                        n_ = pg.op("dve", lambda e: e.tensor_tensor(out=RS[0:64, :], in0=a_num, in1=RS[0:64, :], op=ALU.mult), a_fin)
                        n_ = pg.op("dve", lambda e: e.tensor_tensor(out=RS[64:128, :], in0=b_num, in1=RS[64:128, :], op=ALU.mult), a_fin)
                        pg.wait("dve", a_fin, n_)
                        pg.op("dve", lambda e: e.tensor_tensor(out=GT[zs], in0=RS, in1=ZT[zs], op=ALU.mult), a_fin)
                        fin_cnt[cur_pj] = a_fin.n
                        pg.wait("sp", a_fin, a_fin.n)
                        pg.dma(lambda e: e.dma_start(out=gs_d[pair, :, J * SB:(J + 1) * SB], in_=GT[zs]), a_gst[zs])
                    deferred_fin.append(fin_dve)
                    fin_countdown[0] = 3
            while deferred_fin:
                deferred_fin.pop(0)()
            ac["unit"] += len(units)
            for e_ in a_gst:
                pg.wait("sp", e_, e_.n)

        phase_M(-1)
        phase_O(-1)
        for l in range(nlayers):
            phase_P(l)
            phase_A(l)
            phase_M(l)
            phase_O(l)
        pg.wait("sp", o_xst[0], o_xst[0].n)
        pg.wait("sp", o_xst[1], o_xst[1].n)
        pg.wait("sp", o_xst[2], o_xst[2].n)

        with nc.Block() as block:
            @block.sync
            def _(e):
                pg.replay("sp", e)

            @block.tensor
            def _(e):
                pg.replay("pe", e)

            @block.scalar
            def _(e):
                pg.replay("act", e)

            @block.vector
            def _(e):
                pg.replay("dve", e)

            @block.gpsimd
            def _(e):
                pg.replay("pool", e)
    return nc


def _group_layout(wcat):
    nf = wcat.shape[1]
    assert nf % 512 == 0
    return np.ascontiguousarray(wcat.reshape(8, 128, nf // 512, 512).transpose(2, 1, 0, 3))


def _layer_weights(l, w_in_a, w_in_b):
    if l % 2 == 0:
        w = w_in_a[l // 2]
        cols = []
        for g in range(3):
            cols.append(w[:, g * 3072:g * 3072 + 1024])
            cols.append(w[:, g * 3072 + 1024:g * 3072 + 2048])
        cols.append(w[:, 9216:10240])
        for g in range(3):
            cols.append(w[:, g * 3072 + 2048:g * 3072 + 3072])
        return _group_layout(np.concatenate(cols, axis=1))
    w = w_in_b[l // 2]
    q = w[:, 0:1024]
    k = w[:, 1024:1280].reshape(1024, 4, 64)
    v = w[:, 1280:1536].reshape(1024, 4, 64)
    z = w[:, 1536:2560]
    kk = np.repeat(k, 4, axis=1).reshape(1024, 1024)
    vv = np.repeat(v, 4, axis=1).reshape(1024, 1024)
    return _group_layout(np.concatenate([q, kk, z, vv], axis=1))


def _dtable(groups, W, join, ltype):
    ng = len(groups)
    PW = 512 if ltype == "A" else 384
    out = np.zeros((8, ng, 3, 128, 2, PW), np.float32)
    slopes = 2.0 ** (-8.0 * np.arange(1, 17) / 16.0)
    k = np.arange(128)[:, None]
    j = np.arange(W)[None, :]
    for gi, (d, ns) in enumerate(groups):
        rel = k + ns - j
        valid = (np.abs(rel) <= ns)
        if ltype == "B":
            sl_ = [(256, 384), (128, 256), (0, 128)]
            cross_last, cross_first = (256, 384), (0, 128)
            reps = 1
        elif d < 16:
            sl_ = [(192, 256), (64, 256), (0, 192), (0, 64)]
            cross_last, cross_first = (448, 512), (0, 64)
            reps = 1
        else:
            sl_ = [(192, 256), (64, 192), (0, 64)]
            cross_last, cross_first = (192, 256), (0, 64)
            reps = 2
        for pair in range(8):
            for h in range(2):
                sl = slopes[2 * pair + h]
                base = np.where(valid, np.exp(-sl * d * np.abs(rel).astype(np.float64)), 0.0).astype(np.float32)
                one = np.concatenate([base[:, a:b] for a, b in sl_], axis=1)
                v1 = one.copy()
                v2 = one.copy()
                if not join:
                    v1[:, cross_last[0]:cross_last[1]] = 0.0
                    v2[:, cross_first[0]:cross_first[1]] = 0.0
                for vi, tb in enumerate((one, v1, v2)):
                    out[pair, gi, vi, :, h, :] = np.concatenate([tb] * reps, axis=1)
    return out


_CACHE = {}


def _get_program(nlayers):
    if nlayers not in _CACHE:
        _CACHE[nlayers] = build_program(nlayers)
    return _CACHE[nlayers]


def make_in_maps(x_prompt, x_sample, c_prompt, c_sample, w_mod, b_mod, ln_g, ln_b,
                 w_in_a, w_out_a, w_in_b, w_out_b, sink_b, nlayers=DEPTH):
    f = lambda a: np.ascontiguousarray(np.asarray(a, dtype=np.float32))
    x_prompt, x_sample, c_prompt, c_sample = f(x_prompt), f(x_sample), f(c_prompt), f(c_sample)
    w_mod, b_mod, ln_g, ln_b = f(w_mod), f(b_mod), f(ln_g), f(ln_b)
    w_in_a, w_out_a, w_in_b, w_out_b, sink_b = f(w_in_a), f(w_out_a), f(w_in_b), f(w_out_b), f(sink_b)
    shared = {}
    for l in range(nlayers):
        shared[f"w{l}"] = _layer_weights(l, w_in_a, w_in_b)
    wo = np.stack([(w_out_a if l % 2 == 0 else w_out_b)[l // 2] for l in range(DEPTH)])
    shared["wo"] = np.ascontiguousarray(wo.reshape(DEPTH, 8, 128, D).transpose(0, 2, 1, 3))
    shared["wm"] = np.ascontiguousarray(w_mod.reshape(DEPTH, 8, 128, 6, 512).transpose(0, 3, 2, 1, 4))
    shared["bmod"] = b_mod
    shared["lng"] = ln_g
    shared["lnb"] = ln_b
    shared["sink"] = sink_b
    shared["ident"] = np.eye(128, dtype=np.float32)
    dA = {j: _dtable(LT["A"]["groups"], 256, j, "A") for j in (False, True)}
    dB = {j: _dtable(LT["B"]["groups"], 384, j, "B") for j in (False, True)}
    in_maps = []
    for i in range(NCORES):
        m = dict(shared)
        if i < 4:
            m["x"] = x_prompt[i]
            m["cvec"] = np.ascontiguousarray(np.stack([c_prompt[i], c_prompt[i]]))
            m["dA"], m["dB"] = dA[True], dB[True]
        else:
            j = 2 * (i - 4)
            m["x"] = np.ascontiguousarray(x_sample[j:j + 2].reshape(TOK, D))
            m["cvec"] = np.ascontiguousarray(c_sample[j:j + 2])
            m["dA"], m["dB"] = dA[False], dB[False]
        in_maps.append(m)
    return in_maps


def kernel(x_prompt, x_sample, c_prompt, c_sample, w_mod, b_mod, ln_g, ln_b,
           w_in_a, w_out_a, w_in_b, w_out_b, sink_b):
    in_maps = make_in_maps(x_prompt, x_sample, c_prompt, c_sample, w_mod, b_mod, ln_g, ln_b,
                           w_in_a, w_out_a, w_in_b, w_out_b, sink_b)
    nc = _get_program(DEPTH)
    res = run_bass_kernel_spmd(nc, in_maps, core_ids=list(range(NCORES)))
    ys = [np.asarray(r["y"], dtype=np.float32) for r in res.results]
    y_prompt = np.stack(ys[0:4]).reshape(4, TOK, D)
    y_sample = np.concatenate([y.reshape(2, TOK // 2, D) for y in ys[4:8]], axis=0)
    return (y_prompt, y_sample)
```
